# Optimizing a Trainium2 kernel written in Bass

```python
import math
import jax, jax.numpy as jnp
from jax import lax
import numpy as np

D_MODEL = 1024
BATCH = 8
SEQ = 2048
DEPTH = 1
DEC_BATCH = 128
DEC_SEQ = 4
PAST_LEN = 16384
PAGE_SIZE = 128

N_META = 16
D_FF = 2816
GLA_HEADS = 4
GLA_DK = D_MODEL // (2 * GLA_HEADS)
GLA_DV = D_MODEL // GLA_HEADS
GLA_GATE_RANK = 16
GLA_GATE_TAU = 16.0
GLA_CHUNK = 64
S5_WIDTH = D_MODEL
S5_GROUP = 16
S5_GROUPS = S5_WIDTH // S5_GROUP
S5_STATE = 64
EPS = 1e-6
IN_SPLITS = (GLA_HEADS * GLA_DK, GLA_HEADS * GLA_DK, GLA_HEADS * GLA_DV, GLA_HEADS * GLA_DV,
             GLA_GATE_RANK, S5_WIDTH, D_MODEL, D_MODEL)
IN_COLS = sum(IN_SPLITS)

kernel_name = "gla_s5_gated_hybrid_step"

F32 = jnp.float32


def _rmsnorm(x, gain):
    x32 = x.astype(F32)
    y = x32 * lax.rsqrt(jnp.mean(x32 * x32, axis=-1, keepdims=True) + EPS)
    return (y * gain.astype(F32)).astype(x.dtype)


def _swiglu(x, w_gate, w_up, w_down):
    return (jax.nn.silu(x @ w_gate) * (x @ w_up)) @ w_down


def _gla_chunked(q, k, v, g, s0, chunk):
    bsz, nh, L, dk = q.shape
    dv = v.shape[-1]
    n = L // chunk
    q = q.reshape(bsz, nh, n, chunk, dk)
    k = k.reshape(bsz, nh, n, chunk, dk)
    v = v.reshape(bsz, nh, n, chunk, dv)
    g = g.reshape(bsz, nh, n, chunk, dk)
    b = jnp.cumsum(g, axis=3)
    b_last = b[:, :, :, -1:, :]
    qe = q * jnp.exp(b)
    ke = k * jnp.exp(-b)
    kd = k * jnp.exp(b_last - b)
    mask = jnp.tril(jnp.ones((chunk, chunk), dtype=bool))
    scores = jnp.einsum('bhncd,bhnsd->bhncs', qe, ke)
    o_intra = jnp.einsum('bhncs,bhnse->bhnce', jnp.where(mask, scores, 0.0), v)
    chunk_kv = jnp.einsum('bhnsd,bhnse->bhnde', kd, v)
    decay = jnp.exp(b_last[:, :, :, 0, :])

    def step(s, inp):
        dec, kv = inp
        return dec[..., None] * s + kv, s

    s_final, s_in = lax.scan(step, s0, (jnp.moveaxis(decay, 2, 0), jnp.moveaxis(chunk_kv, 2, 0)))
    s_in = jnp.moveaxis(s_in, 0, 2)
    o_inter = jnp.einsum('bhncd,bhnde->bhnce', qe, s_in)
    return (o_intra + o_inter).reshape(bsz, nh, L, dv), s_final


def _lin_comb(left, right):
    a1, b1 = left
    a2, b2 = right
    return a1 * a2, a2 * b1 + b2


def _s5(u, s_re, s_im, p):
    bsz, L, _ = u.shape
    uc = u.astype(F32).reshape(bsz, L, S5_GROUPS, S5_GROUP)
    lam = lax.complex(p["s5_a_re"].astype(F32), p["s5_a_im"].astype(F32))
    dt = jnp.exp(p["s5_log_dt"].astype(F32))[:, None]
    a_bar = jnp.exp(lam * dt)
    b_bar = ((a_bar - 1.0) / lam)[:, :, None] * lax.complex(p["s5_b_re"].astype(F32), p["s5_b_im"].astype(F32))
    c = lax.complex(p["s5_c_re"].astype(F32), p["s5_c_im"].astype(F32))
    bu = jnp.einsum('blgc,gpc->blgp', uc.astype(jnp.complex64), b_bar)
    h0 = lax.complex(s_re.astype(F32), s_im.astype(F32))
    bu = bu.at[:, 0].add(a_bar * h0)
    a_seq = jnp.broadcast_to(a_bar, bu.shape)
    _, hs = lax.associative_scan(_lin_comb, (a_seq, bu), axis=1)
    y = jnp.real(jnp.einsum('gcp,blgp->blgc', c, hs)) + p["s5_d"].astype(F32) * uc
    return y.reshape(bsz, L, S5_WIDTH), hs[:, -1]


def _layer(h, s_gla, s5_re, s5_im, segments, p):
    bsz, L, _ = h.shape
    dt = h.dtype
    h = h + 0.5 * _swiglu(_rmsnorm(h, p["norm_ffn1"]), p["ffn1_w_gate"], p["ffn1_w_up"], p["ffn1_w_down"])
    u = _rmsnorm(h, p["norm_mix"])
    z = u @ p["w_in"]
    offsets = []
    acc = 0
    for w in IN_SPLITS[:-1]:
        acc += w
        offsets.append(acc)
    q, k, v, r, g_lr, u_s5, gate_a, gate_b = jnp.split(z, offsets, axis=-1)

    g = jax.nn.log_sigmoid((g_lr @ p["gla_w_gate_up"] + p["gla_b_gate"]).astype(F32)) / GLA_GATE_TAU

    def heads(x, d):
        return x.reshape(bsz, L, GLA_HEADS, d).transpose(0, 2, 1, 3).astype(F32)

    qh = heads(q, GLA_DK) * (GLA_DK ** -0.5)
    kh = heads(k, GLA_DK)
    vh = heads(v, GLA_DV)
    gh = heads(g, GLA_DK)
    s = s_gla.astype(F32)
    outs = []
    start = 0
    for seg_len, chunk in segments:
        sl = slice(start, start + seg_len)
        o_seg, s = _gla_chunked(qh[:, :, sl], kh[:, :, sl], vh[:, :, sl], gh[:, :, sl], s, chunk)
        outs.append(o_seg)
        start += seg_len
    o = jnp.concatenate(outs, axis=2)
    o = o * lax.rsqrt(jnp.mean(o * o, axis=-1, keepdims=True) + EPS)
    o = o.transpose(0, 2, 1, 3).reshape(bsz, L, GLA_HEADS * GLA_DV) * p["gla_norm"].astype(F32)
    gla_out = (o.astype(dt) * jax.nn.silu(r)) @ p["gla_w_out"]

    y5, h5_last = _s5(u_s5, s5_re, s5_im, p)
    y5 = jax.nn.gelu(y5).astype(dt)
    s5_out = (y5 @ p["s5_w_glu_a"]) * jax.nn.sigmoid(y5 @ p["s5_w_glu_b"])

    merged = jax.nn.sigmoid(gate_a) * gla_out + jax.nn.sigmoid(gate_b) * s5_out
    h = h + merged @ p["w_out"]
    h = h + 0.5 * _swiglu(_rmsnorm(h, p["norm_ffn2"]), p["ffn2_w_gate"], p["ffn2_w_up"], p["ffn2_w_down"])
    return (h, s.astype(s_gla.dtype), jnp.real(h5_last).astype(s5_re.dtype),
            jnp.imag(h5_last).astype(s5_im.dtype))


def setup_inputs(seed: int = 0) -> dict:
    key = jax.random.key(seed)
    ks = iter(jax.random.split(key, 48))

    def nrm(shape, scale):
        return jax.random.normal(next(ks), shape, F32) * scale

    def gain(shape):
        return 1.0 + nrm(shape, 0.01)

    Dp = DEPTH
    d = {}
    d["x_prompt"] = nrm((BATCH, SEQ, D_MODEL), 1.0)
    d["x_sample"] = nrm((DEC_BATCH, DEC_SEQ, D_MODEL), 1.0)
    d["state_gla"] = nrm((Dp, DEC_BATCH, GLA_HEADS, GLA_DK, GLA_DV), 0.5)
    d["state_s5_re"] = nrm((Dp, DEC_BATCH, S5_GROUPS, S5_STATE), 0.1)
    d["state_s5_im"] = nrm((Dp, DEC_BATCH, S5_GROUPS, S5_STATE), 0.1)
    d["meta_tokens"] = nrm((N_META, D_MODEL), 1.0)
    d["norm_ffn1"] = gain((Dp, D_MODEL))
    d["ffn1_w_gate"] = nrm((Dp, D_MODEL, D_FF), D_MODEL ** -0.5)
    d["ffn1_w_up"] = nrm((Dp, D_MODEL, D_FF), D_MODEL ** -0.5)
    d["ffn1_w_down"] = nrm((Dp, D_FF, D_MODEL), D_FF ** -0.5)
    d["norm_mix"] = gain((Dp, D_MODEL))
    d["w_in"] = nrm((Dp, D_MODEL, IN_COLS), D_MODEL ** -0.5)
    d["gla_w_gate_up"] = nrm((Dp, GLA_GATE_RANK, GLA_HEADS * GLA_DK), GLA_GATE_RANK ** -0.5)
    d["gla_b_gate"] = nrm((Dp, GLA_HEADS * GLA_DK), 0.1)
    d["gla_norm"] = gain((Dp, GLA_HEADS * GLA_DV))
    d["gla_w_out"] = nrm((Dp, GLA_HEADS * GLA_DV, D_MODEL), (GLA_HEADS * GLA_DV) ** -0.5)
    d["s5_a_re"] = -0.5 + nrm((Dp, S5_GROUPS, S5_STATE), 0.01)
    d["s5_a_im"] = jnp.pi * jnp.arange(S5_STATE, dtype=F32) + nrm((Dp, S5_GROUPS, S5_STATE), 0.01)
    d["s5_log_dt"] = jax.random.uniform(next(ks), (Dp, S5_GROUPS), F32,
                                        minval=math.log(0.001), maxval=math.log(0.1))
    d["s5_b_re"] = nrm((Dp, S5_GROUPS, S5_STATE, S5_GROUP), (2 * S5_GROUP) ** -0.5)
    d["s5_b_im"] = nrm((Dp, S5_GROUPS, S5_STATE, S5_GROUP), (2 * S5_GROUP) ** -0.5)
    d["s5_c_re"] = nrm((Dp, S5_GROUPS, S5_GROUP, S5_STATE), (2 * S5_STATE) ** -0.5)
    d["s5_c_im"] = nrm((Dp, S5_GROUPS, S5_GROUP, S5_STATE), (2 * S5_STATE) ** -0.5)
    d["s5_d"] = nrm((Dp, S5_GROUPS, S5_GROUP), 1.0)
    d["s5_w_glu_a"] = nrm((Dp, S5_WIDTH, D_MODEL), S5_WIDTH ** -0.5)
    d["s5_w_glu_b"] = nrm((Dp, S5_WIDTH, D_MODEL), S5_WIDTH ** -0.5)
    d["w_out"] = nrm((Dp, D_MODEL, D_MODEL), D_MODEL ** -0.5)
    d["norm_ffn2"] = gain((Dp, D_MODEL))
    d["ffn2_w_gate"] = nrm((Dp, D_MODEL, D_FF), D_MODEL ** -0.5)
    d["ffn2_w_up"] = nrm((Dp, D_MODEL, D_FF), D_MODEL ** -0.5)
    d["ffn2_w_down"] = nrm((Dp, D_FF, D_MODEL), D_FF ** -0.5)
    d["norm_final"] = gain((D_MODEL,))
    return d


def reference(x_prompt, x_sample, state_gla, state_s5_re, state_s5_im, meta_tokens,
              norm_ffn1, ffn1_w_gate, ffn1_w_up, ffn1_w_down, norm_mix, w_in,
              gla_w_gate_up, gla_b_gate, gla_norm, gla_w_out,
              s5_a_re, s5_a_im, s5_log_dt, s5_b_re, s5_b_im, s5_c_re, s5_c_im, s5_d,
              s5_w_glu_a, s5_w_glu_b, w_out, norm_ffn2, ffn2_w_gate, ffn2_w_up, ffn2_w_down,
              norm_final):
    layer_params = dict(
        norm_ffn1=norm_ffn1, ffn1_w_gate=ffn1_w_gate, ffn1_w_up=ffn1_w_up, ffn1_w_down=ffn1_w_down,
        norm_mix=norm_mix, w_in=w_in, gla_w_gate_up=gla_w_gate_up, gla_b_gate=gla_b_gate,
        gla_norm=gla_norm, gla_w_out=gla_w_out, s5_a_re=s5_a_re, s5_a_im=s5_a_im,
        s5_log_dt=s5_log_dt, s5_b_re=s5_b_re, s5_b_im=s5_b_im, s5_c_re=s5_c_re, s5_c_im=s5_c_im,
        s5_d=s5_d, s5_w_glu_a=s5_w_glu_a, s5_w_glu_b=s5_w_glu_b, w_out=w_out,
        norm_ffn2=norm_ffn2, ffn2_w_gate=ffn2_w_gate, ffn2_w_up=ffn2_w_up, ffn2_w_down=ffn2_w_down)
    bsz = x_prompt.shape[0]
    dt = x_prompt.dtype
    meta = jnp.broadcast_to(meta_tokens.astype(dt)[None], (bsz, N_META, D_MODEL))
    h_p = jnp.concatenate([meta, x_prompt], axis=1)
    h_s = x_sample
    prompt_segments = ((N_META, N_META), (SEQ, GLA_CHUNK))
    sample_segments = ((DEC_SEQ, DEC_SEQ),)
    gla_p, re_p, im_p, gla_s, re_s, im_s = [], [], [], [], [], []
    for layer in range(DEPTH):
        p = {name: w[layer] for name, w in layer_params.items()}
        zero_gla = jnp.zeros((bsz, GLA_HEADS, GLA_DK, GLA_DV), state_gla.dtype)
        zero_s5 = jnp.zeros((bsz, S5_GROUPS, S5_STATE), state_s5_re.dtype)
        h_p, sg, sr, si = _layer(h_p, zero_gla, zero_s5, zero_s5, prompt_segments, p)
        gla_p.append(sg); re_p.append(sr); im_p.append(si)
        h_s, sg, sr, si = _layer(h_s, state_gla[layer], state_s5_re[layer], state_s5_im[layer],
                                 sample_segments, p)
        gla_s.append(sg); re_s.append(sr); im_s.append(si)
    y_prompt = _rmsnorm(h_p[:, N_META:], norm_final)
    y_sample = _rmsnorm(h_s, norm_final)
    return (y_prompt, y_sample, jnp.stack(gla_p), jnp.stack(re_p), jnp.stack(im_p),
            jnp.stack(gla_s), jnp.stack(re_s), jnp.stack(im_s))
```

```python
import math
import numpy as np
import concourse.bass as bass
import concourse.mybir as mybir
from concourse.bass_utils import run_bass_kernel_spmd

F32 = mybir.dt.float32
BF16 = mybir.dt.bfloat16
I32 = mybir.dt.int32
AF = mybir.ActivationFunctionType
ALU = mybir.AluOpType

D = 1024
DFF = 2816
NCORES = 8
SEGS = [(0, 80), (80, 512), (592, 512), (1104, 512), (1616, 512)]
NCOL = 2128
GI_FFN1, GI_MIX, GI_FFN2, GI_FINAL, GI_GLA = 0, 1, 2, 3, 4
EPS = 1e-6
CHAIN_ENG = "dve"
NW5 = 30
NW1, NW2, NW3 = 16, 20, 16


class Prog:
    def __init__(self, nc):
        self.nc = nc
        self.ops = []
        self.lastw = {}
        self.readers = {}
        self.barrier_ops = None
        self.barrier_done = set()

    def op(self, eng, fn, reads=(), writes=(), dma=None):
        i = len(self.ops)
        deps = {}
        psr = [r for r in reads if isinstance(r, tuple) and r[0] == "ps"]
        if psr:
            reads = [r for r in reads if not (isinstance(r, tuple) and r[0] == "ps")]
            writes = list(writes) + psr
        for r in reads:
            w = self.lastw.get(r)
            if w is not None:
                deps[w] = "raw"
        for r in writes:
            w = self.lastw.get(r)
            if w is not None:
                deps[w] = "raw"
            for rd in self.readers.get(r, ()):
                deps.setdefault(rd, "war")
        if self.barrier_ops is not None and eng not in self.barrier_done:
            for b in self.barrier_ops:
                deps[b] = "raw"
            self.barrier_done.add(eng)
        self.ops.append(dict(eng=eng, fn=fn, deps=deps, dma=dma, marked=False, semname=None, semval=None))
        for r in reads:
            self.readers.setdefault(r, []).append(i)
        for r in writes:
            self.lastw[r] = i
            self.readers[r] = []
        return i

    def barrier(self):
        last = {}
        for i, o in enumerate(self.ops):
            key = o["eng"] if o["dma"] is None else ("dma", o["dma"])
            last[key] = i
        self.barrier_ops = list(last.values())
        self.barrier_done = set()

    @staticmethod
    def _skip(p, o, kind):
        if p["dma"] is None and o["dma"] is None and p["eng"] == o["eng"]:
            if o["eng"] == "pe":
                return True
        return False

    def emit(self):
        nc = self.nc
        engs = {"pe": nc.tensor, "act": nc.scalar, "dve": nc.vector, "pool": nc.gpsimd, "sp": nc.sync}
        ops = self.ops
        for o in ops:
            for d, kind in o["deps"].items():
                if not self._skip(ops[d], o, kind):
                    ops[d]["marked"] = True
        semnames = set()
        for o in ops:
            o["semname"] = ("d_" + o["dma"]) if o["dma"] is not None else ("e_" + o["eng"])
            semnames.add(o["semname"])
        sems = {s: nc.semaphore(s).__enter__() for s in sorted(semnames)}
        cnt = {s: 0 for s in semnames}
        waited = {}
        nwait = 0
        for o in ops:
            E = engs[o["eng"]]
            need = {}
            for d, kind in o["deps"].items():
                p = ops[d]
                if self._skip(p, o, kind):
                    continue
                need[p["semname"]] = max(need.get(p["semname"], 0), p["semval"])
            for s, v in need.items():
                if waited.get((o["eng"], s), 0) < v:
                    E.wait_ge(sems[s], v)
                    waited[(o["eng"], s)] = v
                    nwait += 1
            inst = o["fn"](E)
            s = o["semname"]
            if o["dma"] is not None:
                cnt[s] += 16
                inst.then_inc(sems[s], 16)
                o["semval"] = cnt[s]
            elif o["marked"]:
                cnt[s] += 1
                inst.then_inc(sems[s], 1)
                o["semval"] = cnt[s]
        for s in sorted(semnames):
            if cnt[s] > 0:
                nc.sync.wait_ge(sems[s], cnt[s])
        return dict(n_ops=len(ops), n_wait=nwait, sem_max=max(cnt.values()))


IN_SHAPES = {
    "xT": [128, 8, NCOL],
    "w1g": [11, 128, 2048], "w1u": [11, 128, 2048], "w1d": [16, 128, 1408],
    "w2g": [11, 128, 2048], "w2u": [11, 128, 2048], "w2d": [16, 128, 1408],
    "win": [24, 128, 2048], "wglr": [128, 128],
    "wgo": [4, 128, 2048], "wa": [4, 128, 2048], "wb": [4, 128, 2048], "wo": [4, 128, 2048],
    "gains": [128, 5, 8], "wup": [17, 512],
    "s5p": [128, 3, 32], "s5b": [128, 2, 1024], "s5c": [128, 2, 1024], "s5d": [128, 32],
    "cmat": [128, 8, 128],
    "cind": [128, 19],
    "sg": [16, 4, 128, 256], "h0": [128, 16, 64],
}
OUT_SHAPES = {
    "yT": [128, 8, NCOL], "gp": [4, 128, 256], "gs": [16, 4, 128, 256],
    "s5po": [128, 64], "s5so": [128, 16, 64],
}


def build_program():
    nc = bass.Bass("TRN2", target_bir_lowering=False)
    P = Prog(nc)
    din = {k: nc.dram_tensor(k, s, F32, kind="ExternalInput").ap() for k, s in IN_SHAPES.items()}
    dout = {k: nc.dram_tensor(k, s, F32, kind="ExternalOutput").ap() for k, s in OUT_SHAPES.items()}

    def sb(name, shape, dt=F32):
        return nc.sbuf_tensor("s_" + name, shape, dt).__enter__()

    PHASES = []
    npe = [0]

    def phase(name):
        PHASES.append((name, npe[0]))

    def mm(out, lhsT, rhs, start=True, stop=True, r=(), w=()):
        npe[0] += 1
        P.op("pe", lambda e: e.matmul(out, lhsT, rhs, start=start, stop=stop), r, w)

    def warm(n):
        for _ in range(n):
            P.op("pe", lambda e: e.matmul(ps[7][:, 0:128], ones_b[:], ones_b[:], start=True, stop=True), ["const"], [("ps", 7)])

    def tr(out, in_, ident, r=(), w=()):
        npe[0] += 1
        P.op("pe", lambda e: e.transpose(out, in_, ident), r, w)

    def actf(out, in_, func, r=(), w=(), bias=None, scale=None):
        kw = {}
        if bias is not None:
            kw["bias"] = bias
        if scale is not None:
            kw["scale"] = scale
        P.op("act", lambda e: e.activation(out=out, in_=in_, func=func, **kw), r, w)

    def tt(eng, out, in0, in1, op, r=(), w=()):
        P.op(eng, lambda e: e.tensor_tensor(out=out, in0=in0, in1=in1, op=op), r, w)

    def ts(eng, out, in0, s1, s2, op0, op1, r=(), w=()):
        P.op(eng, lambda e: e.tensor_scalar(out=out, in0=in0, scalar1=s1, scalar2=s2, op0=op0, op1=op1), r, w)

    def tsm(eng, out, in0, s1, r=(), w=()):
        P.op(eng, lambda e: e.tensor_scalar_mul(out=out, in0=in0, scalar1=s1), r, w)

    def stt(out, in0, scalar, in1, op0, op1, r=(), w=()):
        P.op("dve", lambda e: e.scalar_tensor_tensor(out=out, in0=in0, scalar=scalar, in1=in1, op0=op0, op1=op1), r, w)

    def cp(eng, out, in_, r=(), w=()):
        if eng == "act":
            P.op("act", lambda e: e.activation(out=out, in_=in_, func=AF.Copy), r, w)
        else:
            P.op(eng, lambda e: e.tensor_copy(out=out, in_=in_), r, w)

    def recip(out, in_, r=(), w=()):
        P.op("dve", lambda e: e.reciprocal(out=out, in_=in_), r, w)

    def memset(eng, ap, val, w=()):
        P.op(eng, lambda e: e.memset(ap, val), (), w)

    def dma(q, out, in_, key, r=(), w=()):
        P.op(q, lambda e: e.dma_start(out=out, in_=in_), r, w, dma=key)

    ps = [nc.psum_tensor("ps%d" % i, [128, 512], F32).__enter__() for i in range(8)]
    psb = [p[:].bitcast(BF16) for p in ps]
    bank_ctr = [0]
    nbanks = [7]

    def bank():
        b = bank_ctr[0] % nbanks[0]
        bank_ctr[0] += 1
        return b

    def PSR(b):
        return ("ps", b)

    evac_ctr = [0]

    def evac_eng():
        evac_ctr[0] += 1
        return "act" if evac_ctr[0] % 2 == 0 else "dve"

    cmat = sb("cmat", [128, 8, 128])
    cind = sb("cind", [128, 19])
    gains = sb("gains", [128, 5, 8])
    wup = sb("wup", [17, 512])
    ident_f = cmat[:, 0, :]
    MSX = (cmat[:, 1, :], cmat[:, 2, :], cmat[:, 3, :], cind[:, 0:2])
    MS0 = (cmat[:, 4, :], cmat[:, 5, :], cmat[:, 6, :], cind[:, 2:19])
    tmask = cmat[:, 7, :]
    ident_b = sb("ident_b", [128, 128], BF16)
    ones_b = sb("ones_b", [128, 128], BF16)
    ones_f = sb("ones_f", [128, 128])
    onec = sb("onec", [128, 1])
    negpi = sb("negpi", [128, 1])
    epsc = sb("epsc", [128, 1])
    Toep = sb("Toep", [128, 32, 128], BF16)
    Win = sb("Win", [128, 32, 2, 128], BF16)
    WX = sb("WX", [128, 32, 2, 128], BF16)
    APW = sb("APW", [128, 8, 2, 64])
    AA = APW[:, 0, 0, :]
    AB = APW[:, 0, 1, :]
    Dcol = sb("Dcol", [128, 32])
    Sx = sb("Sx", [128, 4, 256])
    Sbf = [sb("Sbf%d" % i, [128, 4, 256], BF16) for i in range(3)]
    Hc = sb("Hc", [128, 64])

    dma("sp", cmat[:], din["cmat"], "c0a", w=["const"])
    dma("sp", cind[:], din["cind"], "c0b", w=["const"])
    dma("sp", gains[:], din["gains"], "c0c", w=["const"])
    dma("sp", wup[:], din["wup"], "c0d", w=["const"])
    dma("sp", Dcol[:], din["s5d"], "c0e", w=["const"])
    memset("dve", ones_b[:], 1.0, w=["const"])
    memset("dve", ones_f[:], 1.0, w=["const"])
    memset("dve", onec[:], 1.0, w=["const"])
    memset("dve", negpi[:], -math.pi, w=["const"])
    memset("dve", epsc[:], EPS, w=["const"])
    memset("dve", Sx[:], 0.0, w=[("Sx", h_) for h_ in range(4)])
    memset("dve", Sbf[0][:], 0.0, w=[("Sbf", 0, h) for h in range(4)])
    memset("dve", Hc[:], 0.0, w=["Hc"])
    cp("dve", ident_b[:], ident_f, r=["const"], w=["const"])

    import os as _os
    KPRE = int(_os.environ.get("KPRE", "99"))

    def s5_precompute():
        temps = []
        if KPRE <= 0:
            return

        def tb(name, shape, dt=F32):
            g = nc.sbuf_tensor("t_" + name, shape, dt)
            t = g.__enter__()
            temps.append(g)
            return t

        prm = tb("prm", [128, 3, 32])
        Bt = tb("Bt", [128, 2, 32, 32])
        Ct = tb("Ct", [128, 2, 32, 32])
        dma("sp", prm[:], din["s5p"], "c1a", w=["prm"])
        dma("sp", Bt[:].rearrange("p a q c -> p a (q c)"), din["s5b"], "c1b", w=["Bt"])
        dma("sp", Ct[:].rearrange("p a q c -> p a (q c)"), din["s5c"], "c1c", w=["Ct"])
        are, aim, ldt = prm[:, 0, :], prm[:, 1, :], prm[:, 2, :]
        dtt = tb("dtt", [128, 32]); lr = tb("lr", [128, 32]); th = tb("th", [128, 32])
        actf(dtt[:], ldt, AF.Exp, r=["prm"], w=["dtt"])
        tt("dve", lr[:], are, dtt[:], ALU.mult, r=["prm", "dtt"], w=["lr"])
        tt("dve", th[:], aim, dtt[:], ALU.mult, r=["prm", "dtt"], w=["th"])
        KS = list(range(-3, 5))
        MAG = tb("MAG", [128, 8, 32]); ARG = tb("ARG", [128, 2, 8, 32]); SC = tb("SC", [128, 2, 8, 32])
        KI = tb("KI", [128, 512], I32); KF = tb("KF", [128, 512])
        OFF = math.pi + 32 * math.pi
        for i, k in enumerate(KS):
            actf(MAG[:, i, :], lr[:], AF.Exp, r=["lr"], w=["MAG"], scale=float(k))
            ts("dve", ARG[:, 0, i, :], th[:], float(k), OFF, ALU.mult, ALU.add, r=["th"], w=["ARG"])
        P.op("dve", lambda e: e.tensor_scalar_add(out=ARG[:, 1, :, :], in0=ARG[:, 0, :, :], scalar1=math.pi / 2), ["ARG"], ["ARG"])
        argf = ARG[:].rearrange("p a k q -> p (a k q)")
        scf = SC[:].rearrange("p a k q -> p (a k q)")
        TWO_PI = 2 * math.pi
        tsm("dve", KI[:], argf, 1.0 / TWO_PI, r=["ARG"], w=["KI"])
        cp("dve", KF[:], KI[:], r=["KI"], w=["KF"])
        stt(argf, KF[:], -TWO_PI, argf, ALU.mult, ALU.add, r=["KF", "ARG"], w=["ARG"])
        ts("dve", KF[:], argf, 0.0, TWO_PI, ALU.is_lt, ALU.mult, r=["ARG"], w=["KF"])
        tt("dve", argf, argf, KF[:], ALU.add, r=["ARG", "KF"], w=["ARG"])
        ts("dve", KF[:], argf, TWO_PI, -TWO_PI, ALU.is_ge, ALU.mult, r=["ARG"], w=["KF"])
        tt("dve", argf, argf, KF[:], ALU.add, r=["ARG", "KF"], w=["ARG"])
        actf(scf, argf, AF.Sin, r=["ARG", "const"], w=["SC"], bias=negpi[:, 0:1], scale=1.0)
        PRE = tb("PRE", [128, 8, 32]); PIM = tb("PIM", [128, 8, 32])
        tt("dve", PRE[:], MAG[:], SC[:, 1, :, :], ALU.mult, r=["MAG", "SC"], w=["PRE"])
        tt("dve", PIM[:], MAG[:], SC[:, 0, :, :], ALU.mult, r=["MAG", "SC"], w=["PIM"])

        if KPRE <= 1:
            P.barrier()
            for g in reversed(temps):
                g.__exit__(None, None, None)
            return
        pw1 = tb("pw1", [128, 32]); pw2 = tb("pw2", [128, 32])
        cp("dve", APW[:, 0, 0, 0:32], PRE[:, 7, :], r=["PRE"], w=["AA"])
        cp("dve", APW[:, 0, 1, 0:32], PIM[:, 7, :], r=["PIM"], w=["AA"])
        for i in range(1, 8):
            cr, ci = APW[:, i - 1, 0, 0:32], APW[:, i - 1, 1, 0:32]
            tt("dve", pw1[:], cr, PRE[:, 7, :], ALU.mult, r=["AA", "PRE"], w=["pw1"])
            tt("dve", pw2[:], ci, PIM[:, 7, :], ALU.mult, r=["AA", "PIM"], w=["pw2"])
            tt("dve", APW[:, i, 0, 0:32], pw1[:], pw2[:], ALU.subtract, r=["pw1", "pw2"], w=["AA"])
            tt("dve", pw1[:], cr, PIM[:, 7, :], ALU.mult, r=["AA", "PIM"], w=["pw1"])
            tt("dve", pw2[:], ci, PRE[:, 7, :], ALU.mult, r=["AA", "PRE"], w=["pw2"])
            tt("dve", APW[:, i, 1, 0:32], pw1[:], pw2[:], ALU.add, r=["pw1", "pw2"], w=["AA"])
        cp("dve", APW[:, :, :, 32:64], APW[:, :, :, 0:32], r=["AA"], w=["AA"])
        nre = tb("nre", [128, 32]); den = tb("den", [128, 32]); t0 = tb("t0", [128, 32]); t1 = tb("t1s", [128, 32])
        cre = tb("cre", [128, 32]); cim = tb("cim", [128, 32])
        P.op("dve", lambda e: e.tensor_scalar_add(out=nre[:], in0=PRE[:, 4, :], scalar1=-1.0), ["PRE"], ["nre"])
        nim = PIM[:, 4, :]
        tt("dve", den[:], are, are, ALU.mult, r=["prm"], w=["den"])
        tt("dve", t0[:], aim, aim, ALU.mult, r=["prm"], w=["t0"])
        tt("dve", den[:], den[:], t0[:], ALU.add, r=["den", "t0"], w=["den"])
        recip(den[:], den[:], r=["den"], w=["den"])
        tt("dve", t0[:], nre[:], are, ALU.mult, r=["nre", "prm"], w=["t0"])
        tt("dve", t1[:], nim, aim, ALU.mult, r=["PIM", "prm"], w=["t1"])
        tt("dve", t0[:], t0[:], t1[:], ALU.add, r=["t0", "t1"], w=["t0"])
        tt("dve", cre[:], t0[:], den[:], ALU.mult, r=["t0", "den"], w=["cre"])
        tt("dve", t0[:], nim, are, ALU.mult, r=["PIM", "prm"], w=["t0"])
        tt("dve", t1[:], nre[:], aim, ALU.mult, r=["nre", "prm"], w=["t1"])
        tt("dve", t0[:], t0[:], t1[:], ALU.subtract, r=["t0", "t1"], w=["t0"])
        tt("dve", cim[:], t0[:], den[:], ALU.mult, r=["t0", "den"], w=["cim"])

        u1 = tb("u1", [128, 32, 32]); u2 = tb("u2", [128, 32, 32])

        def bc(x):
            return x.unsqueeze(2).to_broadcast([128, 32, 32])

        def cmul(ore, oim, xr, xi, yr, yi, rr, ww, neg_im=False):
            tt("dve", u1[:], yr, bc(xr), ALU.mult, r=rr, w=["u1"])
            tt("dve", u2[:], yi, bc(xi), ALU.mult, r=rr, w=["u2"])
            tt("dve", ore, u1[:], u2[:], ALU.subtract, r=["u1", "u2"], w=ww)
            tt("dve", u1[:], yi, bc(xr), ALU.mult, r=rr, w=["u1"])
            tt("dve", u2[:], yr, bc(xi), ALU.mult, r=rr, w=["u2"])
            if neg_im:
                stt(oim, u1[:], -1.0, u2[:], ALU.mult, ALU.subtract, r=["u1", "u2"], w=ww)
            else:
                tt("dve", oim, u1[:], u2[:], ALU.add, r=["u1", "u2"], w=ww)

        BB = tb("BB", [128, 2, 32, 32])
        cmul(BB[:, 0], BB[:, 1], cre[:], cim[:], Bt[:, 0], Bt[:, 1], ["cre", "cim", "Bt"], ["BB"])

        if KPRE <= 2:
            P.barrier()
            for g in reversed(temps):
                g.__exit__(None, None, None)
            return
        BP = tb("BP", [128, 2, 32, 4, 32])
        for j in range(4):
            idx = (3 - j) + 3
            cmul(BP[:, 0, :, j, :], BP[:, 1, :, j, :], PRE[:, idx, :], PIM[:, idx, :], BB[:, 0], BB[:, 1],
                 ["PRE", "PIM", "BB"], [("BP", j)])
        BPf = BP[:].rearrange("p a q j c -> p a q (j c)")
        for q0 in range(0, 32, 2):
            b = bank()
            for ql in range(2):
                for ri in range(2):
                    sl = ql * 2 + ri
                    tr(ps[b][:, sl * 128:(sl + 1) * 128], BPf[:, ri, q0 + ql, :], ident_f,
                       r=[("BP", j) for j in range(4)] + ["const"], w=[PSR(b)])
            cp(evac_eng(), WX[:, q0:q0 + 2, :, :], ps[b][:].rearrange("p (q r m) -> p q r m", q=2, r=2), r=[PSR(b)], w=["WX"])

        if KPRE <= 3:
            P.barrier()
            for g in reversed(temps):
                g.__exit__(None, None, None)
            return
        LL = BP
        for j in range(4):
            idx = 3 - j
            cmul(LL[:, 0, :, j, :], LL[:, 1, :, j, :], PRE[:, idx, :], PIM[:, idx, :], BB[:, 0], BB[:, 1],
                 ["PRE", "PIM", "BB"], [("LL", j)] + [("BP", j_) for j_ in range(4)])
        CP = tb("CP", [128, 2, 32, 5, 32])
        for k in range(5):
            idx = k + 3
            cmul(CP[:, 0, :, k, :], CP[:, 1, :, k, :], PRE[:, idx, :], PIM[:, idx, :], Ct[:, 0], Ct[:, 1],
                 ["PRE", "PIM", "Ct"], [("CP", k)], neg_im=True)
        LLf = LL[:].rearrange("p a q j c -> p a q (j c)")
        CPf = CP[:].rearrange("p a q k c -> p a q (k c)")
        allL = [("LL", j) for j in range(4)]
        allC = [("CP", k) for k in range(5)]

        if KPRE <= 4:
            P.barrier()
            for g in reversed(temps):
                g.__exit__(None, None, None)
            return
        for q0 in range(0, 32, 4):
            b = bank()
            for ql in range(4):
                q = q0 + ql
                o = ps[b][:, ql * 128:(ql + 1) * 128]
                mm(o, LLf[:, 0, q, :], CPf[:, 0, q, 0:128], True, False, r=allL + allC, w=[PSR(b)])
                mm(o, LLf[:, 1, q, :], CPf[:, 1, q, 0:128], False, True, r=allL + allC, w=[PSR(b)])
            tt("dve", Toep[:, q0:q0 + 4, :], ps[b][:].rearrange("p (q m) -> p q m", q=4),
               tmask.unsqueeze(1).to_broadcast([128, 4, 128]), ALU.mult, r=[PSR(b), "const"], w=["Toep"])

        if KPRE <= 5:
            P.barrier()
            for g in reversed(temps):
                g.__exit__(None, None, None)
            return
        for ri in range(2):
            cp("dve" if ri == 0 else "act", Win[:, :, ri, :], CPf[:, ri, :, 32:160], r=allC, w=["Win"])
        P.barrier()
        for g in reversed(temps):
            g.__exit__(None, None, None)

    s5_precompute()

    NSLOT = 5
    wbf = [sb("wbf%d" % i, [128, 2048], BF16) for i in range(NSLOT)]
    h = sb("h", [128, 8, 512])
    u = sb("u", [128, 8, 512], BF16)
    rstd = sb("rstd", [128, 512]); rtmp = sb("rtmp", [128, 512])
    mg = sb("mg", [128, 8, 512], BF16)
    y5T = sb("y5T", [128, 8, 512], BF16)
    ARENA_WORDS = 20480
    arena = sb("arena", [128, ARENA_WORDS])

    class Carver:
        def __init__(self):
            self.off = 0

        def f32(self, shape):
            n = int(np.prod(shape))
            a = arena[:, self.off:self.off + n]
            self.off += n
            assert self.off <= ARENA_WORDS, self.off
            return self._shape(a, shape)

        def bf(self, shape):
            n = int(np.prod(shape))
            w = (n + 1) // 2
            a = arena[:, self.off:self.off + w].bitcast(BF16)[:, 0:n]
            self.off += w
            assert self.off <= ARENA_WORDS, self.off
            return self._shape(a, shape)

        @staticmethod
        def _shape(a, shape):
            if len(shape) == 1:
                return a
            if len(shape) == 2:
                return a.rearrange("p (a b) -> p a b", a=shape[0])
            if len(shape) == 3:
                return a.rearrange("p (a b c) -> p a b c", a=shape[0], b=shape[1])
            raise ValueError

    seq = []
    for (c0, N) in SEGS:
        for pre in ("w1",):
            for j in range(11):
                seq.append((pre + "g", j, 2048)); seq.append((pre + "u", j, 2048))
            for t in range(16):
                seq.append((pre + "d", t, 1408))
        for t in range(12):
            seq.append(("win", t, 2048))
        seq.append(("wglr", None, 128))
        for t in range(12, 16):
            seq.append(("win", t, 2048))
        for t in range(4):
            seq.append(("wgo", t, 2048))
        for t in range(16, 20):
            seq.append(("win", t, 2048))
        for t in range(4):
            seq.append(("win", 20 + t, 2048))
        for t in range(4):
            seq.append(("wa", t, 2048)); seq.append(("wb", t, 2048))
        for t in range(4):
            seq.append(("wo", t, 2048))
        for pre in ("w2",):
            for j in range(11):
                seq.append((pre + "g", j, 2048)); seq.append((pre + "u", j, 2048))
            for t in range(16):
                seq.append((pre + "d", t, 1408))
    ws_state = dict(issued=0, k=0)
    PF = 2
    pfcur = [PF]

    def ws_issue(k):
        name, t, E = seq[k]
        src = din[name] if t is None else din[name][t]
        slot = k % NSLOT
        dma("pool", wbf[slot][:, 0:E], src, "w%d" % slot, w=[("w", slot)])

    def ws_prefetch():
        k = ws_state["k"]
        while ws_state["issued"] < min(len(seq), k + NSLOT):
            ws_issue(ws_state["issued"])
            ws_state["issued"] += 1

    def ws_next(name, t):
        k = ws_state["k"]
        assert seq[k][0] == name and seq[k][1] == t, (seq[k], name, t)
        while ws_state["issued"] < min(len(seq), k + 1 + pfcur[0]):
            ws_issue(ws_state["issued"])
            ws_state["issued"] += 1
        ws_state["k"] += 1
        return wbf[k % NSLOT], ("w", k % NSLOT)

    def rmsnorm(N, gi, dst, dst_key, src=None, src_key="h"):
        src = h if src is None else src
        b = bank()
        for c in range(8):
            if c % 2 == 0:
                actf(u[:, c, :N], src[:, c, :N], AF.Square, r=[(src_key, c)], w=[("u", c)])
            else:
                tt("dve", u[:, c, :N], src[:, c, :N], src[:, c, :N], ALU.mult, r=[(src_key, c)], w=[("u", c)])
        for c in range(8):
            mm(ps[b][:, :N], ones_b[:], u[:, c, :N], c == 0, c == 7, r=[("u", c), "const"], w=[PSR(b)])
        actf(rtmp[:, :N], ps[b][:, :N], AF.Ln, r=[PSR(b)], w=["rtmp"], bias=epsc[:, 0:1], scale=1.0 / D)
        actf(rstd[:, :N], rtmp[:, :N], AF.Exp, r=["rtmp"], w=["rstd"], scale=-0.5)
        for c in range(8):
            stt(dst[:, c, :N], src[:, c, :N], gains[:, gi, c:c + 1], rstd[:, :N], ALU.mult, ALU.mult,
                r=[(src_key, c), "rstd", "const"], w=[(dst_key, c)])

    def ffn(N, gi, pre, need_barrier=True, dst=None, dst_key="h"):
        if need_barrier:
            ws_prefetch()
            P.barrier()
        cv = Carver()
        act = cv.bf([22, 512])
        sgt = [cv.f32([512]) for _ in range(2)]
        phase("ffn_norm")
        rmsnorm(N, gi, u, "u")
        if N == 512:
            warm(NW5)
        phase("ffn_gu")
        pfcur[0] = 3
        for j in range(11):
            wg, rg = ws_next(pre + "g", j)
            wu, ru = ws_next(pre + "u", j)
            wgv = wg[:, 0:2048].rearrange("p (k m) -> p k m", k=8)
            wuv = wu[:, 0:2048].rearrange("p (k m) -> p k m", k=8)
            for half in range(2):
                c = 2 * j + half
                bg = bank()
                for kt in range(8):
                    mm(ps[bg][:, :N], wgv[:, kt, half * 128:(half + 1) * 128], u[:, kt, :N], kt == 0, kt == 7,
                       r=[rg, ("u", kt)], w=[PSR(bg)])
                bu = bank()
                for kt in range(8):
                    mm(ps[bu][:, :N], wuv[:, kt, half * 128:(half + 1) * 128], u[:, kt, :N], kt == 0, kt == 7,
                       r=[ru, ("u", kt)], w=[PSR(bu)])
                s = c % 2
                actf(sgt[s][:, :N], ps[bg][:, :N], AF.Silu, r=[PSR(bg)], w=[("sgt", s)])
                tt("dve", act[:, c, :N], sgt[s][:, :N], ps[bu][:, :N], ALU.mult, r=[("sgt", s), PSR(bu)], w=[("act", c)])
        phase("ffn_down")
        for o in range(8):
            b = bank()
            for kh in range(2):
                wd, rd = ws_next(pre + "d", 2 * o + kh)
                wdv = wd[:, 0:1408].rearrange("p (k m) -> p k m", k=11)
                for k in range(11):
                    ct = 11 * kh + k
                    mm(ps[b][:, :N], wdv[:, k, :], act[:, ct, :N], ct == 0, ct == 21, r=[rd, ("act", ct)], w=[PSR(b)])
            dstb = h if dst is None else dst
            stt(dstb[:, o, :N], ps[b][:, :N], 0.5, h[:, o, :N], ALU.mult, ALU.add, r=[PSR(b), ("h", o)], w=[(dst_key, o)])
        pfcur[0] = PF

    def proj_fm(wv, sub, N, rkey, src=None, skey="u"):
        src = u if src is None else src
        b = bank()
        for kt in range(8):
            sk = "y5T" if skey == "y5T_" else (skey, kt)
            mm(ps[b][:, :N], wv[:, kt, sub * 128:(sub + 1) * 128], src[:, kt, :N], kt == 0, kt == 7,
               r=[rkey, sk], w=[PSR(b)])
        return b

    xch = [0]

    KDUMP = int(_os.environ.get("KDUMP", "0"))

    def dbg_dump(name, ap, shape, rkeys):
        if not KDUMP:
            return
        d = nc.dram_tensor("dbg_" + name, shape, F32, kind="ExternalOutput").ap()
        dma("pool", d, ap, "dbg", r=rkeys)

    KMIX = int(_os.environ.get("KMIX", "99"))
    KGLA = int(_os.environ.get("KGLA", "99"))

    def mixer(si, c0seg, N):
        small = (N == 80)
        NMC = N // 4
        ws_prefetch()
        P.barrier()
        cv = Carver()
        NS = 128 if small else 512
        ohat = cv.bf([8, NS])
        siga = cv.bf([8, NS])
        qT32 = cv.f32([4, NS]); kT32 = cv.f32([4, NS])
        ktm = cv.f32([NS // 128, 512]); vtm = cv.bf([NS // 128, 1024]); silur = cv.bf([8, NS])
        gtm = cv.f32([512]); eb = cv.f32([4, 128]); enb = cv.f32([4, 128])
        erev = gtm
        NKDM = 4
        kd = cv.bf([512]); kdm = [cv.bf([128]) for _ in range(NKDM)]
        kctr = [0]
        qe = cv.bf([4, 128]); ke = cv.bf([4, 128]); scT = cv.bf([4, 128])
        o32 = cv.f32([8, 128]); osq = cv.bf([8, 128]); rs = cv.f32([4, 128])
        otmp2 = [cv.f32([128]) for _ in range(2)]
        glr = cv.f32([NS])[0:17, :]
        memset("dve", glr[:, :], 1.0, w=["glr"])
        if small:
            NSL = 8
            Sld = [cv.f32([256]) for _ in range(NSL)]
            Sout = [cv.f32([256]) for _ in range(NSL)]
            Sbs = cv.bf([16, 256])

        tiles = [(0, 0, 80)] if small else [(ti, 128 * ti, 128) for ti in range(4)]
        phase("mix_norm")
        rmsnorm(N, GI_MIX, u, "u")
        if not small:
            warm(NW5)
        phase("mix_proj")
        for t in range(2):
            wt, rk = ws_next("win", t)
            wv = wt[:, 0:2048].rearrange("p (k m) -> p k m", k=8)
            for sub in range(2):
                hh = 2 * t + sub
                b = proj_fm(wv, sub, N, rk)
                cp(evac_eng(), qT32[:, hh, :N], ps[b][:, :N], r=[PSR(b)], w=[("qT", hh)])
        for t in range(2):
            wt, rk = ws_next("win", 2 + t)
            wv = wt[:, 0:2048].rearrange("p (k m) -> p k m", k=8)
            for sub in range(2):
                hh = 2 * t + sub
                b = proj_fm(wv, sub, N, rk)
                cp(evac_eng(), kT32[:, hh, :N], ps[b][:, :N], r=[PSR(b)], w=[("kT", hh)])
            for (ti, tc0, R) in tiles:
                b = bank()
                for kt in range(8):
                    mm(ps[b][:R, 0:256], u[:, kt, tc0:tc0 + R], wv[:, kt, :], kt == 0, kt == 7, r=[rk, ("u", kt)], w=[PSR(b)])
                cp(evac_eng(), ktm[:R, ti, 256 * t:256 * t + 256], ps[b][:R, 0:256], r=[PSR(b)], w=[("ktm", ti)])
        for t in range(4):
            wt, rk = ws_next("win", 4 + t)
            wv = wt[:, 0:2048].rearrange("p (k m) -> p k m", k=8)
            for (ti, tc0, R) in tiles:
                b = bank()
                for kt in range(8):
                    mm(ps[b][:R, 0:256], u[:, kt, tc0:tc0 + R], wv[:, kt, :], kt == 0, kt == 7, r=[rk, ("u", kt)], w=[PSR(b)])
                cp(evac_eng(), vtm[:R, ti, 256 * t:256 * t + 256], ps[b][:R, 0:256], r=[PSR(b)], w=[("vtm", ti)])
        for t in range(4):
            wt, rk = ws_next("win", 8 + t)
            wv = wt[:, 0:2048].rearrange("p (k m) -> p k m", k=8)
            for sub in range(2):
                c8 = 2 * t + sub
                b = proj_fm(wv, sub, N, rk)
                actf(silur[:, c8, :N], ps[b][:, :N], AF.Silu, r=[PSR(b)], w=[("silur", c8)])
        wt, rk = ws_next("wglr", None)
        wv = wt[:, 0:128].rearrange("p (k m) -> p k m", k=8)
        b = bank()
        for kt in range(8):
            mm(ps[b][:16, :N], wv[:, kt, :], u[:, kt, :N], kt == 0, kt == 7, r=[rk, ("u", kt)], w=[PSR(b)])
        cp("dve", glr[0:16, :N], ps[b][:16, :N], r=[PSR(b)], w=["glr"])

        if KMIX <= 1:
            return
        phase("gla_core")
        def emit_gate_a(t):
            wt, rk = ws_next("win", 12 + t)
            wv = wt[:, 0:2048].rearrange("p (k m) -> p k m", k=8)
            for sub in range(2):
                o = 2 * t + sub
                b = proj_fm(wv, sub, N, rk)
                actf(siga[:, o, :N], ps[b][:, :N], AF.Sigmoid, r=[PSR(b)], w=[("siga", o)])

        nbanks[0] = 5
        for (ti, tc0, R) in tiles:
            mask, tri, trirev, ind = MS0 if small else MSX
            if small:
                chunks = [(0, 0, 16, ("x", None))] + [(1 + bb, 16 + 4 * bb, 20 + 4 * bb, ("s", bb)) for bb in range(16)]
            else:
                chunks = [(0, 0, 64, ("x", None)), (1, 64, 128, ("x", None))]
            b1 = bank()
            mm(ps[b1][:R, :], glr[0:17, tc0:tc0 + R], wup[0:17, :], r=["glr", "const"], w=[PSR(b1)])
            if not small:
                warm(NW1)
            actf(gtm[:R, :], ps[b1][:R, :], AF.Exp, r=[PSR(b1)], w=["gtm"], scale=-1.0)
            actf(gtm[:R, :], gtm[:R, :], AF.Ln, r=["gtm", "const"], w=["gtm"], bias=onec[:R, 0:1], scale=1.0)
            if KGLA <= 1:
                continue
            b2 = bank()
            for hh in range(4):
                mm(ps[b2][:, hh * 128:hh * 128 + R], gtm[:R, hh * 128:(hh + 1) * 128], tri[:R, :R], r=["gtm", "const"], w=[PSR(b2)])
            psv = ps[b2][:].rearrange("p (h c) -> p h c", h=4)[:, :, :R]
            actf(eb[:, :, :R], psv, AF.Exp, r=[PSR(b2)], w=["eb"])
            actf(enb[:, :, :R], psv, AF.Exp, r=[PSR(b2)], w=["enb"], scale=-1.0)
            if KGLA <= 2:
                continue
            b3 = bank()
            mm(ps[b3][:R, :], trirev[:R, :R], gtm[:R, :], r=["gtm", "const"], w=[PSR(b3)])
            if not small:
                warm(NW2)
            actf(erev[:R, :], ps[b3][:R, :], AF.Exp, r=[PSR(b3)], w=["gtm"])
            tt("dve", kd[:R, :], ktm[:R, ti, :], erev[:R, :], ALU.mult, r=[("ktm", ti), "gtm"], w=["kd"])
            if KGLA <= 3:
                continue
            for hh in range(4):
                stt(qe[:, hh, :R], qT32[:, hh, tc0:tc0 + R], 128.0 ** -0.5, eb[:, hh, :R], ALU.mult, ALU.mult,
                    r=[("qT", hh), "eb"], w=[("qe", hh)])
                tt("dve", ke[:, hh, :R], kT32[:, hh, tc0:tc0 + R], enb[:, hh, :R], ALU.mult, r=[("kT", hh), "enb"], w=[("ke", hh)])
            b4 = bank()
            for hh in range(4):
                mm(ps[b4][:R, hh * 128:hh * 128 + R], ke[:, hh, :R], qe[:, hh, :R], r=[("ke", hh), ("qe", hh)], w=[PSR(b4)])
            for hh in range(4):
                tt("dve", scT[:R, hh, :R], ps[b4][:R, hh * 128:hh * 128 + R], mask[:R, :R], ALU.mult,
                   r=[PSR(b4), "const"], w=[("scT", hh)])
            if KGLA <= 4:
                continue
            if not small:
                emit_gate_a(ti)
            bo = [5, 6]
            x0 = xch[0]
            for hh in range(4):
                xc = x0
                ent = []
                kvb = []
                LAG = 3

                def upd(ci):
                    nonlocal xc
                    (cidx, lo, hi, kind) = chunks[ci]
                    bk, half = kvb[ci]
                    dec = eb[:, hh, hi - 1:hi]
                    if kind[0] == "x":
                        ent.append((Sbf[xc % 3][:, hh, :], ("Sbf", xc % 3, hh)))
                        stt(Sx[:, hh, :], Sx[:, hh, :], dec, ps[bk][:, half:half + 256], ALU.mult, ALU.add,
                            r=[("Sx", hh), "eb", PSR(bk)], w=[("Sx", hh)])
                        xc += 1
                        cp("act", Sbf[xc % 3][:, hh, :], Sx[:, hh, :], r=[("Sx", hh)], w=[("Sbf", xc % 3, hh)])
                    else:
                        bb = kind[1]
                        sl = bb % NSL
                        ent.append((Sbs[:, bb, :], ("Sbs", bb)))
                        stt(Sout[sl][:, :], Sld[sl][:, :], dec, ps[bk][:, half:half + 256], ALU.mult, ALU.add,
                            r=[("Sld", sl), "eb", PSR(bk)], w=[("Sout", sl)])
                        dma("sp", dout["gs"][bb, hh], Sout[sl][:, :], "sst%d" % sl, r=[("Sout", sl)])

                for ci, (cidx, lo, hi, kind) in enumerate(chunks):
                    kslot = (kctr[0]) % NKDM
                    kctr[0] += 1
                    tsm("dve", kdm[kslot][:R, :], kd[:R, hh * 128:(hh + 1) * 128], ind[:R, cidx:cidx + 1],
                        r=["kd", "const"], w=[("kdm", kslot)])
                    if len(kvb) % 2 == 0:
                        bk = bank()
                    half = (len(kvb) % 2) * 256
                    mm(ps[bk][:, half:half + 256], kdm[kslot][:R, :], vtm[:R, ti, hh * 256:(hh + 1) * 256],
                       r=[("kdm", kslot), ("vtm", ti)], w=[PSR(bk)])
                    kvb.append((bk, half))
                    if kind[0] == "s":
                        bb = kind[1]
                        sl = bb % NSL
                        dma("sp", Sld[sl][:, :], din["sg"][bb, hh], "sld%d" % sl, w=[("Sld", sl)])
                        cp("act", Sbs[:, bb, :], Sld[sl][:, :], r=[("Sld", sl)], w=[("Sbs", bb)])
                    if ci >= LAG:
                        upd(ci - LAG)
                for ci in range(max(0, len(chunks) - LAG), len(chunks)):
                    upd(ci)
                if hh == 3:
                    xch[0] = xc
                if KGLA <= 5:
                    continue
                pso = ps[bo[hh // 2]]
                for e2 in range(2):
                    base = (hh % 2) * 256 + e2 * 128
                    mm(pso[:, base:base + R], vtm[:R, ti, hh * 256 + e2 * 128:hh * 256 + (e2 + 1) * 128], scT[:R, hh, :R],
                       True, False, r=[("vtm", ti), ("scT", hh)], w=[PSR(bo[hh // 2])])
                    for ci, (cidx, lo, hi, kind) in enumerate(chunks):
                        Sap, Skey = ent[ci]
                        mm(pso[:, base + lo:base + hi], Sap[:, e2 * 128:(e2 + 1) * 128], qe[:, hh, lo:hi],
                           False, ci == len(chunks) - 1, r=[Skey, ("qe", hh)], w=[PSR(bo[hh // 2])])
            if KGLA <= 6:
                continue
            if not small:
                warm(NW3)
            for k2 in range(2):
                pv = ps[bo[k2]][:].rearrange("p (c n) -> p c n", c=4)[:, :, :R]
                cp("dve", o32[:, 4 * k2:4 * k2 + 4, :R], pv, r=[PSR(bo[k2])], w=[("o32", k2)])
                actf(osq[:, 4 * k2:4 * k2 + 4, :R], pv, AF.Square, r=[PSR(bo[k2])], w=[("osq", k2)])
            if KGLA <= 7:
                continue
            bS = bank()
            for hh in range(4):
                for e2 in range(2):
                    mm(ps[bS][:, hh * 128:hh * 128 + R], ones_b[:], osq[:, 2 * hh + e2, :R], e2 == 0, e2 == 1,
                       r=[("osq", hh // 2), "const"], w=[PSR(bS)])
            if KGLA <= 8:
                continue
            psS = ps[bS][:].rearrange("p (h c) -> p h c", h=4)[:, :, :R]
            actf(rs[:, :, :R], psS, AF.Ln, r=[PSR(bS)], w=["rs"], bias=epsc[:, 0:1], scale=1.0 / 256)
            actf(rs[:, :, :R], rs[:, :, :R], AF.Exp, r=["rs"], w=["rs"], scale=-0.5)
            if KGLA <= 9:
                continue
            if si == 0 and KDUMP:
                dbg_dump("o32", o32[:], [128, 8, 128], [("o32", 0), ("o32", 1)])
                dbg_dump("rs", rs[:], [128, 4, 128], ["rs"])
                dbg_dump("silur", silur[:, :, 0:128], [128, 8, 128], [("silur", c) for c in range(8)])
                dbg_dump("scT", scT[:], [128, 4, 128], [("scT", c) for c in range(4)])
                dbg_dump("qe", qe[:], [128, 4, 128], [("qe", c) for c in range(4)])
                dbg_dump("ke", ke[:], [128, 4, 128], [("ke", c) for c in range(4)])
            for c8 in range(8):
                stt(otmp2[c8 % 2][:, :R], o32[:, c8, :R], gains[:, GI_GLA, c8:c8 + 1], rs[:, c8 // 2, :R], ALU.mult, ALU.mult,
                    r=[("o32", c8 // 4), "rs", "const"], w=[("otmp", c8 % 2)])
                tt("pool", ohat[:, c8, tc0:tc0 + R], otmp2[c8 % 2][:, :R], silur[:, c8, tc0:tc0 + R], ALU.mult,
                   r=[("otmp", c8 % 2), ("silur", c8)], w=[("ohat", c8)])

        if KMIX <= 2:
            return
        nbanks[0] = 7
        if si == 0:
            dbg_dump("ohat", ohat[:, :, 0:128], [128, 8, 128], [("ohat", c) for c in range(8)])
        phase("gla_out")
        if small:
            for t in range(4):
                emit_gate_a(t)
        for t in range(4):
            wt, rk = ws_next("wgo", t)
            wv = wt[:, 0:2048].rearrange("p (k m) -> p k m", k=8)
            for sub in range(2):
                o = 2 * t + sub
                b = proj_fm(wv, sub, N, rk, src=ohat, skey="ohat")
                tt("dve", mg[:, o, :N], ps[b][:, :N], siga[:, o, :N], ALU.mult, r=[PSR(b), ("siga", o)], w=[("mg", o)])

        if KMIX <= 3:
            return
        ws_prefetch()
        P.barrier()
        cv = Carver()
        Uraw = cv.bf([4096])
        UtmA = Uraw.rearrange("p (q j c) -> p q j c", q=32, j=4)
        UtmAf = Uraw.rearrange("p (q x) -> p q x", q=32)
        Utm = Uraw.rearrange("p (j f) -> p j f", j=4)
        Uq = cv.bf([32, 128])
        HS = cv.f32([NMC + 1, 64])
        Hbf = [cv.bf([2, 4, 128]) for _ in range(2)]
        yv = cv.f32([4, 128]); yt = cv.f32([4, 128])
        ct1 = cv.f32([64]); ct2 = cv.f32([64])
        if small:
            H0t = cv.f32([16, 64]); Xsm = cv.f32([16, 64]); Hso = cv.f32([16, 64])
            st1 = cv.f32([16, 64]); st2 = cv.f32([16, 64])
        Cs_buf = cv.f32([17, 64])
        chain_tmp_off = cv.off
        sbt = [cv.f32([NS]) for _ in range(2)]
        gtmp = [cv.f32([NS]) for _ in range(2)]
        mrg = cv.bf([8, NS])
        phase("s5_proj")
        uv = u[:, :, 0:N].rearrange("p k (n j) -> p k n j", j=4)
        for t in range(4):
            wt, rk = ws_next("win", 16 + t)
            wv = wt[:, 0:2048].rearrange("p (k m) -> p k m", k=8)
            for j in range(4):
                b = bank()
                for kt in range(8):
                    mm(ps[b][:NMC, 0:256], uv[:, kt, :, j], wv[:, kt, :], kt == 0, kt == 7, r=[rk, ("u", kt)], w=[PSR(b)])
                cp(evac_eng(), UtmA[:NMC, 8 * t:8 * t + 8, j, :], ps[b][:NMC, 0:256].rearrange("n (q c) -> n q c", q=8),
                   r=[PSR(b)], w=[("Utm", t)])
        allU = [("Utm", t) for t in range(4)]
        phase("s5_trX")
        for g8 in range(4):
            b = bank()
            for sl in range(8):
                q = 8 * g8 + sl
                tr(psb[b][:, sl * 128:sl * 128 + NMC], UtmAf[:NMC, q, :], ident_b[:NMC, :NMC],
                   r=allU + ["const"], w=[PSR(b)])
            cp(evac_eng(), Uq[:, 8 * g8:8 * g8 + 8, :NMC], psb[b][:].rearrange("p (s n) -> p s n", s=8)[:, :, :NMC],
               r=[PSR(b)], w=[("Uq", 8 * g8 + i_) for i_ in range(8)])
        for q0 in range(0, 32, 4):
            for ri in range(2):
                b = bank()
                for ql in range(4):
                    q = q0 + ql
                    mm(ps[b][:, ql * 128:ql * 128 + NMC], WX[:, q, ri, :], Uq[:, q, :NMC], r=["WX", ("Uq", q)], w=[PSR(b)])
                pv = ps[b][:].rearrange("p (q n) -> p q n", q=4)
                col = ri * 32 + q0
                if small:
                    cp(evac_eng(), HS[:, 1:5, col:col + 4].rearrange("p n q -> p q n"), pv[:, :, 0:4], r=[PSR(b)], w=["HS"])
                    cp(evac_eng(), Xsm[:, :, col:col + 4].rearrange("p n q -> p q n"), pv[:, :, 4:20], r=[PSR(b)], w=["Xsm"])
                else:
                    cp(evac_eng(), HS[:, 1:1 + NMC, col:col + 4].rearrange("p n q -> p q n"), pv[:, :, :NMC], r=[PSR(b)], w=["HS"])
        if KMIX <= 4:
            return
        phase("s5_chain")
        for t in range(4):
            wg_, rg = ws_next("win", 20 + t)
            wgv = wg_[:, 0:2048].rearrange("p (k m) -> p k m", k=8)
            for sub in range(2):
                o = 2 * t + sub
                bg = proj_fm(wgv, sub, N, rg)
                actf(mrg[:, o, :N], ps[bg][:, :N], AF.Sigmoid, r=[PSR(bg)], w=[("mrg", o)])
        CE = CHAIN_ENG

        def cstep(dst, src, xin, pw, k_src, k_x, k_dst):
            tt(CE, ct1[:], src, APW[:, pw - 1, 0, :], ALU.mult, r=[k_src, "AA"], w=["ct1"])
            tt(CE, ct2[:], src, APW[:, pw - 1, 1, :], ALU.mult, r=[k_src, "AA"], w=["ct2"])
            tt(CE, dst, xin, ct1[:], ALU.add, r=[k_x, "ct1"], w=[k_dst])
            tt(CE, dst[:, 0:32], dst[:, 0:32], ct2[:, 32:64], ALU.subtract, r=[k_dst, "ct2"], w=[k_dst])
            tt(CE, dst[:, 32:64], dst[:, 32:64], ct2[:, 0:32], ALU.add, r=[k_dst, "ct2"], w=[k_dst])

        if small:
            cp(CE, HS[:, 0, :], Hc[:], r=["Hc", "HS"], w=[("HSs", 0)])
            for n in range(4):
                cstep(HS[:, n + 1, :], HS[:, n, :], HS[:, n + 1, :], 1, ("HSs", n), "HS", ("HSs", n + 1))
            cp(CE, Hc[:], HS[:, 4, :], r=[("HSs", 4)], w=["Hc"])
            cp(CE, HS[:, 0, 0:1], HS[:, 0, 0:1], r=[("HSs", n_) for n_ in range(5)], w=["HS"])
        else:
            NB, BL = 16, 8
            cvc = Carver(); cvc.off = chain_tmp_off
            Cs = Cs_buf; bt1 = cvc.f32([NB, 64]); bt2 = cvc.f32([NB, 64])

            def bulk(dst, src, pw, srcb=False):
                ar = APW[:, pw - 1, 0, :].unsqueeze(1).to_broadcast([128, NB, 64])
                ai = APW[:, pw - 1, 1, :].unsqueeze(1).to_broadcast([128, NB, 64])
                tt(CE, bt1[:], src, ar, ALU.mult, r=["HS", "Cs", "AA"], w=["bt1"])
                tt(CE, bt2[:], src, ai, ALU.mult, r=["HS", "Cs", "AA"], w=["bt2"])
                tt(CE, dst, dst, bt1[:], ALU.add, r=["HS", "bt1"], w=["HS"])
                tt(CE, dst[:, :, 0:32], dst[:, :, 0:32], bt2[:, :, 32:64], ALU.subtract, r=["HS", "bt2"], w=["HS"])
                tt(CE, dst[:, :, 32:64], dst[:, :, 32:64], bt2[:, :, 0:32], ALU.add, r=["HS", "bt2"], w=["HS"])

            V = HS[:, 1:1 + NMC, :].rearrange("p (m i) c -> p m i c", i=BL)
            W = HS[:, 0:NMC, :].rearrange("p (m i) c -> p m i c", i=BL)
            for i in range(1, BL):
                bulk(V[:, :, i, :], V[:, :, i - 1, :], 1)
            cp(CE, Cs[:, 0, :], Hc[:], r=["Hc"], w=[("Cs", 0)])
            for m in range(NB):
                cstep(Cs[:, m + 1, :], Cs[:, m, :], V[:, m, BL - 1, :], BL, ("Cs", m), "HS", ("Cs", m + 1))
            cp(CE, Cs[:, 0, 0:1], Cs[:, 0, 0:1], r=[("Cs", m_) for m_ in range(NB + 1)], w=["Cs"])
            for i in range(1, BL):
                bulk(W[:, :, i, :], Cs[:, 0:NB, :], i)
            cp(CE, W[:, :, 0, :], Cs[:, 0:NB, :], r=["Cs"], w=["HS"])
            cp(CE, Hc[:], Cs[:, NB, :], r=["Cs"], w=["Hc"])
            ws_prefetch()
            P.barrier()
        if small:
            dma("sp", H0t[:].rearrange("p b c -> p (b c)"), din["h0"].rearrange("p b c -> p (b c)"), "h0", w=["H0t"])
            AAb = AA[:].unsqueeze(1).to_broadcast([128, 16, 64])
            ABb = AB[:].unsqueeze(1).to_broadcast([128, 16, 64])
            tt("dve", st1[:], H0t[:], AAb, ALU.mult, r=["H0t", "AA"], w=["st1"])
            tt("dve", st2[:], H0t[:], ABb, ALU.mult, r=["H0t", "AA"], w=["st2"])
            tt("dve", Hso[:], Xsm[:], st1[:], ALU.add, r=["Xsm", "st1"], w=["Hso"])
            tt("dve", Hso[:, :, 0:32], Hso[:, :, 0:32], st2[:, :, 32:64], ALU.subtract, r=["Hso", "st2"], w=["Hso"])
            tt("dve", Hso[:, :, 32:64], Hso[:, :, 32:64], st2[:, :, 0:32], ALU.add, r=["Hso", "st2"], w=["Hso"])
            dma("sp", dout["s5so"].rearrange("p b c -> p (b c)"), Hso[:].rearrange("p b c -> p (b c)"), "s5so", r=["Hso"])
        if KMIX <= 5:
            return
        phase("s5_Y")
        def hb_cast(bq):
            hb = Hbf[bq % 2]
            for ri in range(2):
                col = ri * 32 + 4 * bq
                if small:
                    cp(evac_eng(), hb[:, ri, :, 0:4], HS[:, 0:4, col:col + 4].rearrange("p n q -> p q n"), r=["HS"], w=[("Hbf", bq % 2)])
                    cp(evac_eng(), hb[:, ri, :, 4:20], H0t[:, :, col:col + 4].rearrange("p n q -> p q n"), r=["H0t"], w=[("Hbf", bq % 2)])
                else:
                    cp(evac_eng(), hb[:, ri, :, :NMC], HS[:, 0:NMC, col:col + 4].rearrange("p n q -> p q n"), r=["HS"], w=[("Hbf", bq % 2)])

        hb_cast(0)
        for bq in range(8):
            hb = Hbf[bq % 2]
            if bq + 1 < 8:
                hb_cast(bq + 1)
            b = bank()
            qs = [4 * bq + ql for ql in range(4)]
            for ql in range(4):
                q = 4 * bq + ql
                o = ps[b][:, ql * 128:ql * 128 + NMC]
                mm(o, Toep[:, q, :], Uq[:, q, :NMC], True, False, r=["Toep", ("Uq", q)], w=[PSR(b)])
                mm(o, Win[:, q, 0, :], hb[:, 0, ql, :NMC], False, False, r=["Win", ("Hbf", bq % 2)], w=[PSR(b)])
                mm(o, Win[:, q, 1, :], hb[:, 1, ql, :NMC], False, True, r=["Win", ("Hbf", bq % 2)], w=[PSR(b)])
            yslot = bq % 2
            yvv = (yv if yslot == 0 else yt)
            tt("dve", yvv[:, :, :NMC], Uq[:, 4 * bq:4 * bq + 4, :NMC],
               Dcol[:, 4 * bq:4 * bq + 4].unsqueeze(2).to_broadcast([128, 4, NMC]), ALU.mult,
               r=[("Uq", q) for q in qs] + ["const"], w=[("yv", yslot)])
            tt("dve", yvv[:, :, :NMC], yvv[:, :, :NMC], ps[b][:].rearrange("p (q n) -> p q n", q=4)[:, :, :NMC], ALU.add,
               r=[("yv", yslot), PSR(b)], w=[("yv", yslot)])
            actf(Uq[:, 4 * bq:4 * bq + 4, :NMC], yvv[:, :, :NMC], AF.Gelu_apprx_tanh, r=[("yv", yslot)], w=[("Uq", q) for q in qs])
        for g8 in range(4):
            b = bank()
            for sl in range(8):
                q = 8 * g8 + sl
                tr(psb[b][:NMC, sl * 128:(sl + 1) * 128], Uq[:, q, :NMC], ident_b[:], r=[("Uq", q), "const"], w=[PSR(b)])
            for j in range(4):
                cp(evac_eng(), Utm[:NMC, j, 256 * g8:256 * g8 + 256].rearrange("n (q c) -> n q c", q=8),
                   psb[b][:NMC, :].rearrange("n (q j c) -> n q j c", q=8, j=4)[:, :, j, :], r=[PSR(b)], w=[("Utm", g8)])
        y5v = y5T[:, :, 0:N].rearrange("p c (n j) -> p c n j", j=4)
        for j in range(4):
            b = bank()
            for ch in range(8):
                tr(psb[b][:, ch * 128:ch * 128 + NMC], Utm[:NMC, j, ch * 128:(ch + 1) * 128], ident_b[:NMC, :NMC],
                   r=allU + ["const"], w=[PSR(b)])
            cp(evac_eng(), y5v[:, :, :, j], psb[b][:].rearrange("p (c n) -> p c n", c=8)[:, :, :NMC], r=[PSR(b)], w=["y5T"])
        if KMIX <= 7:
            return
        if si == 0:
            dbg_dump("mg", mg[:, :, 0:128], [128, 8, 128], [("mg", c) for c in range(8)])
            dbg_dump("y5T", y5T[:, :, 0:128], [128, 8, 128], ["y5T"])
        phase("glu")
        for t in range(4):
            wa_, ra = ws_next("wa", t)
            wb_, rb = ws_next("wb", t)
            wav = wa_[:, 0:2048].rearrange("p (k m) -> p k m", k=8)
            wbv = wb_[:, 0:2048].rearrange("p (k m) -> p k m", k=8)
            for sub in range(2):
                o = 2 * t + sub
                s_ = o % 2
                ba = proj_fm(wav, sub, N, ra, src=y5T, skey="y5T_")
                bb_ = proj_fm(wbv, sub, N, rb, src=y5T, skey="y5T_")
                actf(sbt[s_][:, :N], ps[bb_][:, :N], AF.Sigmoid, r=[PSR(bb_)], w=[("sbt", s_)])
                tt("dve", gtmp[s_][:, :N], ps[ba][:, :N], sbt[s_][:, :N], ALU.mult, r=[PSR(ba), ("sbt", s_)], w=[("gtmp", s_)])
                tt("dve", gtmp[s_][:, :N], gtmp[s_][:, :N], mrg[:, o, :N], ALU.mult, r=[("gtmp", s_), ("mrg", o)], w=[("gtmp", s_)])
                tt("dve", mrg[:, o, :N], gtmp[s_][:, :N], mg[:, o, :N], ALU.add, r=[("gtmp", s_), ("mg", o)], w=[("mrg", o)])
        if si == 0:
            dbg_dump("mrg", mrg[:, :, 0:128], [128, 8, 128], [("mrg", c) for c in range(8)])
        for t in range(4):
            wt, rk = ws_next("wo", t)
            wv = wt[:, 0:2048].rearrange("p (k m) -> p k m", k=8)
            for sub in range(2):
                o = 2 * t + sub
                b = proj_fm(wv, sub, N, rk, src=mrg, skey="mrg")
                tt("dve", h[:, o, :N], ps[b][:, :N], h[:, o, :N], ALU.add, r=[PSR(b), ("h", o)], w=[("h", o)])

    import os
    STOP = int(os.environ.get("KSTOP", "99"))
    for si, (c0, N) in enumerate(SEGS):
        if STOP <= 1 or (STOP < 10 and si >= 1):
            break
        dma("sp", h[:, :, :N], din["xT"][:, :, c0:c0 + N], "xin", w=[("h", c) for c in range(8)])
        if STOP >= 2:
            ffn(N, GI_FFN1, "w1", need_barrier=(si == 0))
        if STOP >= 3:
            mixer(si, c0, N)
        cv = Carver(); cv.off = 8192
        yout = cv.f32([8, 512])
        ffn(N, GI_FFN2, "w2", dst=yout, dst_key="yout")
        phase("final_norm")
        rmsnorm(N, GI_FINAL, yout, "yout", src=yout, src_key="yout")
        dma("sp", dout["yT"][:, :, c0:c0 + N], yout[:, :, :N], "yout", r=[("yout", c) for c in range(8)])
    dma("sp", dout["gp"].rearrange("h d e -> d h e"), Sx[:], "gp", r=[("Sx", h_) for h_ in range(4)])
    dma("sp", dout["s5po"], Hc[:], "s5po", r=["Hc"])
    phase("end")
    stats = P.emit()
    if _os.environ.get("KPHASE"):
        import json as _json
        _json.dump(PHASES, open(_os.environ["KPHASE"], "w"))
    return nc, stats


def _tile_cols(W, c0, ncols):
    K = W.shape[0]
    return np.ascontiguousarray(W[:, c0:c0 + ncols].reshape(K // 128, 128, ncols).transpose(1, 0, 2).reshape(128, -1))


def _gain(g):
    return g.reshape(8, 128).T


def _prep_shared(inp):
    f = lambda a: np.asarray(a, dtype=np.float32)
    sh = {}
    for pre, a, b_, c in (("w1", "ffn1_w_gate", "ffn1_w_up", "ffn1_w_down"), ("w2", "ffn2_w_gate", "ffn2_w_up", "ffn2_w_down")):
        Wg, Wu, Wd = f(inp[a])[0], f(inp[b_])[0], f(inp[c])[0]
        sh[pre + "g"] = np.stack([_tile_cols(Wg, 256 * j, 256) for j in range(11)])
        sh[pre + "u"] = np.stack([_tile_cols(Wu, 256 * j, 256) for j in range(11)])
        dt = []
        for o in range(8):
            for kh in range(2):
                blk = Wd[11 * kh * 128:(11 * kh + 11) * 128, 128 * o:128 * o + 128]
                dt.append(blk.reshape(11, 128, 128).transpose(1, 0, 2).reshape(128, 1408))
        sh[pre + "d"] = np.stack(dt)
    Win = f(inp["w_in"])[0]
    cols = []
    cols += [0, 256, 512, 768]
    cols += [1024 + 256 * i for i in range(4)]
    cols += [2048 + 256 * i for i in range(4)]
    cols += [4112 + 256 * i for i in range(4)]
    cols += [3088 + 256 * i for i in range(4)]
    cols += [5136 + 256 * i for i in range(4)]
    sh["win"] = np.stack([_tile_cols(Win, c, 256) for c in cols])
    sh["wglr"] = _tile_cols(Win, 3072, 16)
    for nm, key in (("wgo", "gla_w_out"), ("wa", "s5_w_glu_a"), ("wb", "s5_w_glu_b"), ("wo", "w_out")):
        W = f(inp[key])[0]
        sh[nm] = np.stack([_tile_cols(W, 256 * t, 256) for t in range(4)])
    sh["gains"] = np.ascontiguousarray(np.stack([_gain(f(inp["norm_ffn1"])[0]), _gain(f(inp["norm_mix"])[0]),
                                                 _gain(f(inp["norm_ffn2"])[0]), _gain(f(inp["norm_final"])),
                                                 _gain(f(inp["gla_norm"])[0])], axis=1))
    sh["wup"] = np.concatenate([f(inp["gla_w_gate_up"])[0], f(inp["gla_b_gate"])], axis=0)

    def lay_gp(a):
        return a.reshape(32, 2, 64).transpose(1, 2, 0).reshape(128, 32)
    are = lay_gp(f(inp["s5_a_re"])[0]); aim = lay_gp(f(inp["s5_a_im"])[0])
    ldt = lay_gp(np.repeat(f(inp["s5_log_dt"])[0][:, None], 64, axis=1))
    sh["s5p"] = np.ascontiguousarray(np.stack([are, aim, ldt], axis=1))

    def lay_b(B):
        Bq = B.reshape(32, 2, 64, 16)
        out = np.zeros((2, 64, 32, 2, 16), np.float32)
        for m in range(2):
            out[m, :, :, m, :] = Bq[:, m].transpose(1, 0, 2)
        return out.reshape(128, 1024)

    def lay_c(C):
        Cq = C.reshape(32, 2, 16, 64)
        out = np.zeros((2, 64, 32, 2, 16), np.float32)
        for m in range(2):
            out[m, :, :, m, :] = Cq[:, m].transpose(2, 0, 1)
        return out.reshape(128, 1024)
    sh["s5b"] = np.stack([lay_b(f(inp["s5_b_re"])[0]), lay_b(f(inp["s5_b_im"])[0])], axis=1)
    sh["s5c"] = np.stack([lay_c(f(inp["s5_c_re"])[0]), lay_c(f(inp["s5_c_im"])[0])], axis=1)
    d = f(inp["s5_d"])[0].reshape(32, 2, 16).transpose(1, 2, 0)
    sh["s5d"] = np.ascontiguousarray(np.broadcast_to(d[None], (4, 2, 16, 32)).reshape(128, 32))
    cm = np.zeros((128, 8, 128), np.float32)
    cm[:, 0, :] = np.eye(128)
    idx = np.arange(128)
    same = (idx[:, None] // 64) == (idx[None, :] // 64)
    cm[:, 1, :] = (same & (idx[:, None] <= idx[None, :])).astype(np.float32)
    cm[:, 2, :] = -cm[:, 1, :] / 16.0
    cm[:, 3, :] = -(same & (idx[:, None] > idx[None, :])).astype(np.float32) / 16.0
    cid = np.where(np.arange(80) < 16, 0, 1 + (np.arange(80) - 16) // 4)
    same0 = cid[:, None] == cid[None, :]
    i80 = np.arange(80)
    cm[:80, 4, :80] = (same0 & (i80[:, None] <= i80[None, :])).astype(np.float32)
    cm[:80, 5, :80] = -cm[:80, 4, :80] / 16.0
    cm[:80, 6, :80] = -(same0 & (i80[:, None] > i80[None, :])).astype(np.float32) / 16.0
    cm[:, 7, :] = ((idx[None, :] // 32) >= (idx[:, None] // 32)).astype(np.float32)
    sh["cmat"] = cm
    ci = np.zeros((128, 19), np.float32)
    ci[:, 0] = (idx < 64); ci[:, 1] = (idx >= 64)
    for s in range(80):
        ci[s, 2 + cid[s]] = 1.0
    sh["cind"] = ci
    return {k: np.ascontiguousarray(v, dtype=np.float32) for k, v in sh.items()}


def _prep_core(inp, c):
    f = lambda a: np.asarray(a, dtype=np.float32)
    toks = np.concatenate([f(inp["meta_tokens"]), f(inp["x_sample"])[16 * c:16 * c + 16].reshape(64, D), f(inp["x_prompt"])[c]], axis=0)
    xT = toks.T.reshape(8, 128, NCOL).transpose(1, 0, 2)
    sg = f(inp["state_gla"])[0, 16 * c:16 * c + 16]

    def lay_h(a):
        return a.reshape(16, 32, 2, 64).transpose(2, 3, 0, 1).reshape(128, 16, 32)
    h0 = np.concatenate([lay_h(f(inp["state_s5_re"])[0, 16 * c:16 * c + 16]), lay_h(f(inp["state_s5_im"])[0, 16 * c:16 * c + 16])], axis=2)
    return {"xT": np.ascontiguousarray(xT), "sg": np.ascontiguousarray(sg), "h0": np.ascontiguousarray(h0)}


_CACHE = {}


def kernel(**inputs):
    if "nc" not in _CACHE:
        _CACHE["nc"] = build_program()
    nc, stats = _CACHE["nc"]
    sh = _prep_shared(inputs)
    in_maps = []
    for c in range(NCORES):
        m = dict(sh)
        m.update(_prep_core(inputs, c))
        in_maps.append(m)
    res = run_bass_kernel_spmd(nc, in_maps, core_ids=list(range(NCORES)))
    R = res.results
    y_prompt = np.zeros((8, 2048, D), np.float32)
    y_sample = np.zeros((128, 4, D), np.float32)
    gla_p = np.zeros((1, 8, 4, 128, 256), np.float32)
    re_p = np.zeros((1, 8, 64, 64), np.float32); im_p = np.zeros((1, 8, 64, 64), np.float32)
    gla_s = np.zeros((1, 128, 4, 128, 256), np.float32)
    re_s = np.zeros((1, 128, 64, 64), np.float32); im_s = np.zeros((1, 128, 64, 64), np.float32)
    for c in range(NCORES):
        r = R[c]
        y = np.asarray(r["yT"]).transpose(1, 0, 2).reshape(D, NCOL).T
        y_sample[16 * c:16 * c + 16] = y[16:80].reshape(16, 4, D)
        y_prompt[c] = y[80:]
        gla_p[0, c] = np.asarray(r["gp"])
        gla_s[0, 16 * c:16 * c + 16] = np.asarray(r["gs"])
        hp = np.asarray(r["s5po"])
        un = lambda a: a.reshape(2, 64, 32).transpose(2, 0, 1).reshape(64, 64)
        re_p[0, c] = un(hp[:, 0:32]); im_p[0, c] = un(hp[:, 32:64])
        hs = np.asarray(r["s5so"])
        un2 = lambda a: a.reshape(2, 64, 16, 32).transpose(2, 3, 0, 1).reshape(16, 64, 64)
        re_s[0, 16 * c:16 * c + 16] = un2(hs[:, :, 0:32]); im_s[0, 16 * c:16 * c + 16] = un2(hs[:, :, 32:64])
    return (y_prompt, y_sample, gla_p, re_p, im_p, gla_s, re_s, im_s)
```

```python
import math
import numpy as np
import concourse.bass as bass
import concourse.mybir as mybir
from concourse.bass_utils import run_bass_kernel_spmd

F32 = mybir.dt.float32
BF16 = mybir.dt.bfloat16
I32 = mybir.dt.int32
AF = mybir.ActivationFunctionType
ALU = mybir.AluOpType

D = 1024
DFF = 2816
NCORES = 8
SEGS = [(0, 80), (80, 512), (592, 512), (1104, 512), (1616, 512)]
NCOL = 2128
GI_FFN1, GI_MIX, GI_FFN2, GI_FINAL, GI_GLA = 0, 1, 2, 3, 4
EPS = 1e-6
CHAIN_ENG = "dve"
NW5 = 30
NW1, NW2, NW3 = 16, 20, 16


class Prog:
    def __init__(self, nc):
        self.nc = nc
        self.ops = []
        self.lastw = {}
        self.readers = {}
        self.barrier_ops = None
        self.barrier_done = set()

    def op(self, eng, fn, reads=(), writes=(), dma=None):
        i = len(self.ops)
        deps = {}
        psr = [r for r in reads if isinstance(r, tuple) and r[0] == "ps"]
        if psr:
            reads = [r for r in reads if not (isinstance(r, tuple) and r[0] == "ps")]
            writes = list(writes) + psr
        for r in reads:
            w = self.lastw.get(r)
            if w is not None:
                deps[w] = "raw"
        for r in writes:
            w = self.lastw.get(r)
            if w is not None:
                deps[w] = "raw"
            for rd in self.readers.get(r, ()):
                deps.setdefault(rd, "war")
        if self.barrier_ops is not None and eng not in self.barrier_done:
            for b in self.barrier_ops:
                deps[b] = "raw"
            self.barrier_done.add(eng)
        self.ops.append(dict(eng=eng, fn=fn, deps=deps, dma=dma, marked=False, semname=None, semval=None))
        for r in reads:
            self.readers.setdefault(r, []).append(i)
        for r in writes:
            self.lastw[r] = i
            self.readers[r] = []
        return i

    def barrier(self):
        last = {}
        for i, o in enumerate(self.ops):
            if o["dma"] is not None and o["dma"].startswith("w"):
                continue
            key = o["eng"] if o["dma"] is None else ("dma", o["dma"])
            last[key] = i
        self.barrier_ops = list(last.values())
        self.barrier_done = set()

    @staticmethod
    def _skip(p, o, kind):
        if p["dma"] is None and o["dma"] is None and p["eng"] == o["eng"]:
            if o["eng"] == "pe":
                return True
        return False

    def emit(self):
        nc = self.nc
        engs = {"pe": nc.tensor, "act": nc.scalar, "dve": nc.vector, "pool": nc.gpsimd, "sp": nc.sync}
        ops = self.ops
        for o in ops:
            for d, kind in o["deps"].items():
                if not self._skip(ops[d], o, kind):
                    ops[d]["marked"] = True
        semnames = set()
        for o in ops:
            o["semname"] = ("d_" + o["dma"]) if o["dma"] is not None else ("e_" + o["eng"])
            semnames.add(o["semname"])
        sems = {s: nc.semaphore(s).__enter__() for s in sorted(semnames)}
        cnt = {s: 0 for s in semnames}
        waited = {}
        nwait = 0
        for o in ops:
            E = engs[o["eng"]]
            need = {}
            for d, kind in o["deps"].items():
                p = ops[d]
                if self._skip(p, o, kind):
                    continue
                need[p["semname"]] = max(need.get(p["semname"], 0), p["semval"])
            for s, v in need.items():
                if waited.get((o["eng"], s), 0) < v:
                    E.wait_ge(sems[s], v)
                    waited[(o["eng"], s)] = v
                    nwait += 1
            inst = o["fn"](E)
            s = o["semname"]
            if o["dma"] is not None:
                cnt[s] += 16
                inst.then_inc(sems[s], 16)
                o["semval"] = cnt[s]
            elif o["marked"]:
                cnt[s] += 1
                inst.then_inc(sems[s], 1)
                o["semval"] = cnt[s]
        for s in sorted(semnames):
            if cnt[s] > 0:
                nc.sync.wait_ge(sems[s], cnt[s])
        return dict(n_ops=len(ops), n_wait=nwait, sem_max=max(cnt.values()))


IN_SHAPES = {
    "xT": [128, 8, NCOL],
    "w1g": [11, 128, 2048], "w1u": [11, 128, 2048], "w1d": [16, 128, 1408],
    "w2g": [11, 128, 2048], "w2u": [11, 128, 2048], "w2d": [16, 128, 1408],
    "win": [24, 128, 2048], "wglr": [128, 128],
    "wgo": [4, 128, 2048], "wa": [4, 128, 2048], "wb": [4, 128, 2048], "wo": [4, 128, 2048],
    "gains": [128, 5, 8], "wup": [17, 512],
    "s5p": [128, 3, 32], "s5b": [128, 2, 1024], "s5c": [128, 2, 1024], "s5d": [128, 32],
    "cmat": [128, 8, 128],
    "cind": [128, 19],
    "sg": [16, 4, 128, 256], "h0": [128, 16, 64],
}
OUT_SHAPES = {
    "yT": [128, 8, NCOL], "gp": [4, 128, 256], "gs": [16, 4, 128, 256],
    "s5po": [128, 64], "s5so": [128, 16, 64],
}


def build_program():
    nc = bass.Bass("TRN2", target_bir_lowering=False)
    P = Prog(nc)
    din = {k: nc.dram_tensor(k, s, F32, kind="ExternalInput").ap() for k, s in IN_SHAPES.items()}
    dout = {k: nc.dram_tensor(k, s, F32, kind="ExternalOutput").ap() for k, s in OUT_SHAPES.items()}

    def sb(name, shape, dt=F32):
        return nc.sbuf_tensor("s_" + name, shape, dt).__enter__()

    PHASES = []
    npe = [0]

    def phase(name):
        PHASES.append((name, npe[0]))

    def mm(out, lhsT, rhs, start=True, stop=True, r=(), w=()):
        npe[0] += 1
        P.op("pe", lambda e: e.matmul(out, lhsT, rhs, start=start, stop=stop), r, w)

    def warm(n):
        for _ in range(n):
            P.op("pe", lambda e: e.matmul(ps[7][:, 0:128], ones_b[:], ones_b[:], start=True, stop=True), ["const"], [("ps", 7)])

    def tr(out, in_, ident, r=(), w=()):
        npe[0] += 1
        P.op("pe", lambda e: e.transpose(out, in_, ident), r, w)

    def actf(out, in_, func, r=(), w=(), bias=None, scale=None):
        kw = {}
        if bias is not None:
            kw["bias"] = bias
        if scale is not None:
            kw["scale"] = scale
        P.op("act", lambda e: e.activation(out=out, in_=in_, func=func, **kw), r, w)

    def tt(eng, out, in0, in1, op, r=(), w=()):
        P.op(eng, lambda e: e.tensor_tensor(out=out, in0=in0, in1=in1, op=op), r, w)

    def ts(eng, out, in0, s1, s2, op0, op1, r=(), w=()):
        P.op(eng, lambda e: e.tensor_scalar(out=out, in0=in0, scalar1=s1, scalar2=s2, op0=op0, op1=op1), r, w)

    def tsm(eng, out, in0, s1, r=(), w=()):
        P.op(eng, lambda e: e.tensor_scalar_mul(out=out, in0=in0, scalar1=s1), r, w)

    def stt(out, in0, scalar, in1, op0, op1, r=(), w=()):
        P.op("dve", lambda e: e.scalar_tensor_tensor(out=out, in0=in0, scalar=scalar, in1=in1, op0=op0, op1=op1), r, w)

    def cp(eng, out, in_, r=(), w=()):
        if eng == "act":
            P.op("act", lambda e: e.activation(out=out, in_=in_, func=AF.Copy), r, w)
        else:
            P.op(eng, lambda e: e.tensor_copy(out=out, in_=in_), r, w)

    def recip(out, in_, r=(), w=()):
        P.op("dve", lambda e: e.reciprocal(out=out, in_=in_), r, w)

    def memset(eng, ap, val, w=()):
        P.op(eng, lambda e: e.memset(ap, val), (), w)

    def dma(q, out, in_, key, r=(), w=()):
        P.op(q, lambda e: e.dma_start(out=out, in_=in_), r, w, dma=key)

    ps = [nc.psum_tensor("ps%d" % i, [128, 512], F32).__enter__() for i in range(8)]
    psb = [p[:].bitcast(BF16) for p in ps]
    bank_ctr = [0]
    nbanks = [7]

    def bank():
        b = bank_ctr[0] % nbanks[0]
        bank_ctr[0] += 1
        return b

    def PSR(b):
        return ("ps", b)

    evac_ctr = [0]

    def evac_eng():
        evac_ctr[0] += 1
        return "act" if evac_ctr[0] % 2 == 0 else "dve"

    cmat = sb("cmat", [128, 8, 128])
    cind = sb("cind", [128, 19])
    gains = sb("gains", [128, 5, 8])
    wup = sb("wup", [17, 512])
    ident_f = cmat[:, 0, :]
    MSX = (cmat[:, 1, :], cmat[:, 2, :], cmat[:, 3, :], cind[:, 0:2])
    MS0 = (cmat[:, 4, :], cmat[:, 5, :], cmat[:, 6, :], cind[:, 2:19])
    tmask = cmat[:, 7, :]
    ident_b = sb("ident_b", [128, 128], BF16)
    ones_b = sb("ones_b", [128, 128], BF16)
    ones_f = sb("ones_f", [128, 128])
    onec = sb("onec", [128, 1])
    negpi = sb("negpi", [128, 1])
    epsc = sb("epsc", [128, 1])
    Toep = sb("Toep", [128, 32, 128], BF16)
    Win = sb("Win", [128, 32, 2, 128], BF16)
    WX = sb("WX", [128, 32, 2, 128], BF16)
    APW = sb("APW", [128, 8, 2, 64])
    AA = APW[:, 0, 0, :]
    AB = APW[:, 0, 1, :]
    Dcol = sb("Dcol", [128, 32])
    Sx = sb("Sx", [128, 4, 256])
    Sbf = [sb("Sbf%d" % i, [128, 4, 256], BF16) for i in range(3)]
    Hc = sb("Hc", [128, 64])

    dma("sp", cmat[:], din["cmat"], "c0a", w=["const"])
    dma("sp", cind[:], din["cind"], "c0b", w=["const"])
    dma("sp", gains[:], din["gains"], "c0c", w=["const"])
    dma("sp", wup[:], din["wup"], "c0d", w=["const"])
    dma("sp", Dcol[:], din["s5d"], "c0e", w=["const"])
    memset("dve", ones_b[:], 1.0, w=["const"])
    memset("dve", ones_f[:], 1.0, w=["const"])
    memset("dve", onec[:], 1.0, w=["const"])
    memset("dve", negpi[:], -math.pi, w=["const"])
    memset("dve", epsc[:], EPS, w=["const"])
    memset("dve", Sx[:], 0.0, w=[("Sx", h_) for h_ in range(4)])
    memset("dve", Sbf[0][:], 0.0, w=[("Sbf", 0, h) for h in range(4)])
    memset("dve", Hc[:], 0.0, w=["Hc"])
    cp("dve", ident_b[:], ident_f, r=["const"], w=["const"])

    import os as _os
    KPRE = int(_os.environ.get("KPRE", "99"))

    def s5_precompute():
        temps = []
        if KPRE <= 0:
            return

        def tb(name, shape, dt=F32):
            g = nc.sbuf_tensor("t_" + name, shape, dt)
            t = g.__enter__()
            temps.append(g)
            return t

        prm = tb("prm", [128, 3, 32])
        Bt = tb("Bt", [128, 2, 32, 32])
        Ct = tb("Ct", [128, 2, 32, 32])
        dma("sp", prm[:], din["s5p"], "c1a", w=["prm"])
        dma("sp", Bt[:].rearrange("p a q c -> p a (q c)"), din["s5b"], "c1b", w=["Bt"])
        dma("sp", Ct[:].rearrange("p a q c -> p a (q c)"), din["s5c"], "c1c", w=["Ct"])
        are, aim, ldt = prm[:, 0, :], prm[:, 1, :], prm[:, 2, :]
        dtt = tb("dtt", [128, 32]); lr = tb("lr", [128, 32]); th = tb("th", [128, 32])
        actf(dtt[:], ldt, AF.Exp, r=["prm"], w=["dtt"])
        tt("dve", lr[:], are, dtt[:], ALU.mult, r=["prm", "dtt"], w=["lr"])
        tt("dve", th[:], aim, dtt[:], ALU.mult, r=["prm", "dtt"], w=["th"])
        KS = list(range(-3, 5))
        MAG = tb("MAG", [128, 8, 32]); ARG = tb("ARG", [128, 2, 8, 32]); SC = tb("SC", [128, 2, 8, 32])
        KI = tb("KI", [128, 512], I32); KF = tb("KF", [128, 512])
        OFF = math.pi + 32 * math.pi
        for i, k in enumerate(KS):
            actf(MAG[:, i, :], lr[:], AF.Exp, r=["lr"], w=["MAG"], scale=float(k))
            ts("dve", ARG[:, 0, i, :], th[:], float(k), OFF, ALU.mult, ALU.add, r=["th"], w=["ARG"])
        P.op("dve", lambda e: e.tensor_scalar_add(out=ARG[:, 1, :, :], in0=ARG[:, 0, :, :], scalar1=math.pi / 2), ["ARG"], ["ARG"])
        argf = ARG[:].rearrange("p a k q -> p (a k q)")
        scf = SC[:].rearrange("p a k q -> p (a k q)")
        TWO_PI = 2 * math.pi
        tsm("dve", KI[:], argf, 1.0 / TWO_PI, r=["ARG"], w=["KI"])
        cp("dve", KF[:], KI[:], r=["KI"], w=["KF"])
        stt(argf, KF[:], -TWO_PI, argf, ALU.mult, ALU.add, r=["KF", "ARG"], w=["ARG"])
        ts("dve", KF[:], argf, 0.0, TWO_PI, ALU.is_lt, ALU.mult, r=["ARG"], w=["KF"])
        tt("dve", argf, argf, KF[:], ALU.add, r=["ARG", "KF"], w=["ARG"])
        ts("dve", KF[:], argf, TWO_PI, -TWO_PI, ALU.is_ge, ALU.mult, r=["ARG"], w=["KF"])
        tt("dve", argf, argf, KF[:], ALU.add, r=["ARG", "KF"], w=["ARG"])
        actf(scf, argf, AF.Sin, r=["ARG", "const"], w=["SC"], bias=negpi[:, 0:1], scale=1.0)
        PRE = tb("PRE", [128, 8, 32]); PIM = tb("PIM", [128, 8, 32])
        tt("dve", PRE[:], MAG[:], SC[:, 1, :, :], ALU.mult, r=["MAG", "SC"], w=["PRE"])
        tt("dve", PIM[:], MAG[:], SC[:, 0, :, :], ALU.mult, r=["MAG", "SC"], w=["PIM"])

        if KPRE <= 1:
            P.barrier()
            for g in reversed(temps):
                g.__exit__(None, None, None)
            return
        pw1 = tb("pw1", [128, 32]); pw2 = tb("pw2", [128, 32])
        cp("dve", APW[:, 0, 0, 0:32], PRE[:, 7, :], r=["PRE"], w=["AA"])
        cp("dve", APW[:, 0, 1, 0:32], PIM[:, 7, :], r=["PIM"], w=["AA"])
        for i in range(1, 8):
            cr, ci = APW[:, i - 1, 0, 0:32], APW[:, i - 1, 1, 0:32]
            tt("dve", pw1[:], cr, PRE[:, 7, :], ALU.mult, r=["AA", "PRE"], w=["pw1"])
            tt("dve", pw2[:], ci, PIM[:, 7, :], ALU.mult, r=["AA", "PIM"], w=["pw2"])
            tt("dve", APW[:, i, 0, 0:32], pw1[:], pw2[:], ALU.subtract, r=["pw1", "pw2"], w=["AA"])
            tt("dve", pw1[:], cr, PIM[:, 7, :], ALU.mult, r=["AA", "PIM"], w=["pw1"])
            tt("dve", pw2[:], ci, PRE[:, 7, :], ALU.mult, r=["AA", "PRE"], w=["pw2"])
            tt("dve", APW[:, i, 1, 0:32], pw1[:], pw2[:], ALU.add, r=["pw1", "pw2"], w=["AA"])
        cp("dve", APW[:, :, :, 32:64], APW[:, :, :, 0:32], r=["AA"], w=["AA"])
        nre = tb("nre", [128, 32]); den = tb("den", [128, 32]); t0 = tb("t0", [128, 32]); t1 = tb("t1s", [128, 32])
        cre = tb("cre", [128, 32]); cim = tb("cim", [128, 32])
        P.op("dve", lambda e: e.tensor_scalar_add(out=nre[:], in0=PRE[:, 4, :], scalar1=-1.0), ["PRE"], ["nre"])
        nim = PIM[:, 4, :]
        tt("dve", den[:], are, are, ALU.mult, r=["prm"], w=["den"])
        tt("dve", t0[:], aim, aim, ALU.mult, r=["prm"], w=["t0"])
        tt("dve", den[:], den[:], t0[:], ALU.add, r=["den", "t0"], w=["den"])
        recip(den[:], den[:], r=["den"], w=["den"])
        tt("dve", t0[:], nre[:], are, ALU.mult, r=["nre", "prm"], w=["t0"])
        tt("dve", t1[:], nim, aim, ALU.mult, r=["PIM", "prm"], w=["t1"])
        tt("dve", t0[:], t0[:], t1[:], ALU.add, r=["t0", "t1"], w=["t0"])
        tt("dve", cre[:], t0[:], den[:], ALU.mult, r=["t0", "den"], w=["cre"])
        tt("dve", t0[:], nim, are, ALU.mult, r=["PIM", "prm"], w=["t0"])
        tt("dve", t1[:], nre[:], aim, ALU.mult, r=["nre", "prm"], w=["t1"])
        tt("dve", t0[:], t0[:], t1[:], ALU.subtract, r=["t0", "t1"], w=["t0"])
        tt("dve", cim[:], t0[:], den[:], ALU.mult, r=["t0", "den"], w=["cim"])

        u1 = tb("u1", [128, 32, 32]); u2 = tb("u2", [128, 32, 32])

        def bc(x):
            return x.unsqueeze(2).to_broadcast([128, 32, 32])

        def cmul(ore, oim, xr, xi, yr, yi, rr, ww, neg_im=False):
            tt("dve", u1[:], yr, bc(xr), ALU.mult, r=rr, w=["u1"])
            tt("dve", u2[:], yi, bc(xi), ALU.mult, r=rr, w=["u2"])
            tt("dve", ore, u1[:], u2[:], ALU.subtract, r=["u1", "u2"], w=ww)
            tt("dve", u1[:], yi, bc(xr), ALU.mult, r=rr, w=["u1"])
            tt("dve", u2[:], yr, bc(xi), ALU.mult, r=rr, w=["u2"])
            if neg_im:
                stt(oim, u1[:], -1.0, u2[:], ALU.mult, ALU.subtract, r=["u1", "u2"], w=ww)
            else:
                tt("dve", oim, u1[:], u2[:], ALU.add, r=["u1", "u2"], w=ww)

        BB = tb("BB", [128, 2, 32, 32])
        cmul(BB[:, 0], BB[:, 1], cre[:], cim[:], Bt[:, 0], Bt[:, 1], ["cre", "cim", "Bt"], ["BB"])

        if KPRE <= 2:
            P.barrier()
            for g in reversed(temps):
                g.__exit__(None, None, None)
            return
        BP = tb("BP", [128, 2, 32, 4, 32])
        for j in range(4):
            idx = (3 - j) + 3
            cmul(BP[:, 0, :, j, :], BP[:, 1, :, j, :], PRE[:, idx, :], PIM[:, idx, :], BB[:, 0], BB[:, 1],
                 ["PRE", "PIM", "BB"], [("BP", j)])
        BPf = BP[:].rearrange("p a q j c -> p a q (j c)")
        for q0 in range(0, 32, 2):
            b = bank()
            for ql in range(2):
                for ri in range(2):
                    sl = ql * 2 + ri
                    tr(ps[b][:, sl * 128:(sl + 1) * 128], BPf[:, ri, q0 + ql, :], ident_f,
                       r=[("BP", j) for j in range(4)] + ["const"], w=[PSR(b)])
            cp(evac_eng(), WX[:, q0:q0 + 2, :, :], ps[b][:].rearrange("p (q r m) -> p q r m", q=2, r=2), r=[PSR(b)], w=["WX"])

        if KPRE <= 3:
            P.barrier()
            for g in reversed(temps):
                g.__exit__(None, None, None)
            return
        LL = BP
        for j in range(4):
            idx = 3 - j
            cmul(LL[:, 0, :, j, :], LL[:, 1, :, j, :], PRE[:, idx, :], PIM[:, idx, :], BB[:, 0], BB[:, 1],
                 ["PRE", "PIM", "BB"], [("LL", j)] + [("BP", j_) for j_ in range(4)])
        CP = tb("CP", [128, 2, 32, 5, 32])
        for k in range(5):
            idx = k + 3
            cmul(CP[:, 0, :, k, :], CP[:, 1, :, k, :], PRE[:, idx, :], PIM[:, idx, :], Ct[:, 0], Ct[:, 1],
                 ["PRE", "PIM", "Ct"], [("CP", k)], neg_im=True)
        LLf = LL[:].rearrange("p a q j c -> p a q (j c)")
        CPf = CP[:].rearrange("p a q k c -> p a q (k c)")
        allL = [("LL", j) for j in range(4)]
        allC = [("CP", k) for k in range(5)]

        if KPRE <= 4:
            P.barrier()
            for g in reversed(temps):
                g.__exit__(None, None, None)
            return
        for q0 in range(0, 32, 4):
            b = bank()
            for ql in range(4):
                q = q0 + ql
                o = ps[b][:, ql * 128:(ql + 1) * 128]
                mm(o, LLf[:, 0, q, :], CPf[:, 0, q, 0:128], True, False, r=allL + allC, w=[PSR(b)])
                mm(o, LLf[:, 1, q, :], CPf[:, 1, q, 0:128], False, True, r=allL + allC, w=[PSR(b)])
            tt("dve", Toep[:, q0:q0 + 4, :], ps[b][:].rearrange("p (q m) -> p q m", q=4),
               tmask.unsqueeze(1).to_broadcast([128, 4, 128]), ALU.mult, r=[PSR(b), "const"], w=["Toep"])

        if KPRE <= 5:
            P.barrier()
            for g in reversed(temps):
                g.__exit__(None, None, None)
            return
        for ri in range(2):
            cp("dve" if ri == 0 else "act", Win[:, :, ri, :], CPf[:, ri, :, 32:160], r=allC, w=["Win"])
        P.barrier()
        for g in reversed(temps):
            g.__exit__(None, None, None)

    s5_precompute()

    NSLOT = 5
    wbf = [sb("wbf%d" % i, [128, 2048], BF16) for i in range(NSLOT)]
    h = sb("h", [128, 8, 512])
    u = sb("u", [128, 8, 512], BF16)
    rstd = sb("rstd", [128, 512]); rtmp = sb("rtmp", [128, 512])
    mg = sb("mg", [128, 8, 512], BF16)
    y5T = sb("y5T", [128, 8, 512], BF16)
    ARENA_WORDS = 20480
    arena = sb("arena", [128, ARENA_WORDS])

    class Carver:
        def __init__(self):
            self.off = 0

        def f32(self, shape):
            n = int(np.prod(shape))
            a = arena[:, self.off:self.off + n]
            self.off += n
            assert self.off <= ARENA_WORDS, self.off
            return self._shape(a, shape)

        def bf(self, shape):
            n = int(np.prod(shape))
            w = (n + 1) // 2
            a = arena[:, self.off:self.off + w].bitcast(BF16)[:, 0:n]
            self.off += w
            assert self.off <= ARENA_WORDS, self.off
            return self._shape(a, shape)

        @staticmethod
        def _shape(a, shape):
            if len(shape) == 1:
                return a
            if len(shape) == 2:
                return a.rearrange("p (a b) -> p a b", a=shape[0])
            if len(shape) == 3:
                return a.rearrange("p (a b c) -> p a b c", a=shape[0], b=shape[1])
            raise ValueError

    seq = []
    for (c0, N) in SEGS:
        for pre in ("w1",):
            for j in range(11):
                seq.append((pre + "g", j, 2048)); seq.append((pre + "u", j, 2048))
            for t in range(16):
                seq.append((pre + "d", t, 1408))
        for t in range(12):
            seq.append(("win", t, 2048))
        seq.append(("wglr", None, 128))
        for t in range(12, 16):
            seq.append(("win", t, 2048))
        for t in range(4):
            seq.append(("wgo", t, 2048))
        for t in range(16, 20):
            seq.append(("win", t, 2048))
        for t in range(4):
            seq.append(("win", 20 + t, 2048))
        for t in range(4):
            seq.append(("wa", t, 2048)); seq.append(("wb", t, 2048))
        for t in range(4):
            seq.append(("wo", t, 2048))
        for pre in ("w2",):
            for j in range(11):
                seq.append((pre + "g", j, 2048)); seq.append((pre + "u", j, 2048))
            for t in range(16):
                seq.append((pre + "d", t, 1408))
    ws_state = dict(issued=0, k=0)
    PF = 2
    pfcur = [PF]

    def ws_issue(k):
        name, t, E = seq[k]
        src = din[name] if t is None else din[name][t]
        slot = k % NSLOT
        dma("pool", wbf[slot][:, 0:E], src, "w%d" % slot, w=[("w", slot)])

    def ws_prefetch():
        k = ws_state["k"]
        while ws_state["issued"] < min(len(seq), k + NSLOT):
            ws_issue(ws_state["issued"])
            ws_state["issued"] += 1

    def ws_next(name, t):
        k = ws_state["k"]
        assert seq[k][0] == name and seq[k][1] == t, (seq[k], name, t)
        while ws_state["issued"] < min(len(seq), k + 1 + pfcur[0]):
            ws_issue(ws_state["issued"])
            ws_state["issued"] += 1
        ws_state["k"] += 1
        return wbf[k % NSLOT], ("w", k % NSLOT)

    def rmsnorm(N, gi, dst, dst_key, src=None, src_key="h"):
        src = h if src is None else src
        b = bank()
        for c in range(8):
            if c % 2 == 0:
                actf(u[:, c, :N], src[:, c, :N], AF.Square, r=[(src_key, c)], w=[("u", c)])
            else:
                tt("dve", u[:, c, :N], src[:, c, :N], src[:, c, :N], ALU.mult, r=[(src_key, c)], w=[("u", c)])
        for c in range(8):
            mm(ps[b][:, :N], ones_b[:], u[:, c, :N], c == 0, c == 7, r=[("u", c), "const"], w=[PSR(b)])
        actf(rtmp[:, :N], ps[b][:, :N], AF.Ln, r=[PSR(b)], w=["rtmp"], bias=epsc[:, 0:1], scale=1.0 / D)
        actf(rstd[:, :N], rtmp[:, :N], AF.Exp, r=["rtmp"], w=["rstd"], scale=-0.5)
        for c in range(8):
            stt(dst[:, c, :N], src[:, c, :N], gains[:, gi, c:c + 1], rstd[:, :N], ALU.mult, ALU.mult,
                r=[(src_key, c), "rstd", "const"], w=[(dst_key, c)])

    def ffn(N, gi, pre, need_barrier=True, dst=None, dst_key="h"):
        if need_barrier:
            ws_prefetch()
            P.barrier()
        cv = Carver()
        act = cv.bf([22, 512])
        sgt = [cv.f32([512]) for _ in range(2)]
        phase("ffn_norm")
        rmsnorm(N, gi, u, "u")
        if N == 512:
            warm(NW5)
        phase("ffn_gu")
        pfcur[0] = 3
        for j in range(11):
            wg, rg = ws_next(pre + "g", j)
            wu, ru = ws_next(pre + "u", j)
            wgv = wg[:, 0:2048].rearrange("p (k m) -> p k m", k=8)
            wuv = wu[:, 0:2048].rearrange("p (k m) -> p k m", k=8)
            for half in range(2):
                c = 2 * j + half
                bg = bank()
                for kt in range(8):
                    mm(ps[bg][:, :N], wgv[:, kt, half * 128:(half + 1) * 128], u[:, kt, :N], kt == 0, kt == 7,
                       r=[rg, ("u", kt)], w=[PSR(bg)])
                bu = bank()
                for kt in range(8):
                    mm(ps[bu][:, :N], wuv[:, kt, half * 128:(half + 1) * 128], u[:, kt, :N], kt == 0, kt == 7,
                       r=[ru, ("u", kt)], w=[PSR(bu)])
                s = c % 2
                actf(sgt[s][:, :N], ps[bg][:, :N], AF.Silu, r=[PSR(bg)], w=[("sgt", s)])
                tt("dve", act[:, c, :N], sgt[s][:, :N], ps[bu][:, :N], ALU.mult, r=[("sgt", s), PSR(bu)], w=[("act", c)])
        phase("ffn_down")
        for o in range(8):
            b = bank()
            for kh in range(2):
                wd, rd = ws_next(pre + "d", 2 * o + kh)
                wdv = wd[:, 0:1408].rearrange("p (k m) -> p k m", k=11)
                for k in range(11):
                    ct = 11 * kh + k
                    mm(ps[b][:, :N], wdv[:, k, :], act[:, ct, :N], ct == 0, ct == 21, r=[rd, ("act", ct)], w=[PSR(b)])
            dstb = h if dst is None else dst
            stt(dstb[:, o, :N], ps[b][:, :N], 0.5, h[:, o, :N], ALU.mult, ALU.add, r=[PSR(b), ("h", o)], w=[(dst_key, o)])
        pfcur[0] = PF

    def proj_fm(wv, sub, N, rkey, src=None, skey="u"):
        src = u if src is None else src
        b = bank()
        for kt in range(8):
            sk = "y5T" if skey == "y5T_" else (skey, kt)
            mm(ps[b][:, :N], wv[:, kt, sub * 128:(sub + 1) * 128], src[:, kt, :N], kt == 0, kt == 7,
               r=[rkey, sk], w=[PSR(b)])
        return b

    xch = [0]

    KDUMP = int(_os.environ.get("KDUMP", "0"))

    def dbg_dump(name, ap, shape, rkeys):
        if not KDUMP:
            return
        d = nc.dram_tensor("dbg_" + name, shape, F32, kind="ExternalOutput").ap()
        dma("pool", d, ap, "dbg", r=rkeys)

    KMIX = int(_os.environ.get("KMIX", "99"))
    KGLA = int(_os.environ.get("KGLA", "99"))

    def mixer(si, c0seg, N):
        small = (N == 80)
        NMC = N // 4
        ws_prefetch()
        P.barrier()
        cv = Carver()
        NS = 128 if small else 512
        ohat = cv.bf([8, NS])
        siga = cv.bf([8, NS])
        qT32 = cv.f32([4, NS]); kT32 = cv.f32([4, NS])
        ktm = cv.f32([NS // 128, 512]); vtm = cv.bf([NS // 128, 1024]); silur = cv.bf([8, NS])
        gtm = cv.f32([512]); eb = cv.f32([4, 128]); enb = cv.f32([4, 128])
        erev = gtm
        NKDM = 4
        kd = cv.bf([512]); kdm = [cv.bf([128]) for _ in range(NKDM)]
        kctr = [0]
        qe = cv.bf([4, 128]); ke = cv.bf([4, 128]); scT = cv.bf([4, 128])
        o32 = cv.f32([8, 128]); osq = cv.bf([8, 128]); rs = cv.f32([4, 128])
        otmp2 = [cv.f32([128]) for _ in range(2)]
        glr = cv.f32([NS])[0:17, :]
        memset("dve", glr[:, :], 1.0, w=["glr"])
        if small:
            NSL = 8
            Sld = [cv.f32([256]) for _ in range(NSL)]
            Sout = [cv.f32([256]) for _ in range(NSL)]
            Sbs = cv.bf([16, 256])

        tiles = [(0, 0, 80)] if small else [(ti, 128 * ti, 128) for ti in range(4)]
        phase("mix_norm")
        rmsnorm(N, GI_MIX, u, "u")
        if not small:
            warm(NW5)
        phase("mix_proj")
        for t in range(2):
            wt, rk = ws_next("win", t)
            wv = wt[:, 0:2048].rearrange("p (k m) -> p k m", k=8)
            for sub in range(2):
                hh = 2 * t + sub
                b = proj_fm(wv, sub, N, rk)
                cp(evac_eng(), qT32[:, hh, :N], ps[b][:, :N], r=[PSR(b)], w=[("qT", hh)])
        for t in range(2):
            wt, rk = ws_next("win", 2 + t)
            wv = wt[:, 0:2048].rearrange("p (k m) -> p k m", k=8)
            for sub in range(2):
                hh = 2 * t + sub
                b = proj_fm(wv, sub, N, rk)
                cp(evac_eng(), kT32[:, hh, :N], ps[b][:, :N], r=[PSR(b)], w=[("kT", hh)])
            for (ti, tc0, R) in tiles:
                b = bank()
                for kt in range(8):
                    mm(ps[b][:R, 0:256], u[:, kt, tc0:tc0 + R], wv[:, kt, :], kt == 0, kt == 7, r=[rk, ("u", kt)], w=[PSR(b)])
                cp(evac_eng(), ktm[:R, ti, 256 * t:256 * t + 256], ps[b][:R, 0:256], r=[PSR(b)], w=[("ktm", ti)])
        for t in range(4):
            wt, rk = ws_next("win", 4 + t)
            wv = wt[:, 0:2048].rearrange("p (k m) -> p k m", k=8)
            for (ti, tc0, R) in tiles:
                b = bank()
                for kt in range(8):
                    mm(ps[b][:R, 0:256], u[:, kt, tc0:tc0 + R], wv[:, kt, :], kt == 0, kt == 7, r=[rk, ("u", kt)], w=[PSR(b)])
                cp(evac_eng(), vtm[:R, ti, 256 * t:256 * t + 256], ps[b][:R, 0:256], r=[PSR(b)], w=[("vtm", ti)])
        for t in range(4):
            wt, rk = ws_next("win", 8 + t)
            wv = wt[:, 0:2048].rearrange("p (k m) -> p k m", k=8)
            for sub in range(2):
                c8 = 2 * t + sub
                b = proj_fm(wv, sub, N, rk)
                actf(silur[:, c8, :N], ps[b][:, :N], AF.Silu, r=[PSR(b)], w=[("silur", c8)])
        wt, rk = ws_next("wglr", None)
        wv = wt[:, 0:128].rearrange("p (k m) -> p k m", k=8)
        b = bank()
        for kt in range(8):
            mm(ps[b][:16, :N], wv[:, kt, :], u[:, kt, :N], kt == 0, kt == 7, r=[rk, ("u", kt)], w=[PSR(b)])
        cp("dve", glr[0:16, :N], ps[b][:16, :N], r=[PSR(b)], w=["glr"])

        if KMIX <= 1:
            return
        phase("gla_core")
        def emit_gate_a(t):
            wt, rk = ws_next("win", 12 + t)
            wv = wt[:, 0:2048].rearrange("p (k m) -> p k m", k=8)
            for sub in range(2):
                o = 2 * t + sub
                b = proj_fm(wv, sub, N, rk)
                actf(siga[:, o, :N], ps[b][:, :N], AF.Sigmoid, r=[PSR(b)], w=[("siga", o)])

        nbanks[0] = 5
        for (ti, tc0, R) in tiles:
            mask, tri, trirev, ind = MS0 if small else MSX
            if small:
                chunks = [(0, 0, 16, ("x", None))] + [(1 + bb, 16 + 4 * bb, 20 + 4 * bb, ("s", bb)) for bb in range(16)]
            else:
                chunks = [(0, 0, 64, ("x", None)), (1, 64, 128, ("x", None))]
            b1 = bank()
            mm(ps[b1][:R, :], glr[0:17, tc0:tc0 + R], wup[0:17, :], r=["glr", "const"], w=[PSR(b1)])
            if not small:
                warm(NW1)
            actf(gtm[:R, :], ps[b1][:R, :], AF.Exp, r=[PSR(b1)], w=["gtm"], scale=-1.0)
            actf(gtm[:R, :], gtm[:R, :], AF.Ln, r=["gtm", "const"], w=["gtm"], bias=onec[:R, 0:1], scale=1.0)
            if KGLA <= 1:
                continue
            b2 = bank()
            for hh in range(4):
                mm(ps[b2][:, hh * 128:hh * 128 + R], gtm[:R, hh * 128:(hh + 1) * 128], tri[:R, :R], r=["gtm", "const"], w=[PSR(b2)])
            psv = ps[b2][:].rearrange("p (h c) -> p h c", h=4)[:, :, :R]
            actf(eb[:, :, :R], psv, AF.Exp, r=[PSR(b2)], w=["eb"])
            actf(enb[:, :, :R], psv, AF.Exp, r=[PSR(b2)], w=["enb"], scale=-1.0)
            if KGLA <= 2:
                continue
            b3 = bank()
            mm(ps[b3][:R, :], trirev[:R, :R], gtm[:R, :], r=["gtm", "const"], w=[PSR(b3)])
            if not small:
                warm(NW2)
            actf(erev[:R, :], ps[b3][:R, :], AF.Exp, r=[PSR(b3)], w=["gtm"])
            tt("dve", kd[:R, :], ktm[:R, ti, :], erev[:R, :], ALU.mult, r=[("ktm", ti), "gtm"], w=["kd"])
            if KGLA <= 3:
                continue
            for hh in range(4):
                stt(qe[:, hh, :R], qT32[:, hh, tc0:tc0 + R], 128.0 ** -0.5, eb[:, hh, :R], ALU.mult, ALU.mult,
                    r=[("qT", hh), "eb"], w=[("qe", hh)])
                tt("dve", ke[:, hh, :R], kT32[:, hh, tc0:tc0 + R], enb[:, hh, :R], ALU.mult, r=[("kT", hh), "enb"], w=[("ke", hh)])
            b4 = bank()
            for hh in range(4):
                mm(ps[b4][:R, hh * 128:hh * 128 + R], ke[:, hh, :R], qe[:, hh, :R], r=[("ke", hh), ("qe", hh)], w=[PSR(b4)])
            for hh in range(4):
                tt("dve", scT[:R, hh, :R], ps[b4][:R, hh * 128:hh * 128 + R], mask[:R, :R], ALU.mult,
                   r=[PSR(b4), "const"], w=[("scT", hh)])
            if KGLA <= 4:
                continue
            if not small:
                emit_gate_a(ti)
            bo = [5, 6]
            x0 = xch[0]
            for hh in range(4):
                xc = x0
                ent = []
                kvb = []
                LAG = 3

                def upd(ci):
                    nonlocal xc
                    (cidx, lo, hi, kind) = chunks[ci]
                    bk, half = kvb[ci]
                    dec = eb[:, hh, hi - 1:hi]
                    if kind[0] == "x":
                        ent.append((Sbf[xc % 3][:, hh, :], ("Sbf", xc % 3, hh)))
                        stt(Sx[:, hh, :], Sx[:, hh, :], dec, ps[bk][:, half:half + 256], ALU.mult, ALU.add,
                            r=[("Sx", hh), "eb", PSR(bk)], w=[("Sx", hh)])
                        xc += 1
                        cp("act", Sbf[xc % 3][:, hh, :], Sx[:, hh, :], r=[("Sx", hh)], w=[("Sbf", xc % 3, hh)])
                    else:
                        bb = kind[1]
                        sl = bb % NSL
                        ent.append((Sbs[:, bb, :], ("Sbs", bb)))
                        stt(Sout[sl][:, :], Sld[sl][:, :], dec, ps[bk][:, half:half + 256], ALU.mult, ALU.add,
                            r=[("Sld", sl), "eb", PSR(bk)], w=[("Sout", sl)])
                        dma("sp", dout["gs"][bb, hh], Sout[sl][:, :], "sst%d" % sl, r=[("Sout", sl)])

                for ci, (cidx, lo, hi, kind) in enumerate(chunks):
                    kslot = (kctr[0]) % NKDM
                    kctr[0] += 1
                    tsm("dve", kdm[kslot][:R, :], kd[:R, hh * 128:(hh + 1) * 128], ind[:R, cidx:cidx + 1],
                        r=["kd", "const"], w=[("kdm", kslot)])
                    if len(kvb) % 2 == 0:
                        bk = bank()
                    half = (len(kvb) % 2) * 256
                    mm(ps[bk][:, half:half + 256], kdm[kslot][:R, :], vtm[:R, ti, hh * 256:(hh + 1) * 256],
                       r=[("kdm", kslot), ("vtm", ti)], w=[PSR(bk)])
                    kvb.append((bk, half))
                    if kind[0] == "s":
                        bb = kind[1]
                        sl = bb % NSL
                        dma("sp", Sld[sl][:, :], din["sg"][bb, hh], "sld%d" % sl, w=[("Sld", sl)])
                        cp("act", Sbs[:, bb, :], Sld[sl][:, :], r=[("Sld", sl)], w=[("Sbs", bb)])
                    if ci >= LAG:
                        upd(ci - LAG)
                for ci in range(max(0, len(chunks) - LAG), len(chunks)):
                    upd(ci)
                if hh == 3:
                    xch[0] = xc
                if KGLA <= 5:
                    continue
                pso = ps[bo[hh // 2]]
                for e2 in range(2):
                    base = (hh % 2) * 256 + e2 * 128
                    mm(pso[:, base:base + R], vtm[:R, ti, hh * 256 + e2 * 128:hh * 256 + (e2 + 1) * 128], scT[:R, hh, :R],
                       True, False, r=[("vtm", ti), ("scT", hh)], w=[PSR(bo[hh // 2])])
                    for ci, (cidx, lo, hi, kind) in enumerate(chunks):
                        Sap, Skey = ent[ci]
                        mm(pso[:, base + lo:base + hi], Sap[:, e2 * 128:(e2 + 1) * 128], qe[:, hh, lo:hi],
                           False, ci == len(chunks) - 1, r=[Skey, ("qe", hh)], w=[PSR(bo[hh // 2])])
            if KGLA <= 6:
                continue
            if not small:
                warm(NW3)
            for k2 in range(2):
                pv = ps[bo[k2]][:].rearrange("p (c n) -> p c n", c=4)[:, :, :R]
                cp("dve", o32[:, 4 * k2:4 * k2 + 4, :R], pv, r=[PSR(bo[k2])], w=[("o32", k2)])
                actf(osq[:, 4 * k2:4 * k2 + 4, :R], pv, AF.Square, r=[PSR(bo[k2])], w=[("osq", k2)])
            if KGLA <= 7:
                continue
            bS = bank()
            for hh in range(4):
                for e2 in range(2):
                    mm(ps[bS][:, hh * 128:hh * 128 + R], ones_b[:], osq[:, 2 * hh + e2, :R], e2 == 0, e2 == 1,
                       r=[("osq", hh // 2), "const"], w=[PSR(bS)])
            if KGLA <= 8:
                continue
            psS = ps[bS][:].rearrange("p (h c) -> p h c", h=4)[:, :, :R]
            actf(rs[:, :, :R], psS, AF.Ln, r=[PSR(bS)], w=["rs"], bias=epsc[:, 0:1], scale=1.0 / 256)
            actf(rs[:, :, :R], rs[:, :, :R], AF.Exp, r=["rs"], w=["rs"], scale=-0.5)
            if KGLA <= 9:
                continue
            if si == 0 and KDUMP:
                dbg_dump("o32", o32[:], [128, 8, 128], [("o32", 0), ("o32", 1)])
                dbg_dump("rs", rs[:], [128, 4, 128], ["rs"])
                dbg_dump("silur", silur[:, :, 0:128], [128, 8, 128], [("silur", c) for c in range(8)])
                dbg_dump("scT", scT[:], [128, 4, 128], [("scT", c) for c in range(4)])
                dbg_dump("qe", qe[:], [128, 4, 128], [("qe", c) for c in range(4)])
                dbg_dump("ke", ke[:], [128, 4, 128], [("ke", c) for c in range(4)])
            for c8 in range(8):
                stt(otmp2[c8 % 2][:, :R], o32[:, c8, :R], gains[:, GI_GLA, c8:c8 + 1], rs[:, c8 // 2, :R], ALU.mult, ALU.mult,
                    r=[("o32", c8 // 4), "rs", "const"], w=[("otmp", c8 % 2)])
                tt("pool", ohat[:, c8, tc0:tc0 + R], otmp2[c8 % 2][:, :R], silur[:, c8, tc0:tc0 + R], ALU.mult,
                   r=[("otmp", c8 % 2), ("silur", c8)], w=[("ohat", c8)])

        if KMIX <= 2:
            return
        nbanks[0] = 7
        if si == 0:
            dbg_dump("ohat", ohat[:, :, 0:128], [128, 8, 128], [("ohat", c) for c in range(8)])
        phase("gla_out")
        if small:
            for t in range(4):
                emit_gate_a(t)
        for t in range(4):
            wt, rk = ws_next("wgo", t)
            wv = wt[:, 0:2048].rearrange("p (k m) -> p k m", k=8)
            for sub in range(2):
                o = 2 * t + sub
                b = proj_fm(wv, sub, N, rk, src=ohat, skey="ohat")
                tt("dve", mg[:, o, :N], ps[b][:, :N], siga[:, o, :N], ALU.mult, r=[PSR(b), ("siga", o)], w=[("mg", o)])

        if KMIX <= 3:
            return
        ws_prefetch()
        P.barrier()
        cv = Carver()
        Uraw = cv.bf([4096])
        UtmA = Uraw.rearrange("p (q j c) -> p q j c", q=32, j=4)
        UtmAf = Uraw.rearrange("p (q x) -> p q x", q=32)
        Utm = Uraw.rearrange("p (j f) -> p j f", j=4)
        Uq = cv.bf([32, 128])
        HS = cv.f32([NMC + 1, 64])
        Hbf = [cv.bf([2, 4, 128]) for _ in range(2)]
        yv = cv.f32([4, 128]); yt = cv.f32([4, 128])
        ct1 = cv.f32([64]); ct2 = cv.f32([64])
        if small:
            H0t = cv.f32([16, 64]); Xsm = cv.f32([16, 64]); Hso = cv.f32([16, 64])
            st1 = cv.f32([16, 64]); st2 = cv.f32([16, 64])
        Cs_buf = cv.f32([17, 64])
        chain_tmp_off = cv.off
        sbt = [cv.f32([NS]) for _ in range(2)]
        gtmp = [cv.f32([NS]) for _ in range(2)]
        mrg = cv.bf([8, NS])
        phase("s5_proj")
        uv = u[:, :, 0:N].rearrange("p k (n j) -> p k n j", j=4)
        for t in range(4):
            wt, rk = ws_next("win", 16 + t)
            wv = wt[:, 0:2048].rearrange("p (k m) -> p k m", k=8)
            for j in range(4):
                b = bank()
                for kt in range(8):
                    mm(ps[b][:NMC, 0:256], uv[:, kt, :, j], wv[:, kt, :], kt == 0, kt == 7, r=[rk, ("u", kt)], w=[PSR(b)])
                cp(evac_eng(), UtmA[:NMC, 8 * t:8 * t + 8, j, :], ps[b][:NMC, 0:256].rearrange("n (q c) -> n q c", q=8),
                   r=[PSR(b)], w=[("Utm", t)])
        allU = [("Utm", t) for t in range(4)]
        phase("s5_trX")
        for g8 in range(4):
            b = bank()
            for sl in range(8):
                q = 8 * g8 + sl
                tr(psb[b][:, sl * 128:sl * 128 + NMC], UtmAf[:NMC, q, :], ident_b[:NMC, :NMC],
                   r=allU + ["const"], w=[PSR(b)])
            cp(evac_eng(), Uq[:, 8 * g8:8 * g8 + 8, :NMC], psb[b][:].rearrange("p (s n) -> p s n", s=8)[:, :, :NMC],
               r=[PSR(b)], w=[("Uq", 8 * g8 + i_) for i_ in range(8)])
        for q0 in range(0, 32, 4):
            for ri in range(2):
                b = bank()
                for ql in range(4):
                    q = q0 + ql
                    mm(ps[b][:, ql * 128:ql * 128 + NMC], WX[:, q, ri, :], Uq[:, q, :NMC], r=["WX", ("Uq", q)], w=[PSR(b)])
                pv = ps[b][:].rearrange("p (q n) -> p q n", q=4)
                col = ri * 32 + q0
                if small:
                    cp(evac_eng(), HS[:, 1:5, col:col + 4].rearrange("p n q -> p q n"), pv[:, :, 0:4], r=[PSR(b)], w=["HS"])
                    cp(evac_eng(), Xsm[:, :, col:col + 4].rearrange("p n q -> p q n"), pv[:, :, 4:20], r=[PSR(b)], w=["Xsm"])
                else:
                    cp(evac_eng(), HS[:, 1:1 + NMC, col:col + 4].rearrange("p n q -> p q n"), pv[:, :, :NMC], r=[PSR(b)], w=["HS"])
        if KMIX <= 4:
            return
        phase("s5_chain")
        for t in range(4):
            wg_, rg = ws_next("win", 20 + t)
            wgv = wg_[:, 0:2048].rearrange("p (k m) -> p k m", k=8)
            for sub in range(2):
                o = 2 * t + sub
                bg = proj_fm(wgv, sub, N, rg)
                actf(mrg[:, o, :N], ps[bg][:, :N], AF.Sigmoid, r=[PSR(bg)], w=[("mrg", o)])
        CE = CHAIN_ENG

        def cstep(dst, src, xin, pw, k_src, k_x, k_dst):
            tt(CE, ct1[:], src, APW[:, pw - 1, 0, :], ALU.mult, r=[k_src, "AA"], w=["ct1"])
            tt(CE, ct2[:], src, APW[:, pw - 1, 1, :], ALU.mult, r=[k_src, "AA"], w=["ct2"])
            tt(CE, dst, xin, ct1[:], ALU.add, r=[k_x, "ct1"], w=[k_dst])
            tt(CE, dst[:, 0:32], dst[:, 0:32], ct2[:, 32:64], ALU.subtract, r=[k_dst, "ct2"], w=[k_dst])
            tt(CE, dst[:, 32:64], dst[:, 32:64], ct2[:, 0:32], ALU.add, r=[k_dst, "ct2"], w=[k_dst])

        if small:
            cp(CE, HS[:, 0, :], Hc[:], r=["Hc", "HS"], w=[("HSs", 0)])
            for n in range(4):
                cstep(HS[:, n + 1, :], HS[:, n, :], HS[:, n + 1, :], 1, ("HSs", n), "HS", ("HSs", n + 1))
            cp(CE, Hc[:], HS[:, 4, :], r=[("HSs", 4)], w=["Hc"])
            cp(CE, HS[:, 0, 0:1], HS[:, 0, 0:1], r=[("HSs", n_) for n_ in range(5)], w=["HS"])
        else:
            NB, BL = 16, 8
            cvc = Carver(); cvc.off = chain_tmp_off
            Cs = Cs_buf; bt1 = cvc.f32([NB, 64]); bt2 = cvc.f32([NB, 64])

            def bulk(dst, src, pw, srcb=False):
                ar = APW[:, pw - 1, 0, :].unsqueeze(1).to_broadcast([128, NB, 64])
                ai = APW[:, pw - 1, 1, :].unsqueeze(1).to_broadcast([128, NB, 64])
                tt(CE, bt1[:], src, ar, ALU.mult, r=["HS", "Cs", "AA"], w=["bt1"])
                tt(CE, bt2[:], src, ai, ALU.mult, r=["HS", "Cs", "AA"], w=["bt2"])
                tt(CE, dst, dst, bt1[:], ALU.add, r=["HS", "bt1"], w=["HS"])
                tt(CE, dst[:, :, 0:32], dst[:, :, 0:32], bt2[:, :, 32:64], ALU.subtract, r=["HS", "bt2"], w=["HS"])
                tt(CE, dst[:, :, 32:64], dst[:, :, 32:64], bt2[:, :, 0:32], ALU.add, r=["HS", "bt2"], w=["HS"])

            V = HS[:, 1:1 + NMC, :].rearrange("p (m i) c -> p m i c", i=BL)
            W = HS[:, 0:NMC, :].rearrange("p (m i) c -> p m i c", i=BL)
            for i in range(1, BL):
                bulk(V[:, :, i, :], V[:, :, i - 1, :], 1)
            cp(CE, Cs[:, 0, :], Hc[:], r=["Hc"], w=[("Cs", 0)])
            for m in range(NB):
                cstep(Cs[:, m + 1, :], Cs[:, m, :], V[:, m, BL - 1, :], BL, ("Cs", m), "HS", ("Cs", m + 1))
            cp(CE, Cs[:, 0, 0:1], Cs[:, 0, 0:1], r=[("Cs", m_) for m_ in range(NB + 1)], w=["Cs"])
            for i in range(1, BL):
                bulk(W[:, :, i, :], Cs[:, 0:NB, :], i)
            cp(CE, W[:, :, 0, :], Cs[:, 0:NB, :], r=["Cs"], w=["HS"])
            cp(CE, Hc[:], Cs[:, NB, :], r=["Cs"], w=["Hc"])
            ws_prefetch()
            P.barrier()
        if small:
            dma("sp", H0t[:].rearrange("p b c -> p (b c)"), din["h0"].rearrange("p b c -> p (b c)"), "h0", w=["H0t"])
            AAb = AA[:].unsqueeze(1).to_broadcast([128, 16, 64])
            ABb = AB[:].unsqueeze(1).to_broadcast([128, 16, 64])
            tt("dve", st1[:], H0t[:], AAb, ALU.mult, r=["H0t", "AA"], w=["st1"])
            tt("dve", st2[:], H0t[:], ABb, ALU.mult, r=["H0t", "AA"], w=["st2"])
            tt("dve", Hso[:], Xsm[:], st1[:], ALU.add, r=["Xsm", "st1"], w=["Hso"])
            tt("dve", Hso[:, :, 0:32], Hso[:, :, 0:32], st2[:, :, 32:64], ALU.subtract, r=["Hso", "st2"], w=["Hso"])
            tt("dve", Hso[:, :, 32:64], Hso[:, :, 32:64], st2[:, :, 0:32], ALU.add, r=["Hso", "st2"], w=["Hso"])
            dma("sp", dout["s5so"].rearrange("p b c -> p (b c)"), Hso[:].rearrange("p b c -> p (b c)"), "s5so", r=["Hso"])
        if KMIX <= 5:
            return
        phase("s5_Y")
        def hb_cast(bq):
            hb = Hbf[bq % 2]
            for ri in range(2):
                col = ri * 32 + 4 * bq
                if small:
                    cp(evac_eng(), hb[:, ri, :, 0:4], HS[:, 0:4, col:col + 4].rearrange("p n q -> p q n"), r=["HS"], w=[("Hbf", bq % 2)])
                    cp(evac_eng(), hb[:, ri, :, 4:20], H0t[:, :, col:col + 4].rearrange("p n q -> p q n"), r=["H0t"], w=[("Hbf", bq % 2)])
                else:
                    cp(evac_eng(), hb[:, ri, :, :NMC], HS[:, 0:NMC, col:col + 4].rearrange("p n q -> p q n"), r=["HS"], w=[("Hbf", bq % 2)])

        hb_cast(0)
        for bq in range(8):
            hb = Hbf[bq % 2]
            if bq + 1 < 8:
                hb_cast(bq + 1)
            b = bank()
            qs = [4 * bq + ql for ql in range(4)]
            for ql in range(4):
                q = 4 * bq + ql
                o = ps[b][:, ql * 128:ql * 128 + NMC]
                mm(o, Toep[:, q, :], Uq[:, q, :NMC], True, False, r=["Toep", ("Uq", q)], w=[PSR(b)])
                mm(o, Win[:, q, 0, :], hb[:, 0, ql, :NMC], False, False, r=["Win", ("Hbf", bq % 2)], w=[PSR(b)])
                mm(o, Win[:, q, 1, :], hb[:, 1, ql, :NMC], False, True, r=["Win", ("Hbf", bq % 2)], w=[PSR(b)])
            yslot = bq % 2
            yvv = (yv if yslot == 0 else yt)
            tt("dve", yvv[:, :, :NMC], Uq[:, 4 * bq:4 * bq + 4, :NMC],
               Dcol[:, 4 * bq:4 * bq + 4].unsqueeze(2).to_broadcast([128, 4, NMC]), ALU.mult,
               r=[("Uq", q) for q in qs] + ["const"], w=[("yv", yslot)])
            tt("dve", yvv[:, :, :NMC], yvv[:, :, :NMC], ps[b][:].rearrange("p (q n) -> p q n", q=4)[:, :, :NMC], ALU.add,
               r=[("yv", yslot), PSR(b)], w=[("yv", yslot)])
            actf(Uq[:, 4 * bq:4 * bq + 4, :NMC], yvv[:, :, :NMC], AF.Gelu_apprx_tanh, r=[("yv", yslot)], w=[("Uq", q) for q in qs])
        for g8 in range(4):
            b = bank()
            for sl in range(8):
                q = 8 * g8 + sl
                tr(psb[b][:NMC, sl * 128:(sl + 1) * 128], Uq[:, q, :NMC], ident_b[:], r=[("Uq", q), "const"], w=[PSR(b)])
            for j in range(4):
                cp(evac_eng(), Utm[:NMC, j, 256 * g8:256 * g8 + 256].rearrange("n (q c) -> n q c", q=8),
                   psb[b][:NMC, :].rearrange("n (q j c) -> n q j c", q=8, j=4)[:, :, j, :], r=[PSR(b)], w=[("Utm", g8)])
        y5v = y5T[:, :, 0:N].rearrange("p c (n j) -> p c n j", j=4)
        for j in range(4):
            b = bank()
            for ch in range(8):
                tr(psb[b][:, ch * 128:ch * 128 + NMC], Utm[:NMC, j, ch * 128:(ch + 1) * 128], ident_b[:NMC, :NMC],
                   r=allU + ["const"], w=[PSR(b)])
            cp(evac_eng(), y5v[:, :, :, j], psb[b][:].rearrange("p (c n) -> p c n", c=8)[:, :, :NMC], r=[PSR(b)], w=["y5T"])
        if KMIX <= 7:
            return
        if si == 0:
            dbg_dump("mg", mg[:, :, 0:128], [128, 8, 128], [("mg", c) for c in range(8)])
            dbg_dump("y5T", y5T[:, :, 0:128], [128, 8, 128], ["y5T"])
        phase("glu")
        for t in range(4):
            wa_, ra = ws_next("wa", t)
            wb_, rb = ws_next("wb", t)
            wav = wa_[:, 0:2048].rearrange("p (k m) -> p k m", k=8)
            wbv = wb_[:, 0:2048].rearrange("p (k m) -> p k m", k=8)
            for sub in range(2):
                o = 2 * t + sub
                s_ = o % 2
                ba = proj_fm(wav, sub, N, ra, src=y5T, skey="y5T_")
                bb_ = proj_fm(wbv, sub, N, rb, src=y5T, skey="y5T_")
                actf(sbt[s_][:, :N], ps[bb_][:, :N], AF.Sigmoid, r=[PSR(bb_)], w=[("sbt", s_)])
                tt("dve", gtmp[s_][:, :N], ps[ba][:, :N], sbt[s_][:, :N], ALU.mult, r=[PSR(ba), ("sbt", s_)], w=[("gtmp", s_)])
                tt("dve", gtmp[s_][:, :N], gtmp[s_][:, :N], mrg[:, o, :N], ALU.mult, r=[("gtmp", s_), ("mrg", o)], w=[("gtmp", s_)])
                tt("dve", mrg[:, o, :N], gtmp[s_][:, :N], mg[:, o, :N], ALU.add, r=[("gtmp", s_), ("mg", o)], w=[("mrg", o)])
        if si == 0:
            dbg_dump("mrg", mrg[:, :, 0:128], [128, 8, 128], [("mrg", c) for c in range(8)])
        for t in range(4):
            wt, rk = ws_next("wo", t)
            wv = wt[:, 0:2048].rearrange("p (k m) -> p k m", k=8)
            for sub in range(2):
                o = 2 * t + sub
                b = proj_fm(wv, sub, N, rk, src=mrg, skey="mrg")
                tt("dve", h[:, o, :N], ps[b][:, :N], h[:, o, :N], ALU.add, r=[PSR(b), ("h", o)], w=[("h", o)])

    import os
    STOP = int(os.environ.get("KSTOP", "99"))
    for si, (c0, N) in enumerate(SEGS):
        if STOP <= 1 or (STOP < 10 and si >= 1):
            break
        dma("sp", h[:, :, :N], din["xT"][:, :, c0:c0 + N], "xin", w=[("h", c) for c in range(8)])
        if STOP >= 2:
            ffn(N, GI_FFN1, "w1", need_barrier=(si == 0))
        if STOP >= 3:
            mixer(si, c0, N)
        cv = Carver(); cv.off = 8192
        yout = cv.f32([8, 512])
        ffn(N, GI_FFN2, "w2", dst=yout, dst_key="yout")
        phase("final_norm")
        rmsnorm(N, GI_FINAL, yout, "yout", src=yout, src_key="yout")
        dma("sp", dout["yT"][:, :, c0:c0 + N], yout[:, :, :N], "yout", r=[("yout", c) for c in range(8)])
    dma("sp", dout["gp"].rearrange("h d e -> d h e"), Sx[:], "gp", r=[("Sx", h_) for h_ in range(4)])
    dma("sp", dout["s5po"], Hc[:], "s5po", r=["Hc"])
    phase("end")
    stats = P.emit()
    if _os.environ.get("KPHASE"):
        import json as _json
        _json.dump(PHASES, open(_os.environ["KPHASE"], "w"))
    return nc, stats


def _tile_cols(W, c0, ncols):
    K = W.shape[0]
    return np.ascontiguousarray(W[:, c0:c0 + ncols].reshape(K // 128, 128, ncols).transpose(1, 0, 2).reshape(128, -1))


def _gain(g):
    return g.reshape(8, 128).T


def _prep_shared(inp):
    f = lambda a: np.asarray(a, dtype=np.float32)
    sh = {}
    for pre, a, b_, c in (("w1", "ffn1_w_gate", "ffn1_w_up", "ffn1_w_down"), ("w2", "ffn2_w_gate", "ffn2_w_up", "ffn2_w_down")):
        Wg, Wu, Wd = f(inp[a])[0], f(inp[b_])[0], f(inp[c])[0]
        sh[pre + "g"] = np.stack([_tile_cols(Wg, 256 * j, 256) for j in range(11)])
        sh[pre + "u"] = np.stack([_tile_cols(Wu, 256 * j, 256) for j in range(11)])
        dt = []
        for o in range(8):
            for kh in range(2):
                blk = Wd[11 * kh * 128:(11 * kh + 11) * 128, 128 * o:128 * o + 128]
                dt.append(blk.reshape(11, 128, 128).transpose(1, 0, 2).reshape(128, 1408))
        sh[pre + "d"] = np.stack(dt)
    Win = f(inp["w_in"])[0]
    cols = []
    cols += [0, 256, 512, 768]
    cols += [1024 + 256 * i for i in range(4)]
    cols += [2048 + 256 * i for i in range(4)]
    cols += [4112 + 256 * i for i in range(4)]
    cols += [3088 + 256 * i for i in range(4)]
    cols += [5136 + 256 * i for i in range(4)]
    sh["win"] = np.stack([_tile_cols(Win, c, 256) for c in cols])
    sh["wglr"] = _tile_cols(Win, 3072, 16)
    for nm, key in (("wgo", "gla_w_out"), ("wa", "s5_w_glu_a"), ("wb", "s5_w_glu_b"), ("wo", "w_out")):
        W = f(inp[key])[0]
        sh[nm] = np.stack([_tile_cols(W, 256 * t, 256) for t in range(4)])
    sh["gains"] = np.ascontiguousarray(np.stack([_gain(f(inp["norm_ffn1"])[0]), _gain(f(inp["norm_mix"])[0]),
                                                 _gain(f(inp["norm_ffn2"])[0]), _gain(f(inp["norm_final"])),
                                                 _gain(f(inp["gla_norm"])[0])], axis=1))
    sh["wup"] = np.concatenate([f(inp["gla_w_gate_up"])[0], f(inp["gla_b_gate"])], axis=0)

    def lay_gp(a):
        return a.reshape(32, 2, 64).transpose(1, 2, 0).reshape(128, 32)
    are = lay_gp(f(inp["s5_a_re"])[0]); aim = lay_gp(f(inp["s5_a_im"])[0])
    ldt = lay_gp(np.repeat(f(inp["s5_log_dt"])[0][:, None], 64, axis=1))
    sh["s5p"] = np.ascontiguousarray(np.stack([are, aim, ldt], axis=1))

    def lay_b(B):
        Bq = B.reshape(32, 2, 64, 16)
        out = np.zeros((2, 64, 32, 2, 16), np.float32)
        for m in range(2):
            out[m, :, :, m, :] = Bq[:, m].transpose(1, 0, 2)
        return out.reshape(128, 1024)

    def lay_c(C):
        Cq = C.reshape(32, 2, 16, 64)
        out = np.zeros((2, 64, 32, 2, 16), np.float32)
        for m in range(2):
            out[m, :, :, m, :] = Cq[:, m].transpose(2, 0, 1)
        return out.reshape(128, 1024)
    sh["s5b"] = np.stack([lay_b(f(inp["s5_b_re"])[0]), lay_b(f(inp["s5_b_im"])[0])], axis=1)
    sh["s5c"] = np.stack([lay_c(f(inp["s5_c_re"])[0]), lay_c(f(inp["s5_c_im"])[0])], axis=1)
    d = f(inp["s5_d"])[0].reshape(32, 2, 16).transpose(1, 2, 0)
    sh["s5d"] = np.ascontiguousarray(np.broadcast_to(d[None], (4, 2, 16, 32)).reshape(128, 32))
    cm = np.zeros((128, 8, 128), np.float32)
    cm[:, 0, :] = np.eye(128)
    idx = np.arange(128)
    same = (idx[:, None] // 64) == (idx[None, :] // 64)
    cm[:, 1, :] = (same & (idx[:, None] <= idx[None, :])).astype(np.float32)
    cm[:, 2, :] = -cm[:, 1, :] / 16.0
    cm[:, 3, :] = -(same & (idx[:, None] > idx[None, :])).astype(np.float32) / 16.0
    cid = np.where(np.arange(80) < 16, 0, 1 + (np.arange(80) - 16) // 4)
    same0 = cid[:, None] == cid[None, :]
    i80 = np.arange(80)
    cm[:80, 4, :80] = (same0 & (i80[:, None] <= i80[None, :])).astype(np.float32)
    cm[:80, 5, :80] = -cm[:80, 4, :80] / 16.0
    cm[:80, 6, :80] = -(same0 & (i80[:, None] > i80[None, :])).astype(np.float32) / 16.0
    cm[:, 7, :] = ((idx[None, :] // 32) >= (idx[:, None] // 32)).astype(np.float32)
    sh["cmat"] = cm
    ci = np.zeros((128, 19), np.float32)
    ci[:, 0] = (idx < 64); ci[:, 1] = (idx >= 64)
    for s in range(80):
        ci[s, 2 + cid[s]] = 1.0
    sh["cind"] = ci
    return {k: np.ascontiguousarray(v, dtype=np.float32) for k, v in sh.items()}


def _prep_core(inp, c):
    f = lambda a: np.asarray(a, dtype=np.float32)
    toks = np.concatenate([f(inp["meta_tokens"]), f(inp["x_sample"])[16 * c:16 * c + 16].reshape(64, D), f(inp["x_prompt"])[c]], axis=0)
    xT = toks.T.reshape(8, 128, NCOL).transpose(1, 0, 2)
    sg = f(inp["state_gla"])[0, 16 * c:16 * c + 16]

    def lay_h(a):
        return a.reshape(16, 32, 2, 64).transpose(2, 3, 0, 1).reshape(128, 16, 32)
    h0 = np.concatenate([lay_h(f(inp["state_s5_re"])[0, 16 * c:16 * c + 16]), lay_h(f(inp["state_s5_im"])[0, 16 * c:16 * c + 16])], axis=2)
    return {"xT": np.ascontiguousarray(xT), "sg": np.ascontiguousarray(sg), "h0": np.ascontiguousarray(h0)}


_CACHE = {}


def kernel(**inputs):
    if "nc" not in _CACHE:
        _CACHE["nc"] = build_program()
    nc, stats = _CACHE["nc"]
    sh = _prep_shared(inputs)
    in_maps = []
    for c in range(NCORES):
        m = dict(sh)
        m.update(_prep_core(inputs, c))
        in_maps.append(m)
    res = run_bass_kernel_spmd(nc, in_maps, core_ids=list(range(NCORES)))
    R = res.results
    y_prompt = np.zeros((8, 2048, D), np.float32)
    y_sample = np.zeros((128, 4, D), np.float32)
    gla_p = np.zeros((1, 8, 4, 128, 256), np.float32)
    re_p = np.zeros((1, 8, 64, 64), np.float32); im_p = np.zeros((1, 8, 64, 64), np.float32)
    gla_s = np.zeros((1, 128, 4, 128, 256), np.float32)
    re_s = np.zeros((1, 128, 64, 64), np.float32); im_s = np.zeros((1, 128, 64, 64), np.float32)
    for c in range(NCORES):
        r = R[c]
        y = np.asarray(r["yT"]).transpose(1, 0, 2).reshape(D, NCOL).T
        y_sample[16 * c:16 * c + 16] = y[16:80].reshape(16, 4, D)
        y_prompt[c] = y[80:]
        gla_p[0, c] = np.asarray(r["gp"])
        gla_s[0, 16 * c:16 * c + 16] = np.asarray(r["gs"])
        hp = np.asarray(r["s5po"])
        un = lambda a: a.reshape(2, 64, 32).transpose(2, 0, 1).reshape(64, 64)
        re_p[0, c] = un(hp[:, 0:32]); im_p[0, c] = un(hp[:, 32:64])
        hs = np.asarray(r["s5so"])
        un2 = lambda a: a.reshape(2, 64, 16, 32).transpose(2, 3, 0, 1).reshape(16, 64, 64)
        re_s[0, 16 * c:16 * c + 16] = un2(hs[:, :, 0:32]); im_s[0, 16 * c:16 * c + 16] = un2(hs[:, :, 32:64])
    return (y_prompt, y_sample, gla_p, re_p, im_p, gla_s, re_s, im_s)
```

```python
import math
import numpy as np
import concourse.bass as bass
import concourse.mybir as mybir
from concourse.bass_utils import run_bass_kernel_spmd

F32 = mybir.dt.float32
BF16 = mybir.dt.bfloat16
I32 = mybir.dt.int32
AF = mybir.ActivationFunctionType
ALU = mybir.AluOpType

D = 1024
DFF = 2816
NCORES = 8
SEGS = [(0, 80), (80, 512), (592, 512), (1104, 512), (1616, 512)]
NCOL = 2128
GI_FFN1, GI_MIX, GI_FFN2, GI_FINAL, GI_GLA = 0, 1, 2, 3, 4
EPS = 1e-6
CHAIN_ENG = "dve"
NW5 = 30
NW1, NW2, NW3 = 16, 20, 16


class Prog:
    def __init__(self, nc):
        self.nc = nc
        self.ops = []
        self.lastw = {}
        self.readers = {}
        self.barrier_ops = None
        self.barrier_done = set()

    def op(self, eng, fn, reads=(), writes=(), dma=None):
        i = len(self.ops)
        deps = {}
        psr = [r for r in reads if isinstance(r, tuple) and r[0] == "ps"]
        if psr:
            reads = [r for r in reads if not (isinstance(r, tuple) and r[0] == "ps")]
            writes = list(writes) + psr
        for r in reads:
            w = self.lastw.get(r)
            if w is not None:
                deps[w] = "raw"
        for r in writes:
            w = self.lastw.get(r)
            if w is not None:
                deps[w] = "raw"
            for rd in self.readers.get(r, ()):
                deps.setdefault(rd, "war")
        if self.barrier_ops is not None and eng not in self.barrier_done:
            for b in self.barrier_ops:
                deps[b] = "raw"
            self.barrier_done.add(eng)
        self.ops.append(dict(eng=eng, fn=fn, deps=deps, dma=dma, marked=False, semname=None, semval=None))
        for r in reads:
            self.readers.setdefault(r, []).append(i)
        for r in writes:
            self.lastw[r] = i
            self.readers[r] = []
        return i

    def barrier(self):
        last = {}
        for i, o in enumerate(self.ops):
            key = o["eng"] if o["dma"] is None else ("dma", o["dma"])
            last[key] = i
        self.barrier_ops = list(last.values())
        self.barrier_done = set()

    @staticmethod
    def _skip(p, o, kind):
        if p["dma"] is None and o["dma"] is None and p["eng"] == o["eng"]:
            if o["eng"] == "pe":
                return True
        return False

    def emit(self):
        nc = self.nc
        engs = {"pe": nc.tensor, "act": nc.scalar, "dve": nc.vector, "pool": nc.gpsimd, "sp": nc.sync}
        ops = self.ops
        for o in ops:
            for d, kind in o["deps"].items():
                if not self._skip(ops[d], o, kind):
                    ops[d]["marked"] = True
        semnames = set()
        for o in ops:
            o["semname"] = ("d_" + o["dma"]) if o["dma"] is not None else ("e_" + o["eng"])
            semnames.add(o["semname"])
        sems = {s: nc.semaphore(s).__enter__() for s in sorted(semnames)}
        cnt = {s: 0 for s in semnames}
        waited = {}
        nwait = 0
        for o in ops:
            E = engs[o["eng"]]
            need = {}
            for d, kind in o["deps"].items():
                p = ops[d]
                if self._skip(p, o, kind):
                    continue
                need[p["semname"]] = max(need.get(p["semname"], 0), p["semval"])
            for s, v in need.items():
                if waited.get((o["eng"], s), 0) < v:
                    E.wait_ge(sems[s], v)
                    waited[(o["eng"], s)] = v
                    nwait += 1
            inst = o["fn"](E)
            s = o["semname"]
            if o["dma"] is not None:
                cnt[s] += 16
                inst.then_inc(sems[s], 16)
                o["semval"] = cnt[s]
            elif o["marked"]:
                cnt[s] += 1
                inst.then_inc(sems[s], 1)
                o["semval"] = cnt[s]
        for s in sorted(semnames):
            if cnt[s] > 0:
                nc.sync.wait_ge(sems[s], cnt[s])
        return dict(n_ops=len(ops), n_wait=nwait, sem_max=max(cnt.values()))


IN_SHAPES = {
    "xT": [128, 8, NCOL],
    "w1g": [11, 128, 2048], "w1u": [11, 128, 2048], "w1d": [16, 128, 1408],
    "w2g": [11, 128, 2048], "w2u": [11, 128, 2048], "w2d": [16, 128, 1408],
    "win": [24, 128, 2048], "wglr": [128, 128],
    "wgo": [4, 128, 2048], "wa": [4, 128, 2048], "wb": [4, 128, 2048], "wo": [4, 128, 2048],
    "gains": [128, 5, 8], "wup": [17, 512],
    "s5p": [128, 3, 32], "s5b": [128, 2, 1024], "s5c": [128, 2, 1024], "s5d": [128, 32],
    "cmat": [128, 8, 128],
    "cind": [128, 19],
    "sg": [16, 4, 128, 256], "h0": [128, 16, 64],
}
OUT_SHAPES = {
    "yT": [128, 8, NCOL], "gp": [4, 128, 256], "gs": [16, 4, 128, 256],
    "s5po": [128, 64], "s5so": [128, 16, 64],
}


def build_program():
    nc = bass.Bass("TRN2", target_bir_lowering=False)
    P = Prog(nc)
    din = {k: nc.dram_tensor(k, s, F32, kind="ExternalInput").ap() for k, s in IN_SHAPES.items()}
    dout = {k: nc.dram_tensor(k, s, F32, kind="ExternalOutput").ap() for k, s in OUT_SHAPES.items()}

    def sb(name, shape, dt=F32):
        return nc.sbuf_tensor("s_" + name, shape, dt).__enter__()

    PHASES = []
    npe = [0]

    def phase(name):
        PHASES.append((name, npe[0]))

    def mm(out, lhsT, rhs, start=True, stop=True, r=(), w=()):
        npe[0] += 1
        P.op("pe", lambda e: e.matmul(out, lhsT, rhs, start=start, stop=stop), r, w)

    def warm(n):
        for _ in range(n):
            P.op("pe", lambda e: e.matmul(ps[7][:, 0:128], ones_b[:], ones_b[:], start=True, stop=True), ["const"], [("ps", 7)])

    def tr(out, in_, ident, r=(), w=()):
        npe[0] += 1
        P.op("pe", lambda e: e.transpose(out, in_, ident), r, w)

    def actf(out, in_, func, r=(), w=(), bias=None, scale=None):
        kw = {}
        if bias is not None:
            kw["bias"] = bias
        if scale is not None:
            kw["scale"] = scale
        P.op("act", lambda e: e.activation(out=out, in_=in_, func=func, **kw), r, w)

    def tt(eng, out, in0, in1, op, r=(), w=()):
        P.op(eng, lambda e: e.tensor_tensor(out=out, in0=in0, in1=in1, op=op), r, w)

    def ts(eng, out, in0, s1, s2, op0, op1, r=(), w=()):
        P.op(eng, lambda e: e.tensor_scalar(out=out, in0=in0, scalar1=s1, scalar2=s2, op0=op0, op1=op1), r, w)

    def tsm(eng, out, in0, s1, r=(), w=()):
        P.op(eng, lambda e: e.tensor_scalar_mul(out=out, in0=in0, scalar1=s1), r, w)

    def stt(out, in0, scalar, in1, op0, op1, r=(), w=()):
        P.op("dve", lambda e: e.scalar_tensor_tensor(out=out, in0=in0, scalar=scalar, in1=in1, op0=op0, op1=op1), r, w)

    def cp(eng, out, in_, r=(), w=()):
        if eng == "act":
            P.op("act", lambda e: e.activation(out=out, in_=in_, func=AF.Copy), r, w)
        else:
            P.op(eng, lambda e: e.tensor_copy(out=out, in_=in_), r, w)

    def recip(out, in_, r=(), w=()):
        P.op("dve", lambda e: e.reciprocal(out=out, in_=in_), r, w)

    def memset(eng, ap, val, w=()):
        P.op(eng, lambda e: e.memset(ap, val), (), w)

    def dma(q, out, in_, key, r=(), w=()):
        P.op(q, lambda e: e.dma_start(out=out, in_=in_), r, w, dma=key)

    ps = [nc.psum_tensor("ps%d" % i, [128, 512], F32).__enter__() for i in range(8)]
    psb = [p[:].bitcast(BF16) for p in ps]
    bank_ctr = [0]
    nbanks = [7]

    def bank():
        b = bank_ctr[0] % nbanks[0]
        bank_ctr[0] += 1
        return b

    def PSR(b):
        return ("ps", b)

    evac_ctr = [0]

    def evac_eng():
        evac_ctr[0] += 1
        return "act" if evac_ctr[0] % 2 == 0 else "dve"

    cmat = sb("cmat", [128, 8, 128])
    cind = sb("cind", [128, 19])
    gains = sb("gains", [128, 5, 8])
    wup = sb("wup", [17, 512])
    ident_f = cmat[:, 0, :]
    MSX = (cmat[:, 1, :], cmat[:, 2, :], cmat[:, 3, :], cind[:, 0:2])
    MS0 = (cmat[:, 4, :], cmat[:, 5, :], cmat[:, 6, :], cind[:, 2:19])
    tmask = cmat[:, 7, :]
    ident_b = sb("ident_b", [128, 128], BF16)
    ones_b = sb("ones_b", [128, 128], BF16)
    ones_f = sb("ones_f", [128, 128])
    onec = sb("onec", [128, 1])
    negpi = sb("negpi", [128, 1])
    epsc = sb("epsc", [128, 1])
    Toep = sb("Toep", [128, 32, 128], BF16)
    Win = sb("Win", [128, 32, 2, 128], BF16)
    WX = sb("WX", [128, 32, 2, 128], BF16)
    APW = sb("APW", [128, 8, 2, 64])
    AA = APW[:, 0, 0, :]
    AB = APW[:, 0, 1, :]
    Dcol = sb("Dcol", [128, 32])
    Sx = sb("Sx", [128, 4, 256])
    Sbf = [sb("Sbf%d" % i, [128, 4, 256], BF16) for i in range(3)]
    Hc = sb("Hc", [128, 64])

    dma("sp", cmat[:], din["cmat"], "c0a", w=["const"])
    dma("sp", cind[:], din["cind"], "c0b", w=["const"])
    dma("sp", gains[:], din["gains"], "c0c", w=["const"])
    dma("sp", wup[:], din["wup"], "c0d", w=["const"])
    dma("sp", Dcol[:], din["s5d"], "c0e", w=["const"])
    memset("dve", ones_b[:], 1.0, w=["const"])
    memset("dve", ones_f[:], 1.0, w=["const"])
    memset("dve", onec[:], 1.0, w=["const"])
    memset("dve", negpi[:], -math.pi, w=["const"])
    memset("dve", epsc[:], EPS, w=["const"])
    memset("dve", Sx[:], 0.0, w=[("Sx", h_) for h_ in range(4)])
    memset("dve", Sbf[0][:], 0.0, w=[("Sbf", 0, h) for h in range(4)])
    memset("dve", Hc[:], 0.0, w=["Hc"])
    cp("dve", ident_b[:], ident_f, r=["const"], w=["const"])

    import os as _os
    KPRE = int(_os.environ.get("KPRE", "99"))

    def s5_precompute():
        temps = []
        if KPRE <= 0:
            return

        def tb(name, shape, dt=F32):
            g = nc.sbuf_tensor("t_" + name, shape, dt)
            t = g.__enter__()
            temps.append(g)
            return t

        prm = tb("prm", [128, 3, 32])
        Bt = tb("Bt", [128, 2, 32, 32])
        Ct = tb("Ct", [128, 2, 32, 32])
        dma("sp", prm[:], din["s5p"], "c1a", w=["prm"])
        dma("sp", Bt[:].rearrange("p a q c -> p a (q c)"), din["s5b"], "c1b", w=["Bt"])
        dma("sp", Ct[:].rearrange("p a q c -> p a (q c)"), din["s5c"], "c1c", w=["Ct"])
        are, aim, ldt = prm[:, 0, :], prm[:, 1, :], prm[:, 2, :]
        dtt = tb("dtt", [128, 32]); lr = tb("lr", [128, 32]); th = tb("th", [128, 32])
        actf(dtt[:], ldt, AF.Exp, r=["prm"], w=["dtt"])
        tt("dve", lr[:], are, dtt[:], ALU.mult, r=["prm", "dtt"], w=["lr"])
        tt("dve", th[:], aim, dtt[:], ALU.mult, r=["prm", "dtt"], w=["th"])
        KS = list(range(-3, 5))
        MAG = tb("MAG", [128, 8, 32]); ARG = tb("ARG", [128, 2, 8, 32]); SC = tb("SC", [128, 2, 8, 32])
        KI = tb("KI", [128, 512], I32); KF = tb("KF", [128, 512])
        OFF = math.pi + 32 * math.pi
        for i, k in enumerate(KS):
            actf(MAG[:, i, :], lr[:], AF.Exp, r=["lr"], w=["MAG"], scale=float(k))
            ts("dve", ARG[:, 0, i, :], th[:], float(k), OFF, ALU.mult, ALU.add, r=["th"], w=["ARG"])
        P.op("dve", lambda e: e.tensor_scalar_add(out=ARG[:, 1, :, :], in0=ARG[:, 0, :, :], scalar1=math.pi / 2), ["ARG"], ["ARG"])
        argf = ARG[:].rearrange("p a k q -> p (a k q)")
        scf = SC[:].rearrange("p a k q -> p (a k q)")
        TWO_PI = 2 * math.pi
        tsm("dve", KI[:], argf, 1.0 / TWO_PI, r=["ARG"], w=["KI"])
        cp("dve", KF[:], KI[:], r=["KI"], w=["KF"])
        stt(argf, KF[:], -TWO_PI, argf, ALU.mult, ALU.add, r=["KF", "ARG"], w=["ARG"])
        ts("dve", KF[:], argf, 0.0, TWO_PI, ALU.is_lt, ALU.mult, r=["ARG"], w=["KF"])
        tt("dve", argf, argf, KF[:], ALU.add, r=["ARG", "KF"], w=["ARG"])
        ts("dve", KF[:], argf, TWO_PI, -TWO_PI, ALU.is_ge, ALU.mult, r=["ARG"], w=["KF"])
        tt("dve", argf, argf, KF[:], ALU.add, r=["ARG", "KF"], w=["ARG"])
        actf(scf, argf, AF.Sin, r=["ARG", "const"], w=["SC"], bias=negpi[:, 0:1], scale=1.0)
        PRE = tb("PRE", [128, 8, 32]); PIM = tb("PIM", [128, 8, 32])
        tt("dve", PRE[:], MAG[:], SC[:, 1, :, :], ALU.mult, r=["MAG", "SC"], w=["PRE"])
        tt("dve", PIM[:], MAG[:], SC[:, 0, :, :], ALU.mult, r=["MAG", "SC"], w=["PIM"])

        if KPRE <= 1:
            P.barrier()
            for g in reversed(temps):
                g.__exit__(None, None, None)
            return
        pw1 = tb("pw1", [128, 32]); pw2 = tb("pw2", [128, 32])
        cp("dve", APW[:, 0, 0, 0:32], PRE[:, 7, :], r=["PRE"], w=["AA"])
        cp("dve", APW[:, 0, 1, 0:32], PIM[:, 7, :], r=["PIM"], w=["AA"])
        for i in range(1, 8):
            cr, ci = APW[:, i - 1, 0, 0:32], APW[:, i - 1, 1, 0:32]
            tt("dve", pw1[:], cr, PRE[:, 7, :], ALU.mult, r=["AA", "PRE"], w=["pw1"])
            tt("dve", pw2[:], ci, PIM[:, 7, :], ALU.mult, r=["AA", "PIM"], w=["pw2"])
            tt("dve", APW[:, i, 0, 0:32], pw1[:], pw2[:], ALU.subtract, r=["pw1", "pw2"], w=["AA"])
            tt("dve", pw1[:], cr, PIM[:, 7, :], ALU.mult, r=["AA", "PIM"], w=["pw1"])
            tt("dve", pw2[:], ci, PRE[:, 7, :], ALU.mult, r=["AA", "PRE"], w=["pw2"])
            tt("dve", APW[:, i, 1, 0:32], pw1[:], pw2[:], ALU.add, r=["pw1", "pw2"], w=["AA"])
        cp("dve", APW[:, :, :, 32:64], APW[:, :, :, 0:32], r=["AA"], w=["AA"])
        nre = tb("nre", [128, 32]); den = tb("den", [128, 32]); t0 = tb("t0", [128, 32]); t1 = tb("t1s", [128, 32])
        cre = tb("cre", [128, 32]); cim = tb("cim", [128, 32])
        P.op("dve", lambda e: e.tensor_scalar_add(out=nre[:], in0=PRE[:, 4, :], scalar1=-1.0), ["PRE"], ["nre"])
        nim = PIM[:, 4, :]
        tt("dve", den[:], are, are, ALU.mult, r=["prm"], w=["den"])
        tt("dve", t0[:], aim, aim, ALU.mult, r=["prm"], w=["t0"])
        tt("dve", den[:], den[:], t0[:], ALU.add, r=["den", "t0"], w=["den"])
        recip(den[:], den[:], r=["den"], w=["den"])
        tt("dve", t0[:], nre[:], are, ALU.mult, r=["nre", "prm"], w=["t0"])
        tt("dve", t1[:], nim, aim, ALU.mult, r=["PIM", "prm"], w=["t1"])
        tt("dve", t0[:], t0[:], t1[:], ALU.add, r=["t0", "t1"], w=["t0"])
        tt("dve", cre[:], t0[:], den[:], ALU.mult, r=["t0", "den"], w=["cre"])
        tt("dve", t0[:], nim, are, ALU.mult, r=["PIM", "prm"], w=["t0"])
        tt("dve", t1[:], nre[:], aim, ALU.mult, r=["nre", "prm"], w=["t1"])
        tt("dve", t0[:], t0[:], t1[:], ALU.subtract, r=["t0", "t1"], w=["t0"])
        tt("dve", cim[:], t0[:], den[:], ALU.mult, r=["t0", "den"], w=["cim"])

        u1 = tb("u1", [128, 32, 32]); u2 = tb("u2", [128, 32, 32])

        def bc(x):
            return x.unsqueeze(2).to_broadcast([128, 32, 32])

        def cmul(ore, oim, xr, xi, yr, yi, rr, ww, neg_im=False):
            tt("dve", u1[:], yr, bc(xr), ALU.mult, r=rr, w=["u1"])
            tt("dve", u2[:], yi, bc(xi), ALU.mult, r=rr, w=["u2"])
            tt("dve", ore, u1[:], u2[:], ALU.subtract, r=["u1", "u2"], w=ww)
            tt("dve", u1[:], yi, bc(xr), ALU.mult, r=rr, w=["u1"])
            tt("dve", u2[:], yr, bc(xi), ALU.mult, r=rr, w=["u2"])
            if neg_im:
                stt(oim, u1[:], -1.0, u2[:], ALU.mult, ALU.subtract, r=["u1", "u2"], w=ww)
            else:
                tt("dve", oim, u1[:], u2[:], ALU.add, r=["u1", "u2"], w=ww)

        BB = tb("BB", [128, 2, 32, 32])
        cmul(BB[:, 0], BB[:, 1], cre[:], cim[:], Bt[:, 0], Bt[:, 1], ["cre", "cim", "Bt"], ["BB"])

        if KPRE <= 2:
            P.barrier()
            for g in reversed(temps):
                g.__exit__(None, None, None)
            return
        BP = tb("BP", [128, 2, 32, 4, 32])
        for j in range(4):
            idx = (3 - j) + 3
            cmul(BP[:, 0, :, j, :], BP[:, 1, :, j, :], PRE[:, idx, :], PIM[:, idx, :], BB[:, 0], BB[:, 1],
                 ["PRE", "PIM", "BB"], [("BP", j)])
        BPf = BP[:].rearrange("p a q j c -> p a q (j c)")
        for q0 in range(0, 32, 2):
            b = bank()
            for ql in range(2):
                for ri in range(2):
                    sl = ql * 2 + ri
                    tr(ps[b][:, sl * 128:(sl + 1) * 128], BPf[:, ri, q0 + ql, :], ident_f,
                       r=[("BP", j) for j in range(4)] + ["const"], w=[PSR(b)])
            cp(evac_eng(), WX[:, q0:q0 + 2, :, :], ps[b][:].rearrange("p (q r m) -> p q r m", q=2, r=2), r=[PSR(b)], w=["WX"])

        if KPRE <= 3:
            P.barrier()
            for g in reversed(temps):
                g.__exit__(None, None, None)
            return
        LL = BP
        for j in range(4):
            idx = 3 - j
            cmul(LL[:, 0, :, j, :], LL[:, 1, :, j, :], PRE[:, idx, :], PIM[:, idx, :], BB[:, 0], BB[:, 1],
                 ["PRE", "PIM", "BB"], [("LL", j)] + [("BP", j_) for j_ in range(4)])
        CP = tb("CP", [128, 2, 32, 5, 32])
        for k in range(5):
            idx = k + 3
            cmul(CP[:, 0, :, k, :], CP[:, 1, :, k, :], PRE[:, idx, :], PIM[:, idx, :], Ct[:, 0], Ct[:, 1],
                 ["PRE", "PIM", "Ct"], [("CP", k)], neg_im=True)
        LLf = LL[:].rearrange("p a q j c -> p a q (j c)")
        CPf = CP[:].rearrange("p a q k c -> p a q (k c)")
        allL = [("LL", j) for j in range(4)]
        allC = [("CP", k) for k in range(5)]

        if KPRE <= 4:
            P.barrier()
            for g in reversed(temps):
                g.__exit__(None, None, None)
            return
        for q0 in range(0, 32, 4):
            b = bank()
            for ql in range(4):
                q = q0 + ql
                o = ps[b][:, ql * 128:(ql + 1) * 128]
                mm(o, LLf[:, 0, q, :], CPf[:, 0, q, 0:128], True, False, r=allL + allC, w=[PSR(b)])
                mm(o, LLf[:, 1, q, :], CPf[:, 1, q, 0:128], False, True, r=allL + allC, w=[PSR(b)])
            tt("dve", Toep[:, q0:q0 + 4, :], ps[b][:].rearrange("p (q m) -> p q m", q=4),
               tmask.unsqueeze(1).to_broadcast([128, 4, 128]), ALU.mult, r=[PSR(b), "const"], w=["Toep"])

        if KPRE <= 5:
            P.barrier()
            for g in reversed(temps):
                g.__exit__(None, None, None)
            return
        for ri in range(2):
            cp("dve" if ri == 0 else "act", Win[:, :, ri, :], CPf[:, ri, :, 32:160], r=allC, w=["Win"])
        P.barrier()
        for g in reversed(temps):
            g.__exit__(None, None, None)

    s5_precompute()

    NSLOT = 5
    wbf = [sb("wbf%d" % i, [128, 2048], BF16) for i in range(NSLOT)]
    h = sb("h", [128, 8, 512])
    u = sb("u", [128, 8, 512], BF16)
    rstd = sb("rstd", [128, 512]); rtmp = sb("rtmp", [128, 512])
    mg = sb("mg", [128, 8, 512], BF16)
    y5T = sb("y5T", [128, 8, 512], BF16)
    ARENA_WORDS = 20480
    arena = sb("arena", [128, ARENA_WORDS])

    class Carver:
        def __init__(self):
            self.off = 0

        def f32(self, shape):
            n = int(np.prod(shape))
            a = arena[:, self.off:self.off + n]
            self.off += n
            assert self.off <= ARENA_WORDS, self.off
            return self._shape(a, shape)

        def bf(self, shape):
            n = int(np.prod(shape))
            w = (n + 1) // 2
            a = arena[:, self.off:self.off + w].bitcast(BF16)[:, 0:n]
            self.off += w
            assert self.off <= ARENA_WORDS, self.off
            return self._shape(a, shape)

        @staticmethod
        def _shape(a, shape):
            if len(shape) == 1:
                return a
            if len(shape) == 2:
                return a.rearrange("p (a b) -> p a b", a=shape[0])
            if len(shape) == 3:
                return a.rearrange("p (a b c) -> p a b c", a=shape[0], b=shape[1])
            raise ValueError

    seq = []
    for (c0, N) in SEGS:
        for pre in ("w1",):
            for j in range(11):
                seq.append((pre + "g", j, 2048)); seq.append((pre + "u", j, 2048))
            for t in range(16):
                seq.append((pre + "d", t, 1408))
        for t in range(12):
            seq.append(("win", t, 2048))
        seq.append(("wglr", None, 128))
        for t in range(12, 16):
            seq.append(("win", t, 2048))
        for t in range(4):
            seq.append(("wgo", t, 2048))
        for t in range(16, 20):
            seq.append(("win", t, 2048))
        for t in range(4):
            seq.append(("win", 20 + t, 2048))
        for t in range(4):
            seq.append(("wa", t, 2048)); seq.append(("wb", t, 2048))
        for t in range(4):
            seq.append(("wo", t, 2048))
        for pre in ("w2",):
            for j in range(11):
                seq.append((pre + "g", j, 2048)); seq.append((pre + "u", j, 2048))
            for t in range(16):
                seq.append((pre + "d", t, 1408))
    ws_state = dict(issued=0, k=0)
    PF = 2
    pfcur = [PF]

    def ws_issue(k):
        name, t, E = seq[k]
        src = din[name] if t is None else din[name][t]
        slot = k % NSLOT
        dma("pool", wbf[slot][:, 0:E], src, "w%d" % slot, w=[("w", slot)])

    def ws_next(name, t):
        k = ws_state["k"]
        assert seq[k][0] == name and seq[k][1] == t, (seq[k], name, t)
        while ws_state["issued"] < min(len(seq), k + 1 + pfcur[0]):
            ws_issue(ws_state["issued"])
            ws_state["issued"] += 1
        ws_state["k"] += 1
        return wbf[k % NSLOT], ("w", k % NSLOT)

    def rmsnorm(N, gi, dst, dst_key, src=None, src_key="h"):
        src = h if src is None else src
        b = bank()
        for c in range(8):
            if c % 2 == 0:
                actf(u[:, c, :N], src[:, c, :N], AF.Square, r=[(src_key, c)], w=[("u", c)])
            else:
                tt("dve", u[:, c, :N], src[:, c, :N], src[:, c, :N], ALU.mult, r=[(src_key, c)], w=[("u", c)])
        for c in range(8):
            mm(ps[b][:, :N], ones_b[:], u[:, c, :N], c == 0, c == 7, r=[("u", c), "const"], w=[PSR(b)])
        actf(rtmp[:, :N], ps[b][:, :N], AF.Ln, r=[PSR(b)], w=["rtmp"], bias=epsc[:, 0:1], scale=1.0 / D)
        actf(rstd[:, :N], rtmp[:, :N], AF.Exp, r=["rtmp"], w=["rstd"], scale=-0.5)
        for c in range(8):
            stt(dst[:, c, :N], src[:, c, :N], gains[:, gi, c:c + 1], rstd[:, :N], ALU.mult, ALU.mult,
                r=[(src_key, c), "rstd", "const"], w=[(dst_key, c)])

    def ffn(N, gi, pre, need_barrier=True, dst=None, dst_key="h"):
        if need_barrier:
            P.barrier()
        cv = Carver()
        act = cv.bf([22, 512])
        sgt = [cv.f32([512]) for _ in range(2)]
        phase("ffn_norm")
        rmsnorm(N, gi, u, "u")
        if N == 512:
            warm(NW5)
        phase("ffn_gu")
        pfcur[0] = 3
        for j in range(11):
            wg, rg = ws_next(pre + "g", j)
            wu, ru = ws_next(pre + "u", j)
            wgv = wg[:, 0:2048].rearrange("p (k m) -> p k m", k=8)
            wuv = wu[:, 0:2048].rearrange("p (k m) -> p k m", k=8)
            for half in range(2):
                c = 2 * j + half
                bg = bank()
                for kt in range(8):
                    mm(ps[bg][:, :N], wgv[:, kt, half * 128:(half + 1) * 128], u[:, kt, :N], kt == 0, kt == 7,
                       r=[rg, ("u", kt)], w=[PSR(bg)])
                bu = bank()
                for kt in range(8):
                    mm(ps[bu][:, :N], wuv[:, kt, half * 128:(half + 1) * 128], u[:, kt, :N], kt == 0, kt == 7,
                       r=[ru, ("u", kt)], w=[PSR(bu)])
                s = c % 2
                actf(sgt[s][:, :N], ps[bg][:, :N], AF.Silu, r=[PSR(bg)], w=[("sgt", s)])
                tt("dve", act[:, c, :N], sgt[s][:, :N], ps[bu][:, :N], ALU.mult, r=[("sgt", s), PSR(bu)], w=[("act", c)])
        phase("ffn_down")
        for o in range(8):
            b = bank()
            for kh in range(2):
                wd, rd = ws_next(pre + "d", 2 * o + kh)
                wdv = wd[:, 0:1408].rearrange("p (k m) -> p k m", k=11)
                for k in range(11):
                    ct = 11 * kh + k
                    mm(ps[b][:, :N], wdv[:, k, :], act[:, ct, :N], ct == 0, ct == 21, r=[rd, ("act", ct)], w=[PSR(b)])
            dstb = h if dst is None else dst
            stt(dstb[:, o, :N], ps[b][:, :N], 0.5, h[:, o, :N], ALU.mult, ALU.add, r=[PSR(b), ("h", o)], w=[(dst_key, o)])
        pfcur[0] = PF

    def proj_fm(wv, sub, N, rkey, src=None, skey="u"):
        src = u if src is None else src
        b = bank()
        for kt in range(8):
            sk = "y5T" if skey == "y5T_" else (skey, kt)
            mm(ps[b][:, :N], wv[:, kt, sub * 128:(sub + 1) * 128], src[:, kt, :N], kt == 0, kt == 7,
               r=[rkey, sk], w=[PSR(b)])
        return b

    xch = [0]

    KDUMP = int(_os.environ.get("KDUMP", "0"))

    def dbg_dump(name, ap, shape, rkeys):
        if not KDUMP:
            return
        d = nc.dram_tensor("dbg_" + name, shape, F32, kind="ExternalOutput").ap()
        dma("pool", d, ap, "dbg", r=rkeys)

    KMIX = int(_os.environ.get("KMIX", "99"))
    KGLA = int(_os.environ.get("KGLA", "99"))

    def mixer(si, c0seg, N):
        small = (N == 80)
        NMC = N // 4
        P.barrier()
        cv = Carver()
        NS = 128 if small else 512
        ohat = cv.bf([8, NS])
        siga = cv.bf([8, NS])
        qT32 = cv.f32([4, NS]); kT32 = cv.f32([4, NS])
        ktm = cv.f32([NS // 128, 512]); vtm = cv.bf([NS // 128, 1024]); silur = cv.bf([8, NS])
        gtm = cv.f32([512]); eb = cv.f32([4, 128]); enb = cv.f32([4, 128])
        erev = gtm
        NKDM = 4
        kd = cv.bf([512]); kdm = [cv.bf([128]) for _ in range(NKDM)]
        kctr = [0]
        qe = cv.bf([4, 128]); ke = cv.bf([4, 128]); scT = cv.bf([4, 128])
        o32 = cv.f32([8, 128]); osq = cv.bf([8, 128]); rs = cv.f32([4, 128])
        otmp2 = [cv.f32([128]) for _ in range(2)]
        glr = cv.f32([NS])[0:17, :]
        memset("dve", glr[:, :], 1.0, w=["glr"])
        if small:
            NSL = 8
            Sld = [cv.f32([256]) for _ in range(NSL)]
            Sout = [cv.f32([256]) for _ in range(NSL)]
            Sbs = cv.bf([16, 256])

        tiles = [(0, 0, 80)] if small else [(ti, 128 * ti, 128) for ti in range(4)]
        phase("mix_norm")
        rmsnorm(N, GI_MIX, u, "u")
        if not small:
            warm(NW5)
        phase("mix_proj")
        for t in range(2):
            wt, rk = ws_next("win", t)
            wv = wt[:, 0:2048].rearrange("p (k m) -> p k m", k=8)
            for sub in range(2):
                hh = 2 * t + sub
                b = proj_fm(wv, sub, N, rk)
                cp(evac_eng(), qT32[:, hh, :N], ps[b][:, :N], r=[PSR(b)], w=[("qT", hh)])
        for t in range(2):
            wt, rk = ws_next("win", 2 + t)
            wv = wt[:, 0:2048].rearrange("p (k m) -> p k m", k=8)
            for sub in range(2):
                hh = 2 * t + sub
                b = proj_fm(wv, sub, N, rk)
                cp(evac_eng(), kT32[:, hh, :N], ps[b][:, :N], r=[PSR(b)], w=[("kT", hh)])
            for (ti, tc0, R) in tiles:
                b = bank()
                for kt in range(8):
                    mm(ps[b][:R, 0:256], u[:, kt, tc0:tc0 + R], wv[:, kt, :], kt == 0, kt == 7, r=[rk, ("u", kt)], w=[PSR(b)])
                cp(evac_eng(), ktm[:R, ti, 256 * t:256 * t + 256], ps[b][:R, 0:256], r=[PSR(b)], w=[("ktm", ti)])
        for t in range(4):
            wt, rk = ws_next("win", 4 + t)
            wv = wt[:, 0:2048].rearrange("p (k m) -> p k m", k=8)
            for (ti, tc0, R) in tiles:
                b = bank()
                for kt in range(8):
                    mm(ps[b][:R, 0:256], u[:, kt, tc0:tc0 + R], wv[:, kt, :], kt == 0, kt == 7, r=[rk, ("u", kt)], w=[PSR(b)])
                cp(evac_eng(), vtm[:R, ti, 256 * t:256 * t + 256], ps[b][:R, 0:256], r=[PSR(b)], w=[("vtm", ti)])
        for t in range(4):
            wt, rk = ws_next("win", 8 + t)
            wv = wt[:, 0:2048].rearrange("p (k m) -> p k m", k=8)
            for sub in range(2):
                c8 = 2 * t + sub
                b = proj_fm(wv, sub, N, rk)
                actf(silur[:, c8, :N], ps[b][:, :N], AF.Silu, r=[PSR(b)], w=[("silur", c8)])
        wt, rk = ws_next("wglr", None)
        wv = wt[:, 0:128].rearrange("p (k m) -> p k m", k=8)
        b = bank()
        for kt in range(8):
            mm(ps[b][:16, :N], wv[:, kt, :], u[:, kt, :N], kt == 0, kt == 7, r=[rk, ("u", kt)], w=[PSR(b)])
        cp("dve", glr[0:16, :N], ps[b][:16, :N], r=[PSR(b)], w=["glr"])

        if KMIX <= 1:
            return
        phase("gla_core")
        def emit_gate_a(t):
            wt, rk = ws_next("win", 12 + t)
            wv = wt[:, 0:2048].rearrange("p (k m) -> p k m", k=8)
            for sub in range(2):
                o = 2 * t + sub
                b = proj_fm(wv, sub, N, rk)
                actf(siga[:, o, :N], ps[b][:, :N], AF.Sigmoid, r=[PSR(b)], w=[("siga", o)])

        nbanks[0] = 5
        for (ti, tc0, R) in tiles:
            mask, tri, trirev, ind = MS0 if small else MSX
            if small:
                chunks = [(0, 0, 16, ("x", None))] + [(1 + bb, 16 + 4 * bb, 20 + 4 * bb, ("s", bb)) for bb in range(16)]
            else:
                chunks = [(0, 0, 64, ("x", None)), (1, 64, 128, ("x", None))]
            b1 = bank()
            mm(ps[b1][:R, :], glr[0:17, tc0:tc0 + R], wup[0:17, :], r=["glr", "const"], w=[PSR(b1)])
            if not small:
                warm(NW1)
            actf(gtm[:R, :], ps[b1][:R, :], AF.Exp, r=[PSR(b1)], w=["gtm"], scale=-1.0)
            actf(gtm[:R, :], gtm[:R, :], AF.Ln, r=["gtm", "const"], w=["gtm"], bias=onec[:R, 0:1], scale=1.0)
            if KGLA <= 1:
                continue
            b2 = bank()
            for hh in range(4):
                mm(ps[b2][:, hh * 128:hh * 128 + R], gtm[:R, hh * 128:(hh + 1) * 128], tri[:R, :R], r=["gtm", "const"], w=[PSR(b2)])
            psv = ps[b2][:].rearrange("p (h c) -> p h c", h=4)[:, :, :R]
            actf(eb[:, :, :R], psv, AF.Exp, r=[PSR(b2)], w=["eb"])
            actf(enb[:, :, :R], psv, AF.Exp, r=[PSR(b2)], w=["enb"], scale=-1.0)
            if KGLA <= 2:
                continue
            b3 = bank()
            mm(ps[b3][:R, :], trirev[:R, :R], gtm[:R, :], r=["gtm", "const"], w=[PSR(b3)])
            if not small:
                warm(NW2)
            actf(erev[:R, :], ps[b3][:R, :], AF.Exp, r=[PSR(b3)], w=["gtm"])
            tt("dve", kd[:R, :], ktm[:R, ti, :], erev[:R, :], ALU.mult, r=[("ktm", ti), "gtm"], w=["kd"])
            if KGLA <= 3:
                continue
            for hh in range(4):
                stt(qe[:, hh, :R], qT32[:, hh, tc0:tc0 + R], 128.0 ** -0.5, eb[:, hh, :R], ALU.mult, ALU.mult,
                    r=[("qT", hh), "eb"], w=[("qe", hh)])
                tt("dve", ke[:, hh, :R], kT32[:, hh, tc0:tc0 + R], enb[:, hh, :R], ALU.mult, r=[("kT", hh), "enb"], w=[("ke", hh)])
            b4 = bank()
            for hh in range(4):
                mm(ps[b4][:R, hh * 128:hh * 128 + R], ke[:, hh, :R], qe[:, hh, :R], r=[("ke", hh), ("qe", hh)], w=[PSR(b4)])
            for hh in range(4):
                tt("dve", scT[:R, hh, :R], ps[b4][:R, hh * 128:hh * 128 + R], mask[:R, :R], ALU.mult,
                   r=[PSR(b4), "const"], w=[("scT", hh)])
            if KGLA <= 4:
                continue
            if not small:
                emit_gate_a(ti)
            bo = [5, 6]
            x0 = xch[0]
            for hh in range(4):
                xc = x0
                ent = []
                kvb = []
                LAG = 3

                def upd(ci):
                    nonlocal xc
                    (cidx, lo, hi, kind) = chunks[ci]
                    bk, half = kvb[ci]
                    dec = eb[:, hh, hi - 1:hi]
                    if kind[0] == "x":
                        ent.append((Sbf[xc % 3][:, hh, :], ("Sbf", xc % 3, hh)))
                        stt(Sx[:, hh, :], Sx[:, hh, :], dec, ps[bk][:, half:half + 256], ALU.mult, ALU.add,
                            r=[("Sx", hh), "eb", PSR(bk)], w=[("Sx", hh)])
                        xc += 1
                        cp("act", Sbf[xc % 3][:, hh, :], Sx[:, hh, :], r=[("Sx", hh)], w=[("Sbf", xc % 3, hh)])
                    else:
                        bb = kind[1]
                        sl = bb % NSL
                        ent.append((Sbs[:, bb, :], ("Sbs", bb)))
                        stt(Sout[sl][:, :], Sld[sl][:, :], dec, ps[bk][:, half:half + 256], ALU.mult, ALU.add,
                            r=[("Sld", sl), "eb", PSR(bk)], w=[("Sout", sl)])
                        dma("pool", dout["gs"][bb, hh], Sout[sl][:, :], "sst%d" % sl, r=[("Sout", sl)])

                for ci, (cidx, lo, hi, kind) in enumerate(chunks):
                    kslot = (kctr[0]) % NKDM
                    kctr[0] += 1
                    tsm("dve", kdm[kslot][:R, :], kd[:R, hh * 128:(hh + 1) * 128], ind[:R, cidx:cidx + 1],
                        r=["kd", "const"], w=[("kdm", kslot)])
                    if len(kvb) % 2 == 0:
                        bk = bank()
                    half = (len(kvb) % 2) * 256
                    mm(ps[bk][:, half:half + 256], kdm[kslot][:R, :], vtm[:R, ti, hh * 256:(hh + 1) * 256],
                       r=[("kdm", kslot), ("vtm", ti)], w=[PSR(bk)])
                    kvb.append((bk, half))
                    if kind[0] == "s":
                        bb = kind[1]
                        sl = bb % NSL
                        dma("sp", Sld[sl][:, :], din["sg"][bb, hh], "sld%d" % sl, w=[("Sld", sl)])
                        cp("act", Sbs[:, bb, :], Sld[sl][:, :], r=[("Sld", sl)], w=[("Sbs", bb)])
                    if ci >= LAG:
                        upd(ci - LAG)
                for ci in range(max(0, len(chunks) - LAG), len(chunks)):
                    upd(ci)
                if hh == 3:
                    xch[0] = xc
                if KGLA <= 5:
                    continue
                pso = ps[bo[hh // 2]]
                for e2 in range(2):
                    base = (hh % 2) * 256 + e2 * 128
                    mm(pso[:, base:base + R], vtm[:R, ti, hh * 256 + e2 * 128:hh * 256 + (e2 + 1) * 128], scT[:R, hh, :R],
                       True, False, r=[("vtm", ti), ("scT", hh)], w=[PSR(bo[hh // 2])])
                    for ci, (cidx, lo, hi, kind) in enumerate(chunks):
                        Sap, Skey = ent[ci]
                        mm(pso[:, base + lo:base + hi], Sap[:, e2 * 128:(e2 + 1) * 128], qe[:, hh, lo:hi],
                           False, ci == len(chunks) - 1, r=[Skey, ("qe", hh)], w=[PSR(bo[hh // 2])])
            if KGLA <= 6:
                continue
            if not small:
                warm(NW3)
            for k2 in range(2):
                pv = ps[bo[k2]][:].rearrange("p (c n) -> p c n", c=4)[:, :, :R]
                cp("dve", o32[:, 4 * k2:4 * k2 + 4, :R], pv, r=[PSR(bo[k2])], w=[("o32", k2)])
                actf(osq[:, 4 * k2:4 * k2 + 4, :R], pv, AF.Square, r=[PSR(bo[k2])], w=[("osq", k2)])
            if KGLA <= 7:
                continue
            bS = bank()
            for hh in range(4):
                for e2 in range(2):
                    mm(ps[bS][:, hh * 128:hh * 128 + R], ones_b[:], osq[:, 2 * hh + e2, :R], e2 == 0, e2 == 1,
                       r=[("osq", hh // 2), "const"], w=[PSR(bS)])
            if KGLA <= 8:
                continue
            psS = ps[bS][:].rearrange("p (h c) -> p h c", h=4)[:, :, :R]
            actf(rs[:, :, :R], psS, AF.Ln, r=[PSR(bS)], w=["rs"], bias=epsc[:, 0:1], scale=1.0 / 256)
            actf(rs[:, :, :R], rs[:, :, :R], AF.Exp, r=["rs"], w=["rs"], scale=-0.5)
            if KGLA <= 9:
                continue
            if si == 0 and KDUMP:
                dbg_dump("o32", o32[:], [128, 8, 128], [("o32", 0), ("o32", 1)])
                dbg_dump("rs", rs[:], [128, 4, 128], ["rs"])
                dbg_dump("silur", silur[:, :, 0:128], [128, 8, 128], [("silur", c) for c in range(8)])
                dbg_dump("scT", scT[:], [128, 4, 128], [("scT", c) for c in range(4)])
                dbg_dump("qe", qe[:], [128, 4, 128], [("qe", c) for c in range(4)])
                dbg_dump("ke", ke[:], [128, 4, 128], [("ke", c) for c in range(4)])
            for c8 in range(8):
                stt(otmp2[c8 % 2][:, :R], o32[:, c8, :R], gains[:, GI_GLA, c8:c8 + 1], rs[:, c8 // 2, :R], ALU.mult, ALU.mult,
                    r=[("o32", c8 // 4), "rs", "const"], w=[("otmp", c8 % 2)])
                tt("pool", ohat[:, c8, tc0:tc0 + R], otmp2[c8 % 2][:, :R], silur[:, c8, tc0:tc0 + R], ALU.mult,
                   r=[("otmp", c8 % 2), ("silur", c8)], w=[("ohat", c8)])

        if KMIX <= 2:
            return
        nbanks[0] = 7
        if si == 0:
            dbg_dump("ohat", ohat[:, :, 0:128], [128, 8, 128], [("ohat", c) for c in range(8)])
        phase("gla_out")
        if small:
            for t in range(4):
                emit_gate_a(t)
        for t in range(4):
            wt, rk = ws_next("wgo", t)
            wv = wt[:, 0:2048].rearrange("p (k m) -> p k m", k=8)
            for sub in range(2):
                o = 2 * t + sub
                b = proj_fm(wv, sub, N, rk, src=ohat, skey="ohat")
                tt("dve", mg[:, o, :N], ps[b][:, :N], siga[:, o, :N], ALU.mult, r=[PSR(b), ("siga", o)], w=[("mg", o)])

        if KMIX <= 3:
            return
        P.barrier()
        cv = Carver()
        Uraw = cv.bf([4096])
        UtmA = Uraw.rearrange("p (q j c) -> p q j c", q=32, j=4)
        UtmAf = Uraw.rearrange("p (q x) -> p q x", q=32)
        Utm = Uraw.rearrange("p (j f) -> p j f", j=4)
        Uq = cv.bf([32, 128])
        HS = cv.f32([NMC + 1, 64])
        Hbf = [cv.bf([2, 4, 128]) for _ in range(2)]
        yv = cv.f32([4, 128]); yt = cv.f32([4, 128])
        ct1 = cv.f32([64]); ct2 = cv.f32([64])
        if small:
            H0t = cv.f32([16, 64]); Xsm = cv.f32([16, 64]); Hso = cv.f32([16, 64])
            st1 = cv.f32([16, 64]); st2 = cv.f32([16, 64])
        Cs_buf = cv.f32([17, 64])
        chain_tmp_off = cv.off
        sbt = [cv.f32([NS]) for _ in range(2)]
        gtmp = [cv.f32([NS]) for _ in range(2)]
        mrg = cv.bf([8, NS])
        phase("s5_proj")
        uv = u[:, :, 0:N].rearrange("p k (n j) -> p k n j", j=4)
        for t in range(4):
            wt, rk = ws_next("win", 16 + t)
            wv = wt[:, 0:2048].rearrange("p (k m) -> p k m", k=8)
            for j in range(4):
                b = bank()
                for kt in range(8):
                    mm(ps[b][:NMC, 0:256], uv[:, kt, :, j], wv[:, kt, :], kt == 0, kt == 7, r=[rk, ("u", kt)], w=[PSR(b)])
                cp(evac_eng(), UtmA[:NMC, 8 * t:8 * t + 8, j, :], ps[b][:NMC, 0:256].rearrange("n (q c) -> n q c", q=8),
                   r=[PSR(b)], w=[("Utm", t)])
        allU = [("Utm", t) for t in range(4)]
        phase("s5_trX")
        for g8 in range(4):
            b = bank()
            for sl in range(8):
                q = 8 * g8 + sl
                tr(psb[b][:, sl * 128:sl * 128 + NMC], UtmAf[:NMC, q, :], ident_b[:NMC, :NMC],
                   r=allU + ["const"], w=[PSR(b)])
            cp(evac_eng(), Uq[:, 8 * g8:8 * g8 + 8, :NMC], psb[b][:].rearrange("p (s n) -> p s n", s=8)[:, :, :NMC],
               r=[PSR(b)], w=[("Uq", 8 * g8 + i_) for i_ in range(8)])
        for q0 in range(0, 32, 4):
            for ri in range(2):
                b = bank()
                for ql in range(4):
                    q = q0 + ql
                    mm(ps[b][:, ql * 128:ql * 128 + NMC], WX[:, q, ri, :], Uq[:, q, :NMC], r=["WX", ("Uq", q)], w=[PSR(b)])
                pv = ps[b][:].rearrange("p (q n) -> p q n", q=4)
                col = ri * 32 + q0
                if small:
                    cp(evac_eng(), HS[:, 1:5, col:col + 4].rearrange("p n q -> p q n"), pv[:, :, 0:4], r=[PSR(b)], w=["HS"])
                    cp(evac_eng(), Xsm[:, :, col:col + 4].rearrange("p n q -> p q n"), pv[:, :, 4:20], r=[PSR(b)], w=["Xsm"])
                else:
                    cp(evac_eng(), HS[:, 1:1 + NMC, col:col + 4].rearrange("p n q -> p q n"), pv[:, :, :NMC], r=[PSR(b)], w=["HS"])
        if KMIX <= 4:
            return
        phase("s5_chain")
        for t in range(4):
            wg_, rg = ws_next("win", 20 + t)
            wgv = wg_[:, 0:2048].rearrange("p (k m) -> p k m", k=8)
            for sub in range(2):
                o = 2 * t + sub
                bg = proj_fm(wgv, sub, N, rg)
                actf(mrg[:, o, :N], ps[bg][:, :N], AF.Sigmoid, r=[PSR(bg)], w=[("mrg", o)])
        CE = CHAIN_ENG

        def cstep(dst, src, xin, pw, k_src, k_x, k_dst):
            tt(CE, ct1[:], src, APW[:, pw - 1, 0, :], ALU.mult, r=[k_src, "AA"], w=["ct1"])
            tt(CE, ct2[:], src, APW[:, pw - 1, 1, :], ALU.mult, r=[k_src, "AA"], w=["ct2"])
            tt(CE, dst, xin, ct1[:], ALU.add, r=[k_x, "ct1"], w=[k_dst])
            tt(CE, dst[:, 0:32], dst[:, 0:32], ct2[:, 32:64], ALU.subtract, r=[k_dst, "ct2"], w=[k_dst])
            tt(CE, dst[:, 32:64], dst[:, 32:64], ct2[:, 0:32], ALU.add, r=[k_dst, "ct2"], w=[k_dst])

        if small:
            cp(CE, HS[:, 0, :], Hc[:], r=["Hc", "HS"], w=[("HSs", 0)])
            for n in range(4):
                cstep(HS[:, n + 1, :], HS[:, n, :], HS[:, n + 1, :], 1, ("HSs", n), "HS", ("HSs", n + 1))
            cp(CE, Hc[:], HS[:, 4, :], r=[("HSs", 4)], w=["Hc"])
            cp(CE, HS[:, 0, 0:1], HS[:, 0, 0:1], r=[("HSs", n_) for n_ in range(5)], w=["HS"])
        else:
            NB, BL = 16, 8
            cvc = Carver(); cvc.off = chain_tmp_off
            Cs = Cs_buf; bt1 = cvc.f32([NB, 64]); bt2 = cvc.f32([NB, 64])

            def bulk(dst, src, pw, srcb=False):
                ar = APW[:, pw - 1, 0, :].unsqueeze(1).to_broadcast([128, NB, 64])
                ai = APW[:, pw - 1, 1, :].unsqueeze(1).to_broadcast([128, NB, 64])
                tt(CE, bt1[:], src, ar, ALU.mult, r=["HS", "Cs", "AA"], w=["bt1"])
                tt(CE, bt2[:], src, ai, ALU.mult, r=["HS", "Cs", "AA"], w=["bt2"])
                tt(CE, dst, dst, bt1[:], ALU.add, r=["HS", "bt1"], w=["HS"])
                tt(CE, dst[:, :, 0:32], dst[:, :, 0:32], bt2[:, :, 32:64], ALU.subtract, r=["HS", "bt2"], w=["HS"])
                tt(CE, dst[:, :, 32:64], dst[:, :, 32:64], bt2[:, :, 0:32], ALU.add, r=["HS", "bt2"], w=["HS"])

            V = HS[:, 1:1 + NMC, :].rearrange("p (m i) c -> p m i c", i=BL)
            W = HS[:, 0:NMC, :].rearrange("p (m i) c -> p m i c", i=BL)
            for i in range(1, BL):
                bulk(V[:, :, i, :], V[:, :, i - 1, :], 1)
            cp(CE, Cs[:, 0, :], Hc[:], r=["Hc"], w=[("Cs", 0)])
            for m in range(NB):
                cstep(Cs[:, m + 1, :], Cs[:, m, :], V[:, m, BL - 1, :], BL, ("Cs", m), "HS", ("Cs", m + 1))
            cp(CE, Cs[:, 0, 0:1], Cs[:, 0, 0:1], r=[("Cs", m_) for m_ in range(NB + 1)], w=["Cs"])
            for i in range(1, BL):
                bulk(W[:, :, i, :], Cs[:, 0:NB, :], i)
            cp(CE, W[:, :, 0, :], Cs[:, 0:NB, :], r=["Cs"], w=["HS"])
            cp(CE, Hc[:], Cs[:, NB, :], r=["Cs"], w=["Hc"])
            P.barrier()
        if small:
            dma("sp", H0t[:].rearrange("p b c -> p (b c)"), din["h0"].rearrange("p b c -> p (b c)"), "h0", w=["H0t"])
            AAb = AA[:].unsqueeze(1).to_broadcast([128, 16, 64])
            ABb = AB[:].unsqueeze(1).to_broadcast([128, 16, 64])
            tt("dve", st1[:], H0t[:], AAb, ALU.mult, r=["H0t", "AA"], w=["st1"])
            tt("dve", st2[:], H0t[:], ABb, ALU.mult, r=["H0t", "AA"], w=["st2"])
            tt("dve", Hso[:], Xsm[:], st1[:], ALU.add, r=["Xsm", "st1"], w=["Hso"])
            tt("dve", Hso[:, :, 0:32], Hso[:, :, 0:32], st2[:, :, 32:64], ALU.subtract, r=["Hso", "st2"], w=["Hso"])
            tt("dve", Hso[:, :, 32:64], Hso[:, :, 32:64], st2[:, :, 0:32], ALU.add, r=["Hso", "st2"], w=["Hso"])
            dma("sp", dout["s5so"].rearrange("p b c -> p (b c)"), Hso[:].rearrange("p b c -> p (b c)"), "s5so", r=["Hso"])
        if KMIX <= 5:
            return
        phase("s5_Y")
        def hb_cast(bq):
            hb = Hbf[bq % 2]
            for ri in range(2):
                col = ri * 32 + 4 * bq
                if small:
                    cp(evac_eng(), hb[:, ri, :, 0:4], HS[:, 0:4, col:col + 4].rearrange("p n q -> p q n"), r=["HS"], w=[("Hbf", bq % 2)])
                    cp(evac_eng(), hb[:, ri, :, 4:20], H0t[:, :, col:col + 4].rearrange("p n q -> p q n"), r=["H0t"], w=[("Hbf", bq % 2)])
                else:
                    cp(evac_eng(), hb[:, ri, :, :NMC], HS[:, 0:NMC, col:col + 4].rearrange("p n q -> p q n"), r=["HS"], w=[("Hbf", bq % 2)])

        hb_cast(0)
        for bq in range(8):
            hb = Hbf[bq % 2]
            if bq + 1 < 8:
                hb_cast(bq + 1)
            b = bank()
            qs = [4 * bq + ql for ql in range(4)]
            for ql in range(4):
                q = 4 * bq + ql
                o = ps[b][:, ql * 128:ql * 128 + NMC]
                mm(o, Toep[:, q, :], Uq[:, q, :NMC], True, False, r=["Toep", ("Uq", q)], w=[PSR(b)])
                mm(o, Win[:, q, 0, :], hb[:, 0, ql, :NMC], False, False, r=["Win", ("Hbf", bq % 2)], w=[PSR(b)])
                mm(o, Win[:, q, 1, :], hb[:, 1, ql, :NMC], False, True, r=["Win", ("Hbf", bq % 2)], w=[PSR(b)])
            yslot = bq % 2
            yvv = (yv if yslot == 0 else yt)
            tt("dve", yvv[:, :, :NMC], Uq[:, 4 * bq:4 * bq + 4, :NMC],
               Dcol[:, 4 * bq:4 * bq + 4].unsqueeze(2).to_broadcast([128, 4, NMC]), ALU.mult,
               r=[("Uq", q) for q in qs] + ["const"], w=[("yv", yslot)])
            tt("dve", yvv[:, :, :NMC], yvv[:, :, :NMC], ps[b][:].rearrange("p (q n) -> p q n", q=4)[:, :, :NMC], ALU.add,
               r=[("yv", yslot), PSR(b)], w=[("yv", yslot)])
            actf(Uq[:, 4 * bq:4 * bq + 4, :NMC], yvv[:, :, :NMC], AF.Gelu_apprx_tanh, r=[("yv", yslot)], w=[("Uq", q) for q in qs])
        for g8 in range(4):
            b = bank()
            for sl in range(8):
                q = 8 * g8 + sl
                tr(psb[b][:NMC, sl * 128:(sl + 1) * 128], Uq[:, q, :NMC], ident_b[:], r=[("Uq", q), "const"], w=[PSR(b)])
            for j in range(4):
                cp(evac_eng(), Utm[:NMC, j, 256 * g8:256 * g8 + 256].rearrange("n (q c) -> n q c", q=8),
                   psb[b][:NMC, :].rearrange("n (q j c) -> n q j c", q=8, j=4)[:, :, j, :], r=[PSR(b)], w=[("Utm", g8)])
        y5v = y5T[:, :, 0:N].rearrange("p c (n j) -> p c n j", j=4)
        for j in range(4):
            b = bank()
            for ch in range(8):
                tr(psb[b][:, ch * 128:ch * 128 + NMC], Utm[:NMC, j, ch * 128:(ch + 1) * 128], ident_b[:NMC, :NMC],
                   r=allU + ["const"], w=[PSR(b)])
            cp(evac_eng(), y5v[:, :, :, j], psb[b][:].rearrange("p (c n) -> p c n", c=8)[:, :, :NMC], r=[PSR(b)], w=["y5T"])
        if KMIX <= 7:
            return
        if si == 0:
            dbg_dump("mg", mg[:, :, 0:128], [128, 8, 128], [("mg", c) for c in range(8)])
            dbg_dump("y5T", y5T[:, :, 0:128], [128, 8, 128], ["y5T"])
        phase("glu")
        for t in range(4):
            wa_, ra = ws_next("wa", t)
            wb_, rb = ws_next("wb", t)
            wav = wa_[:, 0:2048].rearrange("p (k m) -> p k m", k=8)
            wbv = wb_[:, 0:2048].rearrange("p (k m) -> p k m", k=8)
            for sub in range(2):
                o = 2 * t + sub
                s_ = o % 2
                ba = proj_fm(wav, sub, N, ra, src=y5T, skey="y5T_")
                bb_ = proj_fm(wbv, sub, N, rb, src=y5T, skey="y5T_")
                actf(sbt[s_][:, :N], ps[bb_][:, :N], AF.Sigmoid, r=[PSR(bb_)], w=[("sbt", s_)])
                tt("dve", gtmp[s_][:, :N], ps[ba][:, :N], sbt[s_][:, :N], ALU.mult, r=[PSR(ba), ("sbt", s_)], w=[("gtmp", s_)])
                tt("dve", gtmp[s_][:, :N], gtmp[s_][:, :N], mrg[:, o, :N], ALU.mult, r=[("gtmp", s_), ("mrg", o)], w=[("gtmp", s_)])
                tt("dve", mrg[:, o, :N], gtmp[s_][:, :N], mg[:, o, :N], ALU.add, r=[("gtmp", s_), ("mg", o)], w=[("mrg", o)])
        if si == 0:
            dbg_dump("mrg", mrg[:, :, 0:128], [128, 8, 128], [("mrg", c) for c in range(8)])
        for t in range(4):
            wt, rk = ws_next("wo", t)
            wv = wt[:, 0:2048].rearrange("p (k m) -> p k m", k=8)
            for sub in range(2):
                o = 2 * t + sub
                b = proj_fm(wv, sub, N, rk, src=mrg, skey="mrg")
                tt("dve", h[:, o, :N], ps[b][:, :N], h[:, o, :N], ALU.add, r=[PSR(b), ("h", o)], w=[("h", o)])

    import os
    STOP = int(os.environ.get("KSTOP", "99"))
    for si, (c0, N) in enumerate(SEGS):
        if STOP <= 1 or (STOP < 10 and si >= 1):
            break
        dma("sp", h[:, :, :N], din["xT"][:, :, c0:c0 + N], "xin", w=[("h", c) for c in range(8)])
        if STOP >= 2:
            ffn(N, GI_FFN1, "w1", need_barrier=(si == 0))
        if STOP >= 3:
            mixer(si, c0, N)
        cv = Carver(); cv.off = 8192
        yout = cv.f32([8, 512])
        ffn(N, GI_FFN2, "w2", dst=yout, dst_key="yout")
        phase("final_norm")
        rmsnorm(N, GI_FINAL, yout, "yout", src=yout, src_key="yout")
        dma("sp", dout["yT"][:, :, c0:c0 + N], yout[:, :, :N], "yout", r=[("yout", c) for c in range(8)])
    dma("sp", dout["gp"].rearrange("h d e -> d h e"), Sx[:], "gp", r=[("Sx", h_) for h_ in range(4)])
    dma("sp", dout["s5po"], Hc[:], "s5po", r=["Hc"])
    phase("end")
    stats = P.emit()
    if _os.environ.get("KPHASE"):
        import json as _json
        _json.dump(PHASES, open(_os.environ["KPHASE"], "w"))
    return nc, stats


def _tile_cols(W, c0, ncols):
    K = W.shape[0]
    return np.ascontiguousarray(W[:, c0:c0 + ncols].reshape(K // 128, 128, ncols).transpose(1, 0, 2).reshape(128, -1))


def _gain(g):
    return g.reshape(8, 128).T


def _prep_shared(inp):
    f = lambda a: np.asarray(a, dtype=np.float32)
    sh = {}
    for pre, a, b_, c in (("w1", "ffn1_w_gate", "ffn1_w_up", "ffn1_w_down"), ("w2", "ffn2_w_gate", "ffn2_w_up", "ffn2_w_down")):
        Wg, Wu, Wd = f(inp[a])[0], f(inp[b_])[0], f(inp[c])[0]
        sh[pre + "g"] = np.stack([_tile_cols(Wg, 256 * j, 256) for j in range(11)])
        sh[pre + "u"] = np.stack([_tile_cols(Wu, 256 * j, 256) for j in range(11)])
        dt = []
        for o in range(8):
            for kh in range(2):
                blk = Wd[11 * kh * 128:(11 * kh + 11) * 128, 128 * o:128 * o + 128]
                dt.append(blk.reshape(11, 128, 128).transpose(1, 0, 2).reshape(128, 1408))
        sh[pre + "d"] = np.stack(dt)
    Win = f(inp["w_in"])[0]
    cols = []
    cols += [0, 256, 512, 768]
    cols += [1024 + 256 * i for i in range(4)]
    cols += [2048 + 256 * i for i in range(4)]
    cols += [4112 + 256 * i for i in range(4)]
    cols += [3088 + 256 * i for i in range(4)]
    cols += [5136 + 256 * i for i in range(4)]
    sh["win"] = np.stack([_tile_cols(Win, c, 256) for c in cols])
    sh["wglr"] = _tile_cols(Win, 3072, 16)
    for nm, key in (("wgo", "gla_w_out"), ("wa", "s5_w_glu_a"), ("wb", "s5_w_glu_b"), ("wo", "w_out")):
        W = f(inp[key])[0]
        sh[nm] = np.stack([_tile_cols(W, 256 * t, 256) for t in range(4)])
    sh["gains"] = np.ascontiguousarray(np.stack([_gain(f(inp["norm_ffn1"])[0]), _gain(f(inp["norm_mix"])[0]),
                                                 _gain(f(inp["norm_ffn2"])[0]), _gain(f(inp["norm_final"])),
                                                 _gain(f(inp["gla_norm"])[0])], axis=1))
    sh["wup"] = np.concatenate([f(inp["gla_w_gate_up"])[0], f(inp["gla_b_gate"])], axis=0)

    def lay_gp(a):
        return a.reshape(32, 2, 64).transpose(1, 2, 0).reshape(128, 32)
    are = lay_gp(f(inp["s5_a_re"])[0]); aim = lay_gp(f(inp["s5_a_im"])[0])
    ldt = lay_gp(np.repeat(f(inp["s5_log_dt"])[0][:, None], 64, axis=1))
    sh["s5p"] = np.ascontiguousarray(np.stack([are, aim, ldt], axis=1))

    def lay_b(B):
        Bq = B.reshape(32, 2, 64, 16)
        out = np.zeros((2, 64, 32, 2, 16), np.float32)
        for m in range(2):
            out[m, :, :, m, :] = Bq[:, m].transpose(1, 0, 2)
        return out.reshape(128, 1024)

    def lay_c(C):
        Cq = C.reshape(32, 2, 16, 64)
        out = np.zeros((2, 64, 32, 2, 16), np.float32)
        for m in range(2):
            out[m, :, :, m, :] = Cq[:, m].transpose(2, 0, 1)
        return out.reshape(128, 1024)
    sh["s5b"] = np.stack([lay_b(f(inp["s5_b_re"])[0]), lay_b(f(inp["s5_b_im"])[0])], axis=1)
    sh["s5c"] = np.stack([lay_c(f(inp["s5_c_re"])[0]), lay_c(f(inp["s5_c_im"])[0])], axis=1)
    d = f(inp["s5_d"])[0].reshape(32, 2, 16).transpose(1, 2, 0)
    sh["s5d"] = np.ascontiguousarray(np.broadcast_to(d[None], (4, 2, 16, 32)).reshape(128, 32))
    cm = np.zeros((128, 8, 128), np.float32)
    cm[:, 0, :] = np.eye(128)
    idx = np.arange(128)
    same = (idx[:, None] // 64) == (idx[None, :] // 64)
    cm[:, 1, :] = (same & (idx[:, None] <= idx[None, :])).astype(np.float32)
    cm[:, 2, :] = -cm[:, 1, :] / 16.0
    cm[:, 3, :] = -(same & (idx[:, None] > idx[None, :])).astype(np.float32) / 16.0
    cid = np.where(np.arange(80) < 16, 0, 1 + (np.arange(80) - 16) // 4)
    same0 = cid[:, None] == cid[None, :]
    i80 = np.arange(80)
    cm[:80, 4, :80] = (same0 & (i80[:, None] <= i80[None, :])).astype(np.float32)
    cm[:80, 5, :80] = -cm[:80, 4, :80] / 16.0
    cm[:80, 6, :80] = -(same0 & (i80[:, None] > i80[None, :])).astype(np.float32) / 16.0
    cm[:, 7, :] = ((idx[None, :] // 32) >= (idx[:, None] // 32)).astype(np.float32)
    sh["cmat"] = cm
    ci = np.zeros((128, 19), np.float32)
    ci[:, 0] = (idx < 64); ci[:, 1] = (idx >= 64)
    for s in range(80):
        ci[s, 2 + cid[s]] = 1.0
    sh["cind"] = ci
    return {k: np.ascontiguousarray(v, dtype=np.float32) for k, v in sh.items()}


def _prep_core(inp, c):
    f = lambda a: np.asarray(a, dtype=np.float32)
    toks = np.concatenate([f(inp["meta_tokens"]), f(inp["x_sample"])[16 * c:16 * c + 16].reshape(64, D), f(inp["x_prompt"])[c]], axis=0)
    xT = toks.T.reshape(8, 128, NCOL).transpose(1, 0, 2)
    sg = f(inp["state_gla"])[0, 16 * c:16 * c + 16]

    def lay_h(a):
        return a.reshape(16, 32, 2, 64).transpose(2, 3, 0, 1).reshape(128, 16, 32)
    h0 = np.concatenate([lay_h(f(inp["state_s5_re"])[0, 16 * c:16 * c + 16]), lay_h(f(inp["state_s5_im"])[0, 16 * c:16 * c + 16])], axis=2)
    return {"xT": np.ascontiguousarray(xT), "sg": np.ascontiguousarray(sg), "h0": np.ascontiguousarray(h0)}


_CACHE = {}


def kernel(**inputs):
    if "nc" not in _CACHE:
        _CACHE["nc"] = build_program()
    nc, stats = _CACHE["nc"]
    sh = _prep_shared(inputs)
    in_maps = []
    for c in range(NCORES):
        m = dict(sh)
        m.update(_prep_core(inputs, c))
        in_maps.append(m)
    res = run_bass_kernel_spmd(nc, in_maps, core_ids=list(range(NCORES)))
    R = res.results
    y_prompt = np.zeros((8, 2048, D), np.float32)
    y_sample = np.zeros((128, 4, D), np.float32)
    gla_p = np.zeros((1, 8, 4, 128, 256), np.float32)
    re_p = np.zeros((1, 8, 64, 64), np.float32); im_p = np.zeros((1, 8, 64, 64), np.float32)
    gla_s = np.zeros((1, 128, 4, 128, 256), np.float32)
    re_s = np.zeros((1, 128, 64, 64), np.float32); im_s = np.zeros((1, 128, 64, 64), np.float32)
    for c in range(NCORES):
        r = R[c]
        y = np.asarray(r["yT"]).transpose(1, 0, 2).reshape(D, NCOL).T
        y_sample[16 * c:16 * c + 16] = y[16:80].reshape(16, 4, D)
        y_prompt[c] = y[80:]
        gla_p[0, c] = np.asarray(r["gp"])
        gla_s[0, 16 * c:16 * c + 16] = np.asarray(r["gs"])
        hp = np.asarray(r["s5po"])
        un = lambda a: a.reshape(2, 64, 32).transpose(2, 0, 1).reshape(64, 64)
        re_p[0, c] = un(hp[:, 0:32]); im_p[0, c] = un(hp[:, 32:64])
        hs = np.asarray(r["s5so"])
        un2 = lambda a: a.reshape(2, 64, 16, 32).transpose(2, 3, 0, 1).reshape(16, 64, 64)
        re_s[0, 16 * c:16 * c + 16] = un2(hs[:, :, 0:32]); im_s[0, 16 * c:16 * c + 16] = un2(hs[:, :, 32:64])
    return (y_prompt, y_sample, gla_p, re_p, im_p, gla_s, re_s, im_s)
```
